# Optimizing a Trainium2 kernel written in Bass

```python
import jax, jax.numpy as jnp
from jax import lax
import numpy as np

D_MODEL = 1024
BATCH = 8
SEQ = 4096
DEPTH = 4

GRID_W = 64
CTX_LEN = 256
D_FF = 2816
N_MOD = 9
W_FOURIER = 256
FOURIER_GROUPS = 4
W_LRU = 512
LRU_HEADS = 8
W_SC = 256
LRU_CONV = 4
SC_CONV = 3
LRU_C = 8.0
N_BRANCH = 3
RMS_EPS = 1e-6
OFF_F = 0
OFF_LX = OFF_F + W_FOURIER
OFF_LG = OFF_LX + W_LRU
OFF_SB = OFF_LG + W_LRU
OFF_SC = OFF_SB + W_SC
OFF_SX = OFF_SC + W_SC
OFF_G = OFF_SX + W_SC
IN_COLS = OFF_G + N_BRANCH * D_MODEL

kernel_name = "hybrid_fourier_rglru_shortconv_macaron_dit"


def rms_norm(x, g):
    xf = x.astype(jnp.float32)
    y = xf * lax.rsqrt(jnp.mean(xf * xf, axis=-1, keepdims=True) + RMS_EPS)
    return (y * g.astype(jnp.float32)).astype(x.dtype)


def modulate(x, shift, scale):
    return x * (1 + scale) + shift


def swiglu(x, wg, wu, wd):
    return (jax.nn.silu(x @ wg) * (x @ wu)) @ wd


def dw_conv(x, w, pad_left):
    k = w.shape[0]
    length = x.shape[1]
    xp = jnp.pad(x, ((0, 0), (pad_left, k - 1 - pad_left), (0, 0)))
    return sum(xp[:, j:j + length] * w[j] for j in range(k))


def lru_conv(x, w, b):
    return dw_conv(x, w, LRU_CONV // 2) + b


def short_conv_seq(x, w):
    return dw_conv(x, w, SC_CONV // 2)


def short_conv_latent(x, w):
    b, length, ch = x.shape
    rows = length // GRID_W
    y = dw_conv(x.reshape(b * rows, GRID_W, ch), w, SC_CONV // 2)
    return y.reshape(b, length, ch)


def fourier_mix(xf):
    b, length, _ = xf.shape
    z = xf.astype(jnp.float32).reshape(b, length, FOURIER_GROUPS, W_FOURIER // FOURIER_GROUPS)
    z = jnp.fft.fft2(z, axes=(1, 3), norm="ortho").real
    return z.reshape(b, length, W_FOURIER).astype(xf.dtype)


def _combine(left, right):
    return (left[0] * right[0], right[0] * left[1] + right[1])


def rglru_scan(xc, wa, ba, wx, bx, lam, h0):
    b, length, w = xc.shape
    xc = xc.astype(jnp.float32)
    xh = xc.reshape(b, length, LRU_HEADS, w // LRU_HEADS)
    r = jax.nn.sigmoid(jnp.einsum("blhi,hij->blhj", xh, wa.astype(jnp.float32))
                       + ba.astype(jnp.float32)).reshape(b, length, w)
    i = jax.nn.sigmoid(jnp.einsum("blhi,hij->blhj", xh, wx.astype(jnp.float32))
                       + bx.astype(jnp.float32)).reshape(b, length, w)
    log_a = -LRU_C * r * jax.nn.softplus(-lam.astype(jnp.float32))
    a = jnp.exp(log_a)
    bterm = jnp.sqrt(-jnp.expm1(2.0 * log_a)) * (i * xc)
    a_cum, h = lax.associative_scan(_combine, (a, bterm), axis=1)
    if h0 is not None:
        h = h + a_cum * h0[:, None, :]
    return h, h[:, -1]


def rglru_bidir(xc, wa, ba, wx, bx, lam, h0_f, h0_b):
    hf, last_f = rglru_scan(xc, wa[0], ba[0], wx[0], bx[0], lam[0], h0_f)
    hb_rev, last_b = rglru_scan(jnp.flip(xc, 1), wa[1], ba[1], wx[1], bx[1], lam[1], h0_b)
    return hf, jnp.flip(hb_rev, 1), last_f, last_b


def merge_branches(p, y_lru, sc_conv_fn, sc_w, wp_f, wp_l, wp_c, w_out):
    y_f = fourier_mix(p[..., OFF_F:OFF_LX])
    y_l = y_lru.astype(p.dtype) * jax.nn.gelu(p[..., OFF_LG:OFF_SB])
    y_c = p[..., OFF_SB:OFF_SC] * sc_conv_fn(p[..., OFF_SC:OFF_SX] * p[..., OFF_SX:OFF_G], sc_w)
    g_f, g_l, g_c = jnp.split(jax.nn.sigmoid(p[..., OFF_G:]), N_BRANCH, axis=-1)
    m = g_f * (y_f @ wp_f) + g_l * (y_l @ wp_l) + g_c * (y_c @ wp_c)
    return m @ w_out


def setup_inputs(seed: int = 0) -> dict:
    key = jax.random.key(seed)
    ks = jax.random.split(key, 26)

    def nrm(k, shape, scale):
        return jax.random.normal(k, shape, jnp.float32) * scale

    u = jax.random.uniform(ks[19], (DEPTH, 2, W_LRU), jnp.float32, 0.9, 0.999)
    s = u ** (1.0 / LRU_C)
    lru_lambda = jnp.log(s) - jnp.log1p(-s)
    blk = W_LRU // LRU_HEADS
    return {
        "x": nrm(ks[0], (BATCH, SEQ, D_MODEL), 1.0),
        "c": nrm(ks[1], (BATCH, D_MODEL), 1.0),
        "ctx": nrm(ks[2], (BATCH, CTX_LEN, D_MODEL), 1.0),
        "c_ctx": nrm(ks[3], (D_MODEL,), 1.0),
        "w_ada": nrm(ks[4], (DEPTH, D_MODEL, N_MOD * D_MODEL), 0.5 * D_MODEL ** -0.5),
        "b_ada": nrm(ks[5], (DEPTH, N_MOD * D_MODEL), 0.01),
        "g_ffn1": 1.0 + nrm(ks[6], (DEPTH, D_MODEL), 0.01),
        "g_mix": 1.0 + nrm(ks[7], (DEPTH, D_MODEL), 0.01),
        "g_ffn2": 1.0 + nrm(ks[8], (DEPTH, D_MODEL), 0.01),
        "ffn_w_gate": nrm(ks[9], (DEPTH, 2, D_MODEL, D_FF), D_MODEL ** -0.5),
        "ffn_w_up": nrm(ks[10], (DEPTH, 2, D_MODEL, D_FF), D_MODEL ** -0.5),
        "ffn_w_down": nrm(ks[11], (DEPTH, 2, D_FF, D_MODEL), D_FF ** -0.5),
        "w_in": nrm(ks[12], (DEPTH, D_MODEL, IN_COLS), D_MODEL ** -0.5),
        "lru_conv_w": nrm(ks[13], (DEPTH, LRU_CONV, W_LRU), LRU_CONV ** -0.5),
        "lru_conv_b": nrm(ks[14], (DEPTH, W_LRU), 0.01),
        "lru_wa": nrm(ks[15], (DEPTH, 2, LRU_HEADS, blk, blk), blk ** -0.5),
        "lru_ba": nrm(ks[16], (DEPTH, 2, LRU_HEADS, blk), 0.01),
        "lru_wx": nrm(ks[17], (DEPTH, 2, LRU_HEADS, blk, blk), blk ** -0.5),
        "lru_bx": nrm(ks[18], (DEPTH, 2, LRU_HEADS, blk), 0.01),
        "lru_lambda": lru_lambda,
        "sc_conv_w": nrm(ks[20], (DEPTH, SC_CONV, W_SC), SC_CONV ** -0.5),
        "wp_fourier": nrm(ks[21], (DEPTH, W_FOURIER, D_MODEL), W_FOURIER ** -0.5),
        "wp_lru": nrm(ks[22], (DEPTH, W_LRU, D_MODEL), W_LRU ** -0.5),
        "wp_conv": nrm(ks[23], (DEPTH, W_SC, D_MODEL), W_SC ** -0.5),
        "w_out": nrm(ks[24], (DEPTH, D_MODEL, D_MODEL), D_MODEL ** -0.5),
        "g_final": 1.0 + nrm(ks[25], (D_MODEL,), 0.01),
    }


def reference(x, c, ctx, c_ctx, w_ada, b_ada, g_ffn1, g_mix, g_ffn2, ffn_w_gate, ffn_w_up,
              ffn_w_down, w_in, lru_conv_w, lru_conv_b, lru_wa, lru_ba, lru_wx, lru_bx,
              lru_lambda, sc_conv_w, wp_fourier, wp_lru, wp_conv, w_out, g_final):
    h = x
    hc = ctx
    s_lat = jax.nn.silu(c)
    s_ctx = jax.nn.silu(c_ctx)
    for l in range(DEPTH):
        last = l == DEPTH - 1
        m = jnp.split((s_lat @ w_ada[l] + b_ada[l])[:, None, :], N_MOD, axis=-1)
        mc = jnp.split(s_ctx @ w_ada[l] + b_ada[l], N_MOD, axis=-1)

        h = h + 0.5 * m[2] * swiglu(modulate(rms_norm(h, g_ffn1[l]), m[0], m[1]),
                                    ffn_w_gate[l, 0], ffn_w_up[l, 0], ffn_w_down[l, 0])
        hc = hc + 0.5 * mc[2] * swiglu(modulate(rms_norm(hc, g_ffn1[l]), mc[0], mc[1]),
                                       ffn_w_gate[l, 0], ffn_w_up[l, 0], ffn_w_down[l, 0])

        u = modulate(rms_norm(h, g_mix[l]), m[3], m[4])
        uc = modulate(rms_norm(hc, g_mix[l]), mc[3], mc[4])

        if last:
            pc_lru = uc @ w_in[l][:, OFF_LX:OFF_LG]
        else:
            pc = uc @ w_in[l]
            pc_lru = pc[..., OFF_LX:OFF_LG]
        xcc = lru_conv(pc_lru, lru_conv_w[l], lru_conv_b[l])
        cf, cb, cf_last, cb_last = rglru_bidir(xcc, lru_wa[l], lru_ba[l], lru_wx[l], lru_bx[l],
                                               lru_lambda[l], None, None)

        p = u @ w_in[l]
        xcl = lru_conv(p[..., OFF_LX:OFF_LG], lru_conv_w[l], lru_conv_b[l])
        lf, lb, _, _ = rglru_bidir(xcl, lru_wa[l], lru_ba[l], lru_wx[l], lru_bx[l],
                                   lru_lambda[l], cf_last, cb_last)
        h = h + m[5] * merge_branches(p, lf + lb, short_conv_latent, sc_conv_w[l],
                                      wp_fourier[l], wp_lru[l], wp_conv[l], w_out[l])
        if not last:
            hc = hc + mc[5] * merge_branches(pc, cf + cb, short_conv_seq, sc_conv_w[l],
                                             wp_fourier[l], wp_lru[l], wp_conv[l], w_out[l])

        h = h + 0.5 * m[8] * swiglu(modulate(rms_norm(h, g_ffn2[l]), m[6], m[7]),
                                    ffn_w_gate[l, 1], ffn_w_up[l, 1], ffn_w_down[l, 1])
        if not last:
            hc = hc + 0.5 * mc[8] * swiglu(modulate(rms_norm(hc, g_ffn2[l]), mc[6], mc[7]),
                                           ffn_w_gate[l, 1], ffn_w_up[l, 1], ffn_w_down[l, 1])
    return rms_norm(h, g_final)
```

```python
import math
from contextlib import ExitStack

import numpy as np
import ml_dtypes

import concourse.bass as bass
import concourse.mybir as mybir
from concourse.bass_utils import run_bass_kernel_spmd

F32 = mybir.dt.float32
BF16 = mybir.dt.bfloat16
AF = mybir.ActivationFunctionType
ALU = mybir.AluOpType

D = 1024
KC = 8
DFF = 2816
FC = 22
NMOD = 9
GRID_W = 64
RMS_EPS = 1e-6
IN_COLS = 5120
N_CORES = 8


class Buf:
    __slots__ = ("w", "r")

    def __init__(self):
        self.w = None
        self.r = []


class Prog:
    ENGS = ("pe", "act", "dve", "pool", "sp")

    def __init__(self, esems, dsems):
        self.streams = {e: [] for e in self.ENGS}
        self.cnt = {e: 0 for e in self.ENGS}
        self.sat = {e: {} for e in self.ENGS}
        self.esem = esems
        self.dsl = {e: [[s, 0] for s in ss] for e, ss in dsems.items()}
        self.dqi = {e: 0 for e in dsems}
        self.semid = {}
        self.all_events = {}

    def _key(self, sem):
        return id(sem)

    def _prune(self, eng, evs):
        best = {}
        for ev in evs:
            if ev is None:
                continue
            sem, val = ev
            k = self._key(sem)
            if eng == "pe" and sem is self.esem["pe"]:
                continue
            if k not in best or best[k][1] < val:
                best[k] = (sem, val)
        waits = []
        for k, (sem, val) in best.items():
            if self.sat[eng].get(k, 0) < val:
                self.sat[eng][k] = val
                waits.append((sem, val))
        return waits

    def _deps(self, reads, writes):
        evs = []
        for b in reads:
            if b.w is not None:
                evs.append(b.w)
        for b in writes:
            if b.w is not None:
                evs.append(b.w)
            evs.extend(b.r)
        return evs

    def _commit(self, ev, reads, writes):
        for b in reads:
            b.r.append(ev)
        for b in writes:
            b.w = ev
            b.r = []
        self.all_events[self._key(ev[0])] = ev

    def op(self, eng, fn, reads=(), writes=(), extra=()):
        evs = self._deps(reads, writes) + list(extra)
        waits = self._prune(eng, evs)
        self.cnt[eng] += 1
        ev = (self.esem[eng], self.cnt[eng])
        self.streams[eng].append((fn, waits, ev, 1))
        self._commit(ev, reads, writes)
        return ev

    def dma(self, eng, fn, reads=(), writes=(), extra=()):
        slots = self.dsl[eng]
        slot = slots[self.dqi[eng] % len(slots)]
        self.dqi[eng] += 1
        evs = self._deps(reads, writes) + list(extra)
        if slot[1] > 0:
            evs.append((slot[0], slot[1]))
        waits = self._prune(eng, evs)
        slot[1] += 16
        ev = (slot[0], slot[1])
        self.streams[eng].append((fn, waits, ev, 16))
        self._commit(ev, reads, writes)
        return ev

    def last_ev(self, eng):
        return (self.esem[eng], self.cnt[eng])

    def barrier(self):
        evs = list(self.all_events.values())
        for e in self.ENGS:
            waits = self._prune(e, evs)
            if waits:
                self.streams[e].append((None, waits, None, 0))

    def emit(self, nc):
        engmap = {}
        with nc.Block() as block:
            def run(eng_name):
                def body(e):
                    for fn, waits, ev, inc in self.streams[eng_name]:
                        for sem, val in waits:
                            e.wait_ge(sem, val)
                        if fn is None:
                            continue
                        ins = fn(e)
                        ins.then_inc(ev[0], inc)
                return body
            block.tensor(run("pe"))
            block.scalar(run("act"))
            block.vector(run("dve"))
            block.gpsimd(run("pool"))
            block.sync(run("sp"))


class TB:
    def __init__(self, t, nb=1):
        self.t = t
        self.b = [Buf() for _ in range(nb)]


class Ring:
    def __init__(self, items):
        self.items = items
        self.i = 0

    def next(self):
        it = self.items[self.i % len(self.items)]
        self.i += 1
        return it


def build(S, C, DEPTH, ffn_tb=1024, mix_tb=512, stop=99, dbg=False, skipffn=False):
    T = S + C
    NBT = T // 128
    NBL = S // 128
    NBC = C // 128
    NT = S // 512
    GS = min(8, NBL)
    NG = NBL // GS
    NV = DEPTH * 3 + 1

    nc = bass.Bass("TRN2", target_bir_lowering=False)

    def din(name, shape, dt=F32):
        return nc.dram_tensor(name, list(shape), dt, kind="ExternalInput").ap()

    x_l = din("x_l", [128, KC, S])
    ctx_l = din("ctx_l", [128, KC, C])
    cvec = din("cvec", [128, KC, 2])
    w_ada_l = din("w_ada_l", [DEPTH, 18, 128, KC, 512])
    b_ada_l = din("b_ada_l", [128, DEPTH, 72])
    gvec = din("gvec", [128, NV, KC])
    wgu_l = din("wgu_l", [DEPTH, 2, FC, 128, 2, KC, 128])
    wd_l = din("wd_l", [DEPTH, 2, KC, 128, FC, 128])
    win_l = din("win_l", [DEPTH, 40, 128, KC, 128])
    wp_l = din("wp_l", [DEPTH, KC, 128, 8, 128])
    wout_l = din("wout_l", [DEPTH, KC, 128, KC, 128])
    lrug_l = din("lrug_l", [DEPTH, 128, 2, 4, 2, 128])
    lconv_l = din("lconv_l", [128, DEPTH, 4, 5])
    lgate_l = din("lgate_l", [128, DEPTH, 2, 4, 3])
    sconv_l = din("sconv_l", [128, DEPTH, 2, 3])
    ccbd_l = din("ccbd_l", [128, 2, 128])
    cctx_l = din("cctx_l", [128, 2, NBC, C])
    NTH = NT // 2
    dft_l = din("dft_l", [NTH, NG, 128, 2, GS, 512], BF16)
    yT = nc.dram_tensor("yT", [128, KC, S], F32, kind="ExternalOutput").ap()
    hbuf = nc.dram_tensor("hbuf_i", [128, KC, T], F32).ap()
    yfs = nc.dram_tensor("yfs_i", [128, 2, T], BF16).ap()
    gls = nc.dram_tensor("gls_i", [128, 4, T], BF16).ap()
    ycs = nc.dram_tensor("ycs_i", [128, 2, T], BF16).ap()
    zscr = nc.dram_tensor("zscr_i", [T // 128, 128, 512], BF16).ap()
    if dbg:
        hdump = nc.dram_tensor("hbuf", [128, KC, T], F32, kind="ExternalOutput").ap()
        yfdump = nc.dram_tensor("yfs", [128, 2, T], BF16, kind="ExternalOutput").ap()
        moddbg = nc.dram_tensor("moddbg", [128, DEPTH, 72, 2], F32, kind="ExternalOutput").ap()
        dbgGL = nc.dram_tensor("dbgGL", [128, 4, T], BF16, kind="ExternalOutput").ap()
        dbgYC = nc.dram_tensor("dbgYC", [128, 2, T], BF16, kind="ExternalOutput").ap()
        dbgLX = nc.dram_tensor("dbgLX", [128, 4, T], BF16, kind="ExternalOutput").ap()
        dbgU = nc.dram_tensor("dbgU", [128, KC, 512], BF16, kind="ExternalOutput").ap()
        dbgR = nc.dram_tensor("dbgR", [128, 512], F32, kind="ExternalOutput").ap()
        dbgSQ = nc.dram_tensor("dbgSQ", [128, KC, 512], BF16, kind="ExternalOutput").ap()
        dbgH = nc.dram_tensor("dbgH", [128, KC, 512], F32, kind="ExternalOutput").ap()
        dbgH0 = nc.dram_tensor("dbgH0", [128, KC, 512], F32, kind="ExternalOutput").ap()
        hdump0 = nc.dram_tensor("hdump0", [128, KC, T], F32, kind="ExternalOutput").ap()

    _uid = [0]

    def sbt(name, shape, dt):
        _uid[0] += 1
        return nc.sbuf_tensor(f"{name}_{_uid[0]}", shape, dt)

    es = ExitStack()
    with es:
        def sb(name, shape, dt):
            return es.enter_context(sbt(name, list(shape), dt))

        esems = {e: es.enter_context(nc.semaphore("es_" + e)) for e in Prog.ENGS}
        dsems = {
            "pool": [es.enter_context(nc.semaphore(f"dq_pool{i}")) for i in range(8)],
            "sp": [es.enter_context(nc.semaphore(f"dq_sp{i}")) for i in range(8)],
        }
        P = Prog(esems, dsems)

        psb = [TB(es.enter_context(nc.psum_tensor(f"ps{i}", [128, 512], F32))) for i in range(8)]
        psring = Ring(psb)

        ones_bf = TB(sb("ones_bf", [128, 128], BF16))
        s2 = TB(sb("s2", [128, KC, 2], F32))
        s2b = TB(sb("s2b", [128, KC, 2], BF16))
        modT = TB(sb("modT", [128, DEPTH, 72, 2], F32))
        Aall = TB(sb("Aall", [128, NV, KC, 2], F32))
        gv = TB(sb("gv", [128, NV, KC], F32))
        bada = TB(sb("bada", [128, DEPTH, 72], F32))
        Gall = TB(sb("Gall", [128, DEPTH * 3, KC, 2], F32))
        lconv = TB(sb("lconv", [128, DEPTH, 4, 5], F32))
        lgate = TB(sb("lgate", [128, DEPTH, 2, 4, 3], F32))
        lc = TB(sb("lc", [128, DEPTH, 2, 4, 2], F32))
        lgh = TB(sb("lgh", [128, DEPTH, 2, 4, 2], F32))
        sconv = TB(sb("sconv", [128, DEPTH, 2, 3], F32))
        ccbd = TB(sb("ccbd", [128, 2, 128], BF16))
        cctx = TB(sb("cctx", [128, 2, NBC, C], BF16))
        lrug = TB(sb("lrug", [128, 2, 4, 2, 128], BF16))
        epsb = TB(sb("epsb", [128, 1], F32))
        oneb = TB(sb("oneb", [128, 1], F32))
        qtrb = TB(sb("qtrb", [128, 1], F32))
        isqb = TB(sb("isqb", [128, 2], BF16))

        hblk = [Buf() for _ in range(T // 128)]
        glsb = [Buf() for _ in range(4)]
        ycsb = Buf()
        yfblk = [Buf() for _ in range(T // 128)]

        def blocks(bl, t0, n):
            return bl[t0 // 128:(t0 + n) // 128]

        def make_tiles(tb, with_ctx=True):
            res = []
            if with_ctx:
                t = S
                while t < T:
                    n = min(tb, T - t)
                    res.append((t, n, True))
                    t += n
            t = 0
            while t < S:
                n = min(tb, S - t)
                res.append((t, n, False))
                t += n
            return res

        def subs_of(n):
            return [(o, min(512, n - o)) for o in range(0, n, 512)]

        P.op("dve", lambda e: e.memset(ones_bf.t[:], 1.0), writes=ones_bf.b)
        P.op("dve", lambda e: e.memset(epsb.t[:], float(D * RMS_EPS)), writes=epsb.b)
        P.op("dve", lambda e: e.memset(oneb.t[:], 1.0), writes=oneb.b)
        P.op("dve", lambda e: e.memset(qtrb.t[:], 0.25), writes=qtrb.b)
        P.op("dve", lambda e: e.memset(isqb.t[:], 1.0 / math.sqrt(S)), writes=isqb.b)
        cv = TB(sb("cv", [128, KC, 2], F32))
        P.dma("sp", lambda e: e.dma_start(out=cv.t[:], in_=cvec), writes=cv.b)
        P.dma("sp", lambda e: e.dma_start(out=gv.t[:], in_=gvec), writes=gv.b)
        P.dma("sp", lambda e: e.dma_start(out=bada.t[:], in_=b_ada_l), writes=bada.b)
        P.dma("sp", lambda e: e.dma_start(out=lconv.t[:], in_=lconv_l), writes=lconv.b)
        P.dma("sp", lambda e: e.dma_start(out=lgate.t[:], in_=lgate_l), writes=lgate.b)
        P.dma("sp", lambda e: e.dma_start(out=sconv.t[:], in_=sconv_l), writes=sconv.b)
        P.dma("pool", lambda e: e.dma_start(out=ccbd.t[:], in_=ccbd_l), writes=ccbd.b)
        P.dma("pool", lambda e: e.dma_start(out=cctx.t[:], in_=cctx_l), writes=cctx.b)
        P.op("act", lambda e: e.activation(out=s2.t[:], in_=cv.t[:], func=AF.Silu),
             reads=cv.b, writes=s2.b)
        P.op("act", lambda e: e.activation(out=s2b.t[:], in_=cv.t[:], func=AF.Silu),
             reads=cv.b, writes=s2b.b)
        lt = TB(sb("lt", [128, DEPTH, 2, 4], F32))
        P.op("act", lambda e: e.activation(out=lt.t[:], in_=lgate.t[:, :, :, :, 2], func=AF.Exp, scale=-1.0),
             reads=lgate.b, writes=lt.b)
        P.op("act", lambda e: e.activation(out=lt.t[:], in_=lt.t[:], func=AF.Ln, bias=1.0),
             reads=lt.b, writes=lt.b)
        P.op("dve", lambda e: e.tensor_scalar(lc.t[:, :, :, :, 0], lt.t[:], -4.0, None, op0=ALU.mult),
             reads=lt.b, writes=lc.b)
        P.op("dve", lambda e: e.tensor_scalar(lc.t[:, :, :, :, 1], lt.t[:], -8.0, None, op0=ALU.mult),
             reads=lt.b + lc.b, writes=lc.b)
        P.op("dve", lambda e: e.tensor_scalar(lgh.t[:], lgate.t[:, :, :, :, 0:2], 0.5, None, op0=ALU.mult),
             reads=lgate.b, writes=lgh.b)

        parb = [Buf() for _ in range(DEPTH + 1)]
        pm_bank = psb[7]
        pmv_all = pm_bank.t[:, 0:144].rearrange("p (j t) -> p j t", t=2)
        NAP = 36

        def adaln_steps(l, wa_ring):
            for piece in range(NAP):
                w32, wa = wa_ring.next()
                P.dma("sp", lambda e, w32=w32, piece=piece: e.dma_start(
                    out=w32.t[:], in_=w_ada_l[l, piece // 2][:, :, (piece % 2) * 256:(piece % 2 + 1) * 256]),
                    writes=w32.b)
                P.op("act", lambda e, w32=w32, wa=wa: e.activation(out=wa.t[:], in_=w32.t[:], func=AF.Copy),
                     reads=w32.b, writes=wa.b)

                def fn(e, wa=wa, piece=piece):
                    ins = None
                    for j in range(2):
                        for kc in range(KC):
                            ins = e.matmul(pmv_all[:, piece * 2 + j, :], wa.t[:, kc, j * 128:(j + 1) * 128],
                                           s2b.t[:, kc, :], start=(kc == 0), stop=(kc == KC - 1))
                    return ins
                P.op("pe", fn, reads=wa.b + s2b.b, writes=pm_bank.b if piece == 0 else ())
                pm_bank.b[0].w = (P.esem["pe"], P.cnt["pe"])
                yield
            for t in range(2):
                P.op("dve", lambda e, t=t: e.tensor_tensor(
                    out=modT.t[:, l, :, t], in0=pmv_all[:, :, t], in1=bada.t[:, l, :], op=ALU.add),
                    reads=pm_bank.b + bada.b, writes=[parb[l]])
            for w in range(3):
                for t in range(2):
                    P.op("dve", lambda e, w=w, t=t: e.scalar_tensor_tensor(
                        out=Aall.t[:, l * 3 + w, :, t], in0=modT.t[:, l, (3 * w + 1) * 8:(3 * w + 2) * 8, t],
                        scalar=1.0, in1=gv.t[:, l * 3 + w, :], op0=ALU.add, op1=ALU.mult),
                        reads=gv.b + [parb[l]], writes=[parb[l]])
                    P.op("dve", lambda e, w=w, t=t: e.tensor_scalar(
                        Gall.t[:, l * 3 + w, :, t], modT.t[:, l, (3 * w + 2) * 8:(3 * w + 3) * 8, t],
                        (1.0 if w == 1 else 0.5), None, op0=ALU.mult),
                        reads=[parb[l]], writes=[parb[l]])
            P.op("dve", lambda e: e.tensor_scalar(Aall.t[:, l * 3:l * 3 + 3], Aall.t[:, l * 3:l * 3 + 3], 32.0, None,
                                                  op0=ALU.mult), reads=[parb[l]], writes=[parb[l]])
            yield

        with ExitStack() as es2:
            wa_ring0 = Ring([(TB(es2.enter_context(sbt(f"wada32_{i}", [128, KC, 256], F32))),
                              TB(es2.enter_context(sbt(f"wada{i}", [128, KC, 256], BF16)))) for i in range(2)])
            for _ in adaln_steps(0, wa_ring0):
                pass
            P.barrier()
        P.op("dve", lambda e: e.tensor_scalar(Aall.t[:, NV - 1, :, 0], gv.t[:, NV - 1, :], 32.0, None, op0=ALU.mult),
             reads=gv.b, writes=[parb[DEPTH]])

        if dbg:
            P.dma("sp", lambda e: e.dma_start(out=moddbg, in_=modT.t[:]), reads=[parb[0]], writes=[Buf()])

        def A_ap(l, w, kc, t):
            return Aall.t[:, l * 3 + w, kc, t:t + 1]

        def B_ap(l, w, kc, t):
            return modT.t[:, l, (3 * w) * 8 + kc, t:t + 1]

        def G_ap(l, w, kc, t):
            return Gall.t[:, l * 3 + w, kc, t:t + 1]

        def norm_mod(hin, n, xn, sqb, rstd, tmps, a_fn, b_fn, pb, stage="all"):
            for si, (o, m) in enumerate(subs_of(n)):
                sq = sqb[si] if isinstance(sqb, list) else sqb
                rs = rstd[si] if isinstance(rstd, list) else rstd
                if stage in ("all", "sq"):
                    P.op("act", lambda e, o=o, m=m, sq=sq: e.activation(
                        out=sq.t[:, :, 0:m], in_=hin.t[:, :, o:o + m], func=AF.Square),
                        reads=[hin.b[si]], writes=sq.b)
                if stage in ("all", "stat"):
                    pn = psring.next()

                    def fn(e, pn=pn, m=m, sq=sq):
                        ins = None
                        for kc in range(KC):
                            ins = e.matmul(pn.t[:, 0:m], ones_bf.t[:], sq.t[:, kc, 0:m],
                                           start=(kc == 0), stop=(kc == KC - 1))
                        return ins
                    P.op("pe", fn, reads=sq.b + ones_bf.b, writes=pn.b)
                    P.op("act", lambda e, pn=pn, m=m, rs=rs: e.activation(
                        out=rs.t[:, 0:m], in_=pn.t[:, 0:m], func=AF.Sqrt, bias=epsb.t[:, 0:1]),
                        reads=pn.b + epsb.b, writes=rs.b)
                    P.op("dve", lambda e, m=m, rs=rs: e.reciprocal(out=rs.t[:, 0:m], in_=rs.t[:, 0:m]),
                         reads=rs.b, writes=rs.b)
                if stage in ("all", "apply"):
                    for kc in range(KC):
                        tm = tmps.next()
                        P.op("dve", lambda e, tm=tm, kc=kc, o=o, m=m, rs=rs: e.tensor_tensor(
                            out=tm.t[:, 0:m], in0=hin.t[:, kc, o:o + m], in1=rs.t[:, 0:m], op=ALU.mult),
                            reads=[hin.b[si]] + rs.b, writes=tm.b)
                        bb = b_fn(kc)
                        aa = a_fn(kc)
                        P.op("act", lambda e, tm=tm, kc=kc, o=o, m=m, bb=bb, aa=aa, si=si: e.activation(
                            out=xn.t[:, kc, o:o + m], in_=tm.t[:, 0:m], func=AF.Identity,
                            bias=bb, scale=aa),
                            reads=tm.b + pb, writes=[xn.b[si]])

        def load_tile(hin, t0, n, from_input):
            if from_input:
                src = ctx_l[:, :, t0 - S:t0 - S + n] if t0 >= S else x_l[:, :, t0:t0 + n]
                rd = []
            else:
                src = hbuf[:, :, t0:t0 + n]
                rd = blocks(hblk, t0, n)
            P.dma("sp", lambda e: e.dma_start(out=hin.t[:, :, 0:n], in_=src), reads=rd, writes=hin.b)

        def store_tile(hin, t0, n):
            P.dma("sp", lambda e: e.dma_start(out=hbuf[:, :, t0:t0 + n], in_=hin.t[:, :, 0:n]),
                  reads=hin.b, writes=blocks(hblk, t0, n))

        def ffn_phase(l, j, from_input, with_ctx, final_norm=False):
            w = 0 if j == 0 else 2
            tiles = make_tiles(ffn_tb, with_ctx)
            NS = ffn_tb // 512
            with ExitStack() as e2:
                def sb2(name, shape, dt):
                    return e2.enter_context(sbt(name, list(shape), dt))
                hins = Ring([TB(sb2(f"f_hin{i}", [128, KC, ffn_tb], F32), NS) for i in range(2)])
                xn = TB(sb2("f_xn", [128, KC, ffn_tb], BF16), NS)
                At = [[None] * NS for _ in range(FC)]
                Atile = sb2("f_A", [128, FC, ffn_tb], BF16)
                Ab = [[Buf() for _ in range(NS)] for _ in range(FC)]
                sqb = [TB(sb2(f"f_sqb{i}", [128, KC, 512], BF16)) for i in range(NS)]
                rstd = [TB(sb2(f"f_rstd{i}", [128, 512], F32)) for i in range(NS)]
                tmps = Ring([TB(sb2(f"f_tmp{i}", [128, 512], F32)) for i in range(2)])
                sgs = Ring([TB(sb2(f"f_sg{i}", [128, 512], F32)) for i in range(2)])
                wgus = Ring([TB(sb2(f"f_wgu{i}", [128, 2, KC, 128], BF16)) for i in range(4)])
                wds = Ring([TB(sb2(f"f_wd{i}", [128, FC, 128], BF16)) for i in range(3)])
                if final_norm:
                    outs = Ring([TB(sb2(f"f_out{i}", [128, 512], F32)) for i in range(2)])

                def do_norm(hin, n, tc, stage="all"):
                    norm_mod(hin, n, xn, sqb, rstd, tmps,
                             lambda kc: A_ap(l, w, kc, tc), lambda kc: B_ap(l, w, kc, tc), [parb[l]], stage=stage)

                cur = hins.next()
                load_tile(cur, tiles[0][0], tiles[0][1], from_input)
                do_norm(cur, tiles[0][1], 1 if tiles[0][2] else 0)
                for ti, (t0, n, isc) in enumerate(tiles):
                    tc = 1 if isc else 0
                    subs = subs_of(n)
                    nxt = None
                    if ti + 1 < len(tiles):
                        nxt = hins.next()
                        load_tile(nxt, tiles[ti + 1][0], tiles[ti + 1][1], from_input)
                    for fc in range(FC):
                        if nxt is not None and fc == 6:
                            do_norm(nxt, tiles[ti + 1][1], 1 if tiles[ti + 1][2] else 0, "sq")
                        if nxt is not None and fc == 12:
                            do_norm(nxt, tiles[ti + 1][1], 1 if tiles[ti + 1][2] else 0, "stat")
                        wgu = wgus.next()
                        P.dma("pool", lambda e, wgu=wgu, fc=fc: e.dma_start(out=wgu.t[:], in_=wgu_l[l, j, fc]),
                              writes=wgu.b)
                        for si, (o, m) in enumerate(subs):
                            pg = psring.next()
                            pu = psring.next()

                            def fn(e, wgu=wgu, pg=pg, pu=pu, o=o, m=m):
                                ins = None
                                for gu, pp in ((0, pg), (1, pu)):
                                    for kc in range(KC):
                                        ins = e.matmul(pp.t[:, 0:m], wgu.t[:, gu, kc, :], xn.t[:, kc, o:o + m],
                                                       start=(kc == 0), stop=(kc == KC - 1))
                                return ins
                            P.op("pe", fn, reads=wgu.b + [xn.b[si]], writes=pg.b + pu.b)
                            sg = sgs.next()
                            P.op("act", lambda e, sg=sg, pg=pg, m=m: e.activation(
                                out=sg.t[:, 0:m], in_=pg.t[:, 0:m], func=AF.Silu),
                                reads=pg.b, writes=sg.b)
                            P.op("dve", lambda e, sg=sg, pu=pu, fc=fc, o=o, m=m: e.tensor_tensor(
                                out=Atile[:, fc, o:o + m], in0=sg.t[:, 0:m], in1=pu.t[:, 0:m], op=ALU.mult),
                                reads=sg.b + pu.b, writes=[Ab[fc][si]])
                    if nxt is not None:
                        do_norm(nxt, tiles[ti + 1][1], 1 if tiles[ti + 1][2] else 0, "apply")
                    for dc in range(KC):
                        wd = wds.next()
                        P.dma("pool", lambda e, wd=wd, dc=dc: e.dma_start(out=wd.t[:], in_=wd_l[l, j, dc]),
                              writes=wd.b)
                        for si, (o, m) in enumerate(subs):
                            po = psring.next()

                            def fn(e, wd=wd, po=po, o=o, m=m):
                                ins = None
                                for fc in range(FC):
                                    ins = e.matmul(po.t[:, 0:m], wd.t[:, fc, :], Atile[:, fc, o:o + m],
                                                   start=(fc == 0), stop=(fc == FC - 1))
                                return ins
                            P.op("pe", fn, reads=wd.b + [Ab[fc][si] for fc in range(FC)], writes=po.b)
                            P.op("dve", lambda e, po=po, dc=dc, o=o, m=m, cur=cur, tc=tc: e.scalar_tensor_tensor(
                                out=cur.t[:, dc, o:o + m], in0=po.t[:, 0:m], scalar=G_ap(l, w, dc, tc),
                                in1=cur.t[:, dc, o:o + m], op0=ALU.mult, op1=ALU.add),
                                reads=po.b + [parb[l]] + [cur.b[si]], writes=[cur.b[si]])
                    if not final_norm:
                        store_tile(cur, t0, n)
                    else:
                        for si, (o, m) in enumerate(subs):
                            sq, rs = sqb[si], rstd[si]
                            P.op("act", lambda e, o=o, m=m, cur=cur, sq=sq: e.activation(
                                out=sq.t[:, :, 0:m], in_=cur.t[:, :, o:o + m], func=AF.Square),
                                reads=[cur.b[si]], writes=sq.b)
                            pn = psring.next()

                            def fn(e, pn=pn, m=m, sq=sq):
                                ins = None
                                for kc in range(KC):
                                    ins = e.matmul(pn.t[:, 0:m], ones_bf.t[:], sq.t[:, kc, 0:m],
                                                   start=(kc == 0), stop=(kc == KC - 1))
                                return ins
                            P.op("pe", fn, reads=sq.b + ones_bf.b, writes=pn.b)
                            P.op("act", lambda e, pn=pn, m=m, rs=rs: e.activation(
                                out=rs.t[:, 0:m], in_=pn.t[:, 0:m], func=AF.Sqrt, bias=epsb.t[:, 0:1]),
                                reads=pn.b + epsb.b, writes=rs.b)
                            P.op("dve", lambda e, m=m, rs=rs: e.reciprocal(out=rs.t[:, 0:m], in_=rs.t[:, 0:m]),
                                 reads=rs.b, writes=rs.b)
                            for kc in range(KC):
                                ot = outs.next()
                                P.op("dve", lambda e, ot=ot, kc=kc, o=o, m=m, cur=cur, rs=rs: e.scalar_tensor_tensor(
                                    out=ot.t[:, 0:m], in0=cur.t[:, kc, o:o + m], scalar=Aall.t[:, NV - 1, kc, 0:1],
                                    in1=rs.t[:, 0:m], op0=ALU.mult, op1=ALU.mult),
                                    reads=[cur.b[si]] + rs.b + [parb[DEPTH]], writes=ot.b)
                                P.dma("sp", lambda e, ot=ot, kc=kc, o=o, m=m, t0=t0: e.dma_start(
                                    out=yT[:, kc, t0 + o:t0 + o + m], in_=ot.t[:, 0:m]),
                                    reads=ot.b, writes=[Buf()])
                    cur = nxt
                P.barrier()

        def mixer(l, from_input_unused, last):
            tb = mix_tb
            tiles1 = make_tiles(tb, True)
            tiles2 = make_tiles(tb, not last)
            with ExitStack() as em:
                def sbm(name, shape, dt):
                    return em.enter_context(sbt(name, list(shape), dt))
                GL = sbm("m_GL", [128, 4, T], BF16)
                el = ExitStack()
                LX = el.enter_context(sbt("m_LX", [128, 4, T], BF16))
                GLb = [[Buf() for _ in range(NBT)] for _ in range(4)]
                LXb = [[Buf() for _ in range(NBT)] for _ in range(4)]
                zsb = [Buf() for _ in range(NBT)]
                P.dma("pool", lambda e: e.dma_start(out=lrug.t[:], in_=lrug_l[l]), writes=lrug.b)

                def dump_mix():
                    if dbg:
                        P.dma("sp", lambda e: e.dma_start(out=dbgGL, in_=GL[:]), writes=[Buf()])
                        P.dma("sp", lambda e: e.dma_start(out=dbgLX, in_=LX[:]), writes=[Buf()])
                        P.dma("sp", lambda e: e.dma_start(out=dbgYC, in_=ycs), reads=[ycsb], writes=[Buf()])
                        P.barrier()

                if True:
                    with ExitStack() as e1:
                        def sb1(name, shape, dt):
                            return e1.enter_context(sbt(name, list(shape), dt))
                        hins1 = Ring([TB(sb1(f"p1_hin{i}", [128, KC, tb], F32)) for i in range(2)])
                        us1 = Ring([TB(sb1(f"p1_u{i}", [128, KC, tb], BF16)) for i in range(2)])
                        sqb = TB(sb1("p1_sqb", [128, KC, 512], BF16))
                        rstd = TB(sb1("p1_rstd", [128, 512], F32))
                        tmps = Ring([TB(sb1(f"p1_tmp{i}", [128, 512], F32)) for i in range(2)])
                        xfb = TB(sb1("p1_xfb", [128, 2, 512], BF16), 2)
                        sBt = Ring([TB(sb1(f"p1_sB{i}", [128, 512], F32)) for i in range(2)])
                        sCt = Ring([TB(sb1(f"p1_sC{i}", [128, 512], F32)) for i in range(2)])
                        qt = TB(sb1("p1_q", [128, 512], F32))
                        yct = TB(sb1("p1_yc", [128, 512], F32))
                        yco = Ring([TB(sb1(f"p1_yco{i}", [128, 512], BF16)) for i in range(2)])
                        zst = Ring([TB(sb1(f"p1_zst{i}", [128, 512], BF16)) for i in range(3)])
                        wres = [TB(sb1(f"p1_w{i}", [128, KC, 128], BF16)) for i in range(16)]
                        for cch in range(16):
                            P.dma("pool", lambda e, cch=cch: e.dma_start(out=wres[cch].t[:], in_=win_l[l, cch]),
                                  writes=wres[cch].b)

                        def p1_norm(hin, u, n, isc):
                            tc = 1 if isc else 0
                            norm_mod(hin, n, u, sqb, rstd, tmps,
                                     lambda kc: A_ap(l, 1, kc, tc), lambda kc: B_ap(l, 1, kc, tc), [parb[l]])

                        def p1_proj(u, t0, n, isc, cchs, sBs, sCs):
                            b0 = t0 // 128
                            nb = n // 128
                            for cch in cchs:
                                wn = wres[cch]
                                pp = psring.next()

                                def fn(e, wn=wn, pp=pp):
                                    ins = None
                                    for kc in range(KC):
                                        ins = e.matmul(pp.t[:, 0:n], wn.t[:, kc, :], u.t[:, kc, 0:n],
                                                       start=(kc == 0), stop=(kc == KC - 1))
                                    return ins
                                P.op("pe", fn, reads=wn.b + u.b, writes=pp.b)
                                if cch < 2:
                                    P.op("act", lambda e, pp=pp, cch=cch: e.activation(
                                        out=xfb.t[:, cch, 0:n], in_=pp.t[:, 0:n], func=AF.Copy),
                                        reads=pp.b, writes=[xfb.b[cch]])
                                elif cch < 6:
                                    cc = cch - 2
                                    P.op("dve", lambda e, pp=pp, cc=cc: e.tensor_copy(
                                        out=LX[:, cc, t0:t0 + n], in_=pp.t[:, 0:n]),
                                        reads=pp.b, writes=LXb[cc][b0:b0 + nb])
                                elif cch < 10:
                                    cc = cch - 6
                                    P.op("act", lambda e, pp=pp, cc=cc: e.activation(
                                        out=GL[:, cc, t0:t0 + n], in_=pp.t[:, 0:n], func=AF.Gelu),
                                        reads=pp.b, writes=GLb[cc][b0:b0 + nb])
                                elif cch < 12:
                                    cc = cch - 10
                                    st = sBt.next()
                                    sBs[cc] = st
                                    P.op("act", lambda e, pp=pp, st=st: e.activation(
                                        out=st.t[:, 0:n], in_=pp.t[:, 0:n], func=AF.Copy),
                                        reads=pp.b, writes=st.b)
                                elif cch < 14:
                                    cc = cch - 12
                                    st = sCt.next()
                                    sCs[cc] = st
                                    P.op("act", lambda e, pp=pp, st=st: e.activation(
                                        out=st.t[:, 0:n], in_=pp.t[:, 0:n], func=AF.Copy),
                                        reads=pp.b, writes=st.b)
                                else:
                                    cc = cch - 14
                                    rw = n if isc else GRID_W
                                    sC = sCs[cc]
                                    sB = sBs[cc]
                                    P.op("dve", lambda e, pp=pp, sC=sC: e.tensor_tensor(
                                        out=qt.t[:, 0:n], in0=sC.t[:, 0:n], in1=pp.t[:, 0:n], op=ALU.mult),
                                        reads=pp.b + sC.b, writes=qt.b)
                                    P.op("act", lambda e, cc=cc: e.activation(
                                        out=yct.t[:, 0:n], in_=qt.t[:, 0:n], func=AF.Copy,
                                        scale=sconv.t[:, l, cc, 1:2]),
                                        reads=qt.b + sconv.b, writes=yct.b)
                                    q3 = qt.t[:, 0:n].rearrange("p (r w) -> p r w", w=rw)
                                    y3 = yct.t[:, 0:n].rearrange("p (r w) -> p r w", w=rw)
                                    P.op("dve", lambda e, cc=cc, q3=q3, y3=y3, rw=rw: e.scalar_tensor_tensor(
                                        out=y3[:, :, 1:rw], in0=q3[:, :, 0:rw - 1], scalar=sconv.t[:, l, cc, 0:1],
                                        in1=y3[:, :, 1:rw], op0=ALU.mult, op1=ALU.add),
                                        reads=qt.b + sconv.b + yct.b, writes=yct.b)
                                    P.op("dve", lambda e, cc=cc, q3=q3, y3=y3, rw=rw: e.scalar_tensor_tensor(
                                        out=y3[:, :, 0:rw - 1], in0=q3[:, :, 1:rw], scalar=sconv.t[:, l, cc, 2:3],
                                        in1=y3[:, :, 0:rw - 1], op0=ALU.mult, op1=ALU.add),
                                        reads=qt.b + sconv.b + yct.b, writes=yct.b)
                                    yo = yco.next()
                                    P.op("dve", lambda e, yo=yo, sB=sB: e.tensor_tensor(
                                        out=yo.t[:, 0:n], in0=yct.t[:, 0:n], in1=sB.t[:, 0:n], op=ALU.mult),
                                        reads=yct.b + sB.b, writes=yo.b)
                                    P.dma("sp", lambda e, yo=yo, cc=cc: e.dma_start(
                                        out=ycs[:, cc, t0:t0 + n], in_=yo.t[:, 0:n]), reads=yo.b, writes=[ycsb])

                        def p1_zdft(t0, n):
                            b0 = t0 // 128
                            for bi in range(n // 128):
                                pz = psring.next()

                                def fn(e, pz=pz, bi=bi):
                                    ins = None
                                    for cs in range(2):
                                        for chc in range(2):
                                            ins = e.matmul(pz.t[:, cs * 256 + chc * 128:cs * 256 + (chc + 1) * 128],
                                                           xfb.t[:, chc, bi * 128:(bi + 1) * 128], ccbd.t[:, cs, :],
                                                           start=True, stop=True)
                                    return ins
                                P.op("pe", fn, reads=xfb.b + ccbd.b, writes=pz.b)
                                zt = zst.next()
                                P.op("act", lambda e, pz=pz, zt=zt: e.activation(
                                    out=zt.t[:], in_=pz.t[:], func=AF.Copy), reads=pz.b, writes=zt.b)
                                P.dma("sp", lambda e, zt=zt, bi=bi: e.dma_start(out=zscr[b0 + bi], in_=zt.t[:]),
                                      reads=zt.b, writes=[zsb[b0 + bi]])

                        cur_h, cur_u = hins1.next(), us1.next()
                        load_tile(cur_h, tiles1[0][0], tiles1[0][1], False)
                        p1_norm(cur_h, cur_u, tiles1[0][1], tiles1[0][2])
                        for ti, (t0, n, isc) in enumerate(tiles1):
                            nxt_h = nxt_u = None
                            if ti + 1 < len(tiles1):
                                nxt_h, nxt_u = hins1.next(), us1.next()
                                load_tile(nxt_h, tiles1[ti + 1][0], tiles1[ti + 1][1], False)
                            sBs, sCs = [None, None], [None, None]
                            p1_proj(cur_u, t0, n, isc, range(0, 8), sBs, sCs)
                            if nxt_h is not None:
                                p1_norm(nxt_h, nxt_u, tiles1[ti + 1][1], tiles1[ti + 1][2])
                            p1_proj(cur_u, t0, n, isc, range(8, 16), sBs, sCs)
                            p1_zdft(t0, n)
                            cur_h, cur_u = nxt_h, nxt_u
                        P.barrier()

                    if stop <= 1.2:
                        dump_mix()
                        return
                    with ExitStack() as e1:
                        Zt = e1.enter_context(sbt("d_Z", [128, NBT, 512], BF16))
                        Zb = [Buf() for _ in range(NBT)]
                        zh = NBT // 2
                        for (za, zb_) in ((0, zh), (zh, NBT)):
                            P.dma("sp", lambda e, za=za, zb_=zb_: e.dma_start(
                                out=Zt[:, za:zb_, :], in_=zscr[za:zb_].rearrange("b p c -> p b c")),
                                reads=zsb[za:zb_], writes=Zb[za:zb_])
                        dfr = Ring([TB(e1.enter_context(sbt(f"d_cs{i}", [128, 2, GS, 512], BF16)))
                                    for i in range(2)])
                        yfo = Ring([TB(e1.enter_context(sbt(f"d_yf{i}", [128, 512], BF16)))
                                    for i in range(6)])
                        bsr = Ring([TB(e1.enter_context(sbt(f"d_bs{i}", [128, 512], F32))) for i in range(2)])
                        isq = 1.0 / math.sqrt(S)
                        p0 = psring.next()

                        def fn0(e, p0=p0):
                            ins = None
                            for chc in range(2):
                                for tin in range(NBL):
                                    ins = e.matmul(p0.t[:, chc:chc + 1], Zt[:, tin, chc * 128:(chc + 1) * 128],
                                                   isqb.t[:, 0:1], start=(tin == 0), stop=(tin == NBL - 1))
                            return ins
                        P.op("pe", fn0, reads=Zb[0:NBL] + isqb.b, writes=p0.b)
                        y0 = yfo.next()
                        P.op("act", lambda e, y0=y0, p0=p0: e.activation(out=y0.t[:, 0:2], in_=p0.t[:, 0:2], func=AF.Copy),
                             reads=p0.b, writes=y0.b)
                        for chc in range(2):
                            P.dma("sp", lambda e, y0=y0, chc=chc: e.dma_start(out=yfs[:, chc, 0:1], in_=y0.t[:, chc:chc + 1],
                                                                                 allow_slow_non_contiguous=True),
                                  reads=y0.b, writes=blocks(yfblk, 0, 128))
                        for tk in range(NTH):
                            pa = [psring.next(), psring.next()]
                            pb = [psring.next(), psring.next()]
                            for g in range(NG):
                                dd = dfr.next()
                                P.dma("sp", lambda e, dd=dd, tk=tk, g=g: e.dma_start(
                                    out=dd.t[:], in_=dft_l[tk, g]), writes=dd.b)
                                for chc in range(2):
                                    def fn(e, dd=dd, g=g, chc=chc, pac=pa[chc], pbc=pb[chc]):
                                        ins = None
                                        for cs, pp in ((0, pac), (1, pbc)):
                                            for blk in range(GS):
                                                tin = g * GS + blk
                                                ins = e.matmul(pp.t[:, :],
                                                               Zt[:, tin, cs * 256 + chc * 128:cs * 256 + (chc + 1) * 128],
                                                               dd.t[:, cs, blk, :],
                                                               start=(g == 0 and blk == 0),
                                                               stop=(g == NG - 1 and blk == GS - 1))
                                        return ins
                                    P.op("pe", fn, reads=dd.b + Zb[g * GS:(g + 1) * GS],
                                         writes=(pa[chc].b + pb[chc].b) if g == 0 else ())
                                    pa[chc].b[0].w = P.last_ev("pe")
                                    pb[chc].b[0].w = P.last_ev("pe")
                            for chc in range(2):
                                bs = bsr.next()
                                P.op("act", lambda e, bs=bs, pbc=pb[chc]: e.activation(
                                    out=bs.t[:], in_=pbc.t[:], func=AF.Copy), reads=pb[chc].b, writes=bs.b)
                                yd, ym = yfo.next(), yfo.next()
                                P.op("dve", lambda e, yd=yd, bs=bs, pac=pa[chc]: e.tensor_tensor(
                                    out=yd.t[:], in0=pac.t[:], in1=bs.t[:], op=ALU.add),
                                    reads=pa[chc].b + bs.b, writes=yd.b)
                                P.op("dve", lambda e, ym=ym, bs=bs, pac=pa[chc]: e.tensor_tensor(
                                    out=ym.t[:, ::-1], in0=pac.t[:], in1=bs.t[:], op=ALU.subtract),
                                    reads=pa[chc].b + bs.b, writes=ym.b)
                                d0 = 1 + 512 * tk
                                P.dma("sp", lambda e, yd=yd, chc=chc, d0=d0: e.dma_start(
                                    out=yfs[:, chc, d0:d0 + 512], in_=yd.t[:]),
                                    reads=yd.b, writes=yfblk[d0 // 128:(d0 + 511) // 128 + 1])
                                m0 = S - 512 * (tk + 1)
                                P.dma("sp", lambda e, ym=ym, chc=chc, m0=m0: e.dma_start(
                                    out=yfs[:, chc, m0:m0 + 512], in_=ym.t[:]),
                                    reads=ym.b, writes=blocks(yfblk, m0, 512))
                        if not last:
                            for chc in range(2):
                                pyc = psring.next()

                                def fn(e, chc=chc, pyc=pyc):
                                    ins = None
                                    for blk in range(NBC):
                                        for cs in range(2):
                                            ins = e.matmul(pyc.t[:, 0:C],
                                                           Zt[:, NBL + blk, cs * 256 + chc * 128:cs * 256 + (chc + 1) * 128],
                                                           cctx.t[:, cs, blk, :],
                                                           start=(blk == 0 and cs == 0),
                                                           stop=(blk == NBC - 1 and cs == 1))
                                    return ins
                                P.op("pe", fn, reads=cctx.b + Zb[NBL:NBT], writes=pyc.b)
                                yo = yfo.next()
                                P.op("act", lambda e, yo=yo, pyc=pyc: e.activation(
                                    out=yo.t[:, 0:C], in_=pyc.t[:, 0:C], func=AF.Copy), reads=pyc.b, writes=yo.b)
                                P.dma("sp", lambda e, yo=yo, chc=chc: e.dma_start(
                                    out=yfs[:, chc, S:T], in_=yo.t[:, 0:C]),
                                    reads=yo.b, writes=blocks(yfblk, S, C))
                        P.barrier()

                if stop <= 1.4:
                    dump_mix()
                    return
                with ExitStack() as e1:
                    def sb1(name, shape, dt):
                        return e1.enter_context(sbt(name, list(shape), dt))
                    xc = TB(sb1("s_xc", [128, T], F32))
                    xcb = TB(sb1("s_xcb", [128, T], BF16))
                    Hf = TB(sb1("s_Hf", [128, T], F32), NBT)
                    def rng(nm, k=2):
                        return Ring([TB(sb1(f"s_{nm}{i}", [128, 512], F32)) for i in range(k)])
                    SG = 3
                    ag = None
                    if l + 1 < DEPTH:
                        ag = adaln_steps(l + 1, Ring([(TB(sb1(f"s_wada32_{i}", [128, KC, 256], F32)),
                                                      TB(sb1(f"s_wada{i}", [128, KC, 256], BF16))) for i in range(2)]))
                    r_r, r_i, r_a, r_e, r_b, r_h = (rng("r", 2), rng("i", SG + 1), rng("a", SG + 1),
                                                    rng("e", SG + 1), rng("b", 3), rng("h", 3))
                    segs = [(S, C), (0, S)]
                    gring7 = Ring(psb[:7])
                    for cc in range(4):
                        if cc > 0:
                            P.dma("sp", lambda e, c0=cc - 1: e.dma_start(out=gls[:, c0, :], in_=GL[:, c0, :]),
                                  reads=list(GLb[cc - 1]), writes=[glsb[cc - 1]])
                        lxall = [b for b in LXb[cc]]
                        P.op("act", lambda e, cc=cc: e.activation(
                            out=xc.t[:], in_=LX[:, cc, :], func=AF.Identity,
                            bias=lconv.t[:, l, cc, 4:5], scale=lconv.t[:, l, cc, 2:3]),
                            reads=lxall + lconv.b, writes=xc.b)
                        for (s0, sn) in segs:
                            for (tap, sh) in ((0, 2), (1, 1)):
                                P.op("dve", lambda e, cc=cc, s0=s0, sn=sn, tap=tap, sh=sh: e.scalar_tensor_tensor(
                                    out=xc.t[:, s0 + sh:s0 + sn], in0=LX[:, cc, s0:s0 + sn - sh],
                                    scalar=lconv.t[:, l, cc, tap:tap + 1], in1=xc.t[:, s0 + sh:s0 + sn],
                                    op0=ALU.mult, op1=ALU.add),
                                    reads=lxall + lconv.b + xc.b, writes=xc.b)
                            P.op("dve", lambda e, cc=cc, s0=s0, sn=sn: e.scalar_tensor_tensor(
                                out=xc.t[:, s0:s0 + sn - 1], in0=LX[:, cc, s0 + 1:s0 + sn],
                                scalar=lconv.t[:, l, cc, 3:4], in1=xc.t[:, s0:s0 + sn - 1],
                                op0=ALU.mult, op1=ALU.add),
                                reads=lxall + lconv.b + xc.b, writes=xc.b)
                        P.op("act", lambda e: e.activation(out=xcb.t[:], in_=xc.t[:], func=AF.Copy),
                             reads=xc.b, writes=xcb.b)
                        for dr in range(2):
                            prev_h = None
                            plist = []
                            for (s0, sn) in segs:
                                pcs = [(s0 + o, m) for (o, m) in subs_of(sn)]
                                if dr == 1:
                                    pcs = pcs[::-1]
                                plist += [(p0, m, s0) for (p0, m) in pcs]
                            for gi in range(0, len(plist), SG):
                                grp = plist[gi:gi + SG]
                                st = []
                                if ag is not None:
                                    next(ag, None)
                                    next(ag, None)
                                for (p0, m, s0) in grp:
                                    pr = gring7.next()
                                    pi = gring7.next()

                                    def fn(e, pr=pr, pi=pi, p0=p0, m=m, cc=cc, dr=dr):
                                        e.matmul(pr.t[:, 0:m], lrug.t[:, dr, cc, 0, :], xcb.t[:, p0:p0 + m],
                                                 start=True, stop=True)
                                        return e.matmul(pi.t[:, 0:m], lrug.t[:, dr, cc, 1, :], xcb.t[:, p0:p0 + m],
                                                        start=True, stop=True)
                                    P.op("pe", fn, reads=lrug.b + xcb.b, writes=pr.b + pi.b)
                                    tr, ti_, ta, te = r_r.next(), r_i.next(), r_a.next(), r_e.next()
                                    P.op("act", lambda e, pr=pr, tr=tr, m=m, cc=cc, dr=dr: e.activation(
                                        out=tr.t[:, 0:m], in_=pr.t[:, 0:m], func=AF.Tanh, scale=0.5,
                                        bias=lgh.t[:, l, dr, cc, 0:1]), reads=pr.b + lgh.b, writes=tr.b)
                                    P.op("act", lambda e, pi=pi, ti_=ti_, m=m, cc=cc, dr=dr: e.activation(
                                        out=ti_.t[:, 0:m], in_=pi.t[:, 0:m], func=AF.Tanh, scale=0.5,
                                        bias=lgh.t[:, l, dr, cc, 1:2]), reads=pi.b + lgh.b, writes=ti_.b)
                                    P.op("act", lambda e, tr=tr, ta=ta, m=m, cc=cc, dr=dr: e.activation(
                                        out=ta.t[:, 0:m], in_=tr.t[:, 0:m], func=AF.Exp,
                                        scale=lc.t[:, l, dr, cc, 0:1], bias=lc.t[:, l, dr, cc, 0:1]),
                                        reads=tr.b + lc.b, writes=ta.b)
                                    P.op("dve", lambda e, ta=ta, m=m: e.tensor_scalar(
                                        ta.t[:, 0:m], ta.t[:, 0:m], 1.0, None, op0=ALU.min),
                                        reads=ta.b, writes=ta.b)
                                    P.op("dve", lambda e, ta=ta, te=te, m=m: e.tensor_tensor(
                                        out=te.t[:, 0:m], in0=ta.t[:, 0:m], in1=ta.t[:, 0:m], op=ALU.mult),
                                        reads=ta.b, writes=te.b)
                                    P.op("dve", lambda e, ti_=ti_, p0=p0, m=m: e.scalar_tensor_tensor(
                                        out=ti_.t[:, 0:m], in0=ti_.t[:, 0:m], scalar=1.0, in1=xc.t[:, p0:p0 + m],
                                        op0=ALU.add, op1=ALU.mult),
                                        reads=ti_.b + xc.b, writes=ti_.b)
                                    st.append((p0, m, s0, ti_, ta, te))
                                for (p0, m, s0, ti_, ta, te) in st:
                                    tb_ = r_b.next()
                                    P.op("act", lambda e, te=te, m=m: e.activation(
                                        out=te.t[:, 0:m], in_=te.t[:, 0:m], func=AF.Sqrt, scale=-0.25,
                                        bias=qtrb.t[:, 0:1]), reads=te.b + qtrb.b, writes=te.b)
                                    P.op("dve", lambda e, te=te, ti_=ti_, tb_=tb_, m=m: e.tensor_tensor(
                                        out=tb_.t[:, 0:m], in0=te.t[:, 0:m], in1=ti_.t[:, 0:m], op=ALU.mult),
                                        reads=te.b + ti_.b, writes=tb_.b)
                                    first_in_seq = (s0 == S and ((dr == 0 and p0 == S) or
                                                                 (dr == 1 and p0 + m == T)))
                                    blks = list(range(p0 // 128, (p0 + m) // 128))
                                    if dr == 0:
                                        init = 0.0 if first_in_seq else Hf.t[:, p0 - 1:p0] if p0 != 0 else Hf.t[:, T - 1:T]
                                        rd = ta.b + tb_.b
                                        if not first_in_seq:
                                            rd = rd + [Hf.b[(p0 - 1) // 128 if p0 != 0 else NBT - 1]]
                                        P.op("dve", lambda e, ta=ta, tb_=tb_, p0=p0, m=m, init=init: e.tensor_tensor_scan(
                                            out=Hf.t[:, p0:p0 + m], data0=ta.t[:, 0:m], data1=tb_.t[:, 0:m],
                                            initial=init, op0=ALU.mult, op1=ALU.add),
                                            reads=rd, writes=[Hf.b[k] for k in blks])
                                    else:
                                        th = r_h.next()
                                        if first_in_seq:
                                            init = 0.0
                                            rd = ta.b + tb_.b
                                        else:
                                            init = prev_h.t[:, 0:1]
                                            rd = ta.b + tb_.b + prev_h.b
                                        P.op("dve", lambda e, ta=ta, tb_=tb_, th=th, m=m, init=init: e.tensor_tensor_scan(
                                            out=th.t[:, 0:m][:, ::-1], data0=ta.t[:, 0:m][:, ::-1],
                                            data1=tb_.t[:, 0:m][:, ::-1],
                                            initial=init, op0=ALU.mult, op1=ALU.add),
                                            reads=rd, writes=th.b)
                                        prev_h = th
                                        P.op("dve", lambda e, th=th, tb_=tb_, p0=p0, m=m: e.tensor_tensor(
                                            out=tb_.t[:, 0:m], in0=th.t[:, 0:m], in1=Hf.t[:, p0:p0 + m], op=ALU.add),
                                            reads=th.b + [Hf.b[k] for k in blks] + tb_.b, writes=tb_.b)
                                        P.op("dve", lambda e, tb_=tb_, p0=p0, m=m, cc=cc: e.tensor_tensor(
                                            out=GL[:, cc, p0:p0 + m], in0=tb_.t[:, 0:m], in1=GL[:, cc, p0:p0 + m],
                                            op=ALU.mult),
                                            reads=tb_.b + [GLb[cc][k] for k in blks],
                                            writes=[GLb[cc][k] for k in blks])
                    P.dma("sp", lambda e: e.dma_start(out=gls[:, 3, :], in_=GL[:, 3, :]),
                          reads=list(GLb[3]), writes=[glsb[3]])
                    if ag is not None:
                        for _ in ag:
                            pass
                    P.barrier()
                    dump_mix()

                el.close()
                if stop <= 1.6:
                    return
                em.close()
                tb2 = 1024
                NS2 = tb2 // 512
                tiles2 = make_tiles(tb2, not last)
                with ExitStack() as e1:
                    def sb1(name, shape, dt):
                        return e1.enter_context(sbt(name, list(shape), dt))
                    hins2 = Ring([TB(sb1(f"p2_hin{i}", [128, KC, tb2], F32), NS2) for i in range(2)])
                    u2 = TB(sb1("p2_u", [128, KC, tb2], BF16), NS2)
                    sqb2 = [TB(sb1(f"p2_sqb{i}", [128, KC, 512], BF16)) for i in range(NS2)]
                    rstd2 = [TB(sb1(f"p2_rstd{i}", [128, 512], F32)) for i in range(NS2)]
                    tmps2 = Ring([TB(sb1(f"p2_tmp{i}", [128, 512], F32)) for i in range(2)])
                    ytl = Ring([TB(sb1(f"p2_y{i}", [128, 8, tb2], BF16)) for i in range(2)])
                    sgm = Ring([TB(sb1(f"p2_sg{i}", [128, 512], F32)) for i in range(3)])
                    mts = Ring([TB(sb1(f"p2_mt{i}", [128, 512], F32)) for i in range(3)])
                    maccs = Ring([TB(sb1(f"p2_macc{i}", [128, 512], F32)) for i in range(4)])
                    Mbt = sb1("p2_Mb", [128, KC, tb2], BF16)
                    Mbb = [[Buf() for _ in range(NS2)] for _ in range(KC)]
                    wgs = Ring([TB(sb1(f"p2_wg{i}", [128, KC, 128], BF16)) for i in range(4)])
                    wps = Ring([TB(sb1(f"p2_wp{i}", [128, 8, 128], BF16)) for i in range(2)])
                    wos = Ring([TB(sb1(f"p2_wo{i}", [128, KC, 128], BF16)) for i in range(2)])

                    def p2_load(hin, yt, t0, n):
                        load_tile(hin, t0, n, False)
                        P.dma("sp", lambda e: e.dma_start(out=yt.t[:, 0:2, 0:n], in_=yfs[:, :, t0:t0 + n]),
                              reads=blocks(yfblk, t0, n), writes=yt.b)
                        P.dma("sp", lambda e: e.dma_start(out=yt.t[:, 2:6, 0:n], in_=gls[:, :, t0:t0 + n]),
                              reads=glsb + yt.b, writes=yt.b)
                        P.dma("sp", lambda e: e.dma_start(out=yt.t[:, 6:8, 0:n], in_=ycs[:, :, t0:t0 + n]),
                              reads=[ycsb] + yt.b, writes=yt.b)

                    def p2_norm(hin, n, isc, stage="all"):
                        tc = 1 if isc else 0
                        norm_mod(hin, n, u2, sqb2, rstd2, tmps2,
                                 lambda kc: A_ap(l, 1, kc, tc), lambda kc: B_ap(l, 1, kc, tc), [parb[l]], stage=stage)

                    def p2_gates(yt, t0, n, hook=None):
                        subs = subs_of(n)
                        for dc in range(KC):
                            if hook is not None:
                                hook(dc)
                            wp = wps.next()
                            P.dma("pool", lambda e, wp=wp, dc=dc: e.dma_start(out=wp.t[:], in_=wp_l[l, dc]),
                                  writes=wp.b)
                            accs = [maccs.next() for _ in subs]
                            for br in range(3):
                                wg = wgs.next()
                                P.dma("pool", lambda e, wg=wg, br=br, dc=dc: e.dma_start(
                                    out=wg.t[:], in_=win_l[l, 16 + br * 8 + dc]), writes=wg.b)
                                k0, nk = ((0, 2), (2, 4), (6, 2))[br]
                                for si, (o, m) in enumerate(subs):
                                    pg = psring.next()
                                    pv = psring.next()

                                    def fn(e, wg=wg, wp=wp, pg=pg, pv=pv, o=o, m=m, k0=k0, nk=nk):
                                        ins = None
                                        for kc in range(KC):
                                            ins = e.matmul(pg.t[:, 0:m], wg.t[:, kc, :], u2.t[:, kc, o:o + m],
                                                           start=(kc == 0), stop=(kc == KC - 1))
                                        for i in range(nk):
                                            ins = e.matmul(pv.t[:, 0:m], wp.t[:, k0 + i, :], yt.t[:, k0 + i, o:o + m],
                                                           start=(i == 0), stop=(i == nk - 1))
                                        return ins
                                    P.op("pe", fn, reads=wg.b + wp.b + [u2.b[si]] + yt.b, writes=pg.b + pv.b)
                                    sg = sgm.next()
                                    P.op("act", lambda e, sg=sg, pg=pg, m=m: e.activation(
                                        out=sg.t[:, 0:m], in_=pg.t[:, 0:m], func=AF.Sigmoid),
                                        reads=pg.b, writes=sg.b)
                                    acc = accs[si]
                                    if br == 0:
                                        P.op("dve", lambda e, sg=sg, pv=pv, acc=acc, m=m: e.tensor_tensor(
                                            out=acc.t[:, 0:m], in0=sg.t[:, 0:m], in1=pv.t[:, 0:m], op=ALU.mult),
                                            reads=sg.b + pv.b, writes=acc.b)
                                    else:
                                        mt = mts.next()
                                        P.op("dve", lambda e, sg=sg, pv=pv, mt=mt, m=m: e.tensor_tensor(
                                            out=mt.t[:, 0:m], in0=sg.t[:, 0:m], in1=pv.t[:, 0:m], op=ALU.mult),
                                            reads=sg.b + pv.b, writes=mt.b)
                                        if br == 1:
                                            P.op("dve", lambda e, mt=mt, acc=acc, m=m: e.tensor_tensor(
                                                out=acc.t[:, 0:m], in0=acc.t[:, 0:m], in1=mt.t[:, 0:m], op=ALU.add),
                                                reads=mt.b + acc.b, writes=acc.b)
                                        else:
                                            P.op("dve", lambda e, mt=mt, acc=acc, m=m, o=o, dc=dc: e.tensor_tensor(
                                                out=Mbt[:, dc, o:o + m], in0=acc.t[:, 0:m], in1=mt.t[:, 0:m],
                                                op=ALU.add),
                                                reads=mt.b + acc.b, writes=[Mbb[dc][si]])

                    def p2_out(hin, t0, n, isc):
                        tc = 1 if isc else 0
                        subs = subs_of(n)
                        for dc in range(KC):
                            wo = wos.next()
                            P.dma("pool", lambda e, wo=wo, dc=dc: e.dma_start(out=wo.t[:], in_=wout_l[l, dc]),
                                  writes=wo.b)
                            for si, (o, m) in enumerate(subs):
                                po = psring.next()

                                def fn(e, wo=wo, po=po, o=o, m=m):
                                    ins = None
                                    for kc in range(KC):
                                        ins = e.matmul(po.t[:, 0:m], wo.t[:, kc, :], Mbt[:, kc, o:o + m],
                                                       start=(kc == 0), stop=(kc == KC - 1))
                                    return ins
                                P.op("pe", fn, reads=wo.b + [Mbb[kc][si] for kc in range(KC)], writes=po.b)
                                P.op("dve", lambda e, po=po, dc=dc, o=o, m=m, si=si: e.scalar_tensor_tensor(
                                    out=hin.t[:, dc, o:o + m], in0=po.t[:, 0:m], scalar=G_ap(l, 1, dc, tc),
                                    in1=hin.t[:, dc, o:o + m], op0=ALU.mult, op1=ALU.add),
                                    reads=po.b + [parb[l]] + [hin.b[si]], writes=[hin.b[si]])
                        store_tile(hin, t0, n)

                    cur_h, cur_y = hins2.next(), ytl.next()
                    p2_load(cur_h, cur_y, tiles2[0][0], tiles2[0][1])
                    p2_norm(cur_h, tiles2[0][1], tiles2[0][2])
                    for ti, (t0, n, isc) in enumerate(tiles2):
                        nxt_h = nxt_y = None
                        if ti + 1 < len(tiles2):
                            nxt_h, nxt_y = hins2.next(), ytl.next()
                            p2_load(nxt_h, nxt_y, tiles2[ti + 1][0], tiles2[ti + 1][1])
                        hook = None
                        if nxt_h is not None:
                            def hook(dc, nxt_h=nxt_h, nn=tiles2[ti + 1][1], ni=tiles2[ti + 1][2]):
                                if dc == 2:
                                    p2_norm(nxt_h, nn, ni, "sq")
                                if dc == 4:
                                    p2_norm(nxt_h, nn, ni, "stat")
                        p2_gates(cur_y, t0, n, hook)
                        if nxt_h is not None:
                            p2_norm(nxt_h, tiles2[ti + 1][1], tiles2[ti + 1][2], "apply")
                        p2_out(cur_h, t0, n, isc)
                        cur_h, cur_y = nxt_h, nxt_y
                    P.barrier()

        P.barrier()
        for l in range(DEPTH):
            last = (l == DEPTH - 1)
            if not skipffn:
                ffn_phase(l, 0, from_input=(l == 0), with_ctx=True)
            if stop <= 1:
                break
            if dbg and l == 0:
                P.dma("sp", lambda e: e.dma_start(out=hdump0, in_=hbuf), writes=[Buf()])
                P.barrier()
            mixer(l, False, last)
            if stop <= 2:
                break
            ffn_phase(l, 1, from_input=False, with_ctx=not last, final_norm=last)
        P.barrier()
        if dbg:
            P.dma("sp", lambda e: e.dma_start(out=hdump, in_=hbuf), writes=[Buf()])
            P.dma("sp", lambda e: e.dma_start(out=yfdump, in_=yfs), writes=[Buf()])
            P.barrier()
        P.emit(nc)
    return nc


def _fm(a):
    a = np.asarray(a)
    lead = a.shape[:-1]
    nk = a.shape[-1] // 128
    a = a.reshape(lead + (nk, 128))
    perm = (len(lead) + 1, len(lead)) + tuple(range(len(lead)))
    return np.ascontiguousarray(a.transpose(perm))


def prep_shared(inp, S, C, DEPTH):
    f = np.float32
    sh = {}
    w_ada = np.asarray(inp["w_ada"], f)
    sh["w_ada_l"] = np.ascontiguousarray(w_ada.reshape(DEPTH, KC, 128, 18, 512).transpose(0, 3, 2, 1, 4))
    sh["b_ada_l"] = np.ascontiguousarray(np.asarray(inp["b_ada"], f).reshape(DEPTH, 72, 128).transpose(2, 0, 1))
    g = np.concatenate([np.stack([inp["g_ffn1"][l], inp["g_mix"][l], inp["g_ffn2"][l]]) for l in range(DEPTH)]
                       + [np.asarray(inp["g_final"])[None]], 0).astype(f)
    sh["gvec"] = np.ascontiguousarray(g.reshape(-1, KC, 128).transpose(2, 0, 1))
    wg = np.asarray(inp["ffn_w_gate"], f).reshape(DEPTH, 2, KC, 128, FC, 128)
    wu = np.asarray(inp["ffn_w_up"], f).reshape(DEPTH, 2, KC, 128, FC, 128)
    gu = np.stack([wg, wu], 2)
    sh["wgu_l"] = np.ascontiguousarray(gu.transpose(0, 1, 5, 4, 2, 3, 6))
    del gu, wg, wu
    wd = np.asarray(inp["ffn_w_down"], f).reshape(DEPTH, 2, FC, 128, KC, 128)
    sh["wd_l"] = np.ascontiguousarray(wd.transpose(0, 1, 4, 3, 2, 5))
    win = np.asarray(inp["w_in"], f).reshape(DEPTH, KC, 128, 40, 128)
    sh["win_l"] = np.ascontiguousarray(win.transpose(0, 3, 2, 1, 4))
    wp = np.concatenate([inp["wp_fourier"], inp["wp_lru"], inp["wp_conv"]], 1).astype(f)
    sh["wp_l"] = np.ascontiguousarray(wp.reshape(DEPTH, 8, 128, KC, 128).transpose(0, 3, 2, 1, 4))
    wo = np.asarray(inp["w_out"], f).reshape(DEPTH, KC, 128, KC, 128)
    sh["wout_l"] = np.ascontiguousarray(wo.transpose(0, 3, 2, 1, 4))
    bd = np.zeros((DEPTH, 128, 2, 4, 2, 128), f)
    for ax, key in enumerate(("lru_wa", "lru_wx")):
        w = np.asarray(inp[key], f)
        for cc in range(4):
            for hl in range(2):
                bd[:, hl * 64:(hl + 1) * 64, :, cc, ax, hl * 64:(hl + 1) * 64] = \
                    w[:, :, cc * 2 + hl].transpose(0, 2, 1, 3)
    sh["lrug_l"] = bd
    lc = np.concatenate([np.asarray(inp["lru_conv_w"], f), np.asarray(inp["lru_conv_b"], f)[:, None]], 1)
    sh["lconv_l"] = np.ascontiguousarray(lc.reshape(DEPTH, 5, 4, 128).transpose(3, 0, 2, 1))
    lg = np.stack([np.asarray(inp["lru_ba"], f).reshape(DEPTH, 2, 512),
                   np.asarray(inp["lru_bx"], f).reshape(DEPTH, 2, 512),
                   np.asarray(inp["lru_lambda"], f)], -1)
    sh["lgate_l"] = np.ascontiguousarray(lg.reshape(DEPTH, 2, 4, 128, 3).transpose(3, 0, 1, 2, 4))
    sc = np.asarray(inp["sc_conv_w"], f)
    sh["sconv_l"] = np.ascontiguousarray(sc.reshape(DEPTH, 3, 2, 128).transpose(3, 0, 2, 1))
    i64 = np.arange(64)
    ang = 2 * np.pi * np.outer(i64, i64) / 64
    cc_, sc_ = np.cos(ang) / 8.0, -np.sin(ang) / 8.0
    ccbd = np.zeros((128, 2, 128), f)
    for h in range(2):
        ccbd[h * 64:(h + 1) * 64, 0, h * 64:(h + 1) * 64] = cc_
        ccbd[h * 64:(h + 1) * 64, 1, h * 64:(h + 1) * 64] = sc_
    sh["ccbd_l"] = ccbd
    tc_ = np.arange(C)
    angc = 2 * np.pi * (np.outer(tc_, tc_) % C) / C
    cs = np.stack([np.cos(angc), np.sin(angc)], 0) / math.sqrt(C)
    sh["cctx_l"] = np.ascontiguousarray(cs.reshape(2, C // 128, 128, C).transpose(2, 0, 1, 3)).astype(f)
    ts_ = np.arange(S, dtype=np.int64)
    NT = S // 512
    NTH = NT // 2
    NBL = S // 128
    GS = min(8, NBL)
    NG = NBL // GS
    dft = np.empty((NTH, NG, 128, 2, GS, 512), ml_dtypes.bfloat16)
    scl = 1.0 / math.sqrt(S)
    for tk in range(NTH):
        prod = np.outer(ts_, ts_[1 + tk * 512:1 + (tk + 1) * 512]) % S
        a = 2 * np.pi * prod / S
        for cs_i, m in enumerate((np.cos(a) * scl, np.sin(a) * scl)):
            m = m.reshape(NG, GS, 128, 512).transpose(0, 2, 1, 3)
            dft[tk, :, :, cs_i] = m.astype(ml_dtypes.bfloat16)
    sh["dft_l"] = dft
    return sh


def prep_core(inp, b):
    f = np.float32
    d = {}
    d["x_l"] = _fm(np.asarray(inp["x"][b], f))
    d["ctx_l"] = _fm(np.asarray(inp["ctx"][b], f))
    cv = np.stack([np.asarray(inp["c"][b], f), np.asarray(inp["c_ctx"], f)], 0)
    d["cvec"] = _fm(cv)
    return d


_NC_CACHE = {}


def run(inp, S, C, DEPTH, n_cores):
    key = (S, C, DEPTH)
    if key not in _NC_CACHE:
        _NC_CACHE[key] = build(S, C, DEPTH)
    nc = _NC_CACHE[key]
    sh = prep_shared(inp, S, C, DEPTH)
    in_maps = []
    for b in range(n_cores):
        m = dict(sh)
        m.update(prep_core(inp, b))
        in_maps.append(m)
    res = run_bass_kernel_spmd(nc, in_maps, core_ids=list(range(n_cores)))
    out = np.empty((n_cores, S, D), np.float32)
    for b in range(n_cores):
        yT = np.asarray(res.results[b]["yT"])
        out[b] = yT.transpose(2, 1, 0).reshape(S, D)
    return out


def kernel(**inputs):
    return run(inputs, 4096, 256, 4, N_CORES)
```

```python
import math
from contextlib import ExitStack

import numpy as np
import ml_dtypes

import concourse.bass as bass
import concourse.mybir as mybir
from concourse.bass_utils import run_bass_kernel_spmd

F32 = mybir.dt.float32
BF16 = mybir.dt.bfloat16
AF = mybir.ActivationFunctionType
ALU = mybir.AluOpType

D = 1024
KC = 8
DFF = 2816
FC = 22
NMOD = 9
GRID_W = 64
RMS_EPS = 1e-6
IN_COLS = 5120
N_CORES = 8


class Buf:
    __slots__ = ("w", "r")

    def __init__(self):
        self.w = None
        self.r = []


class Prog:
    ENGS = ("pe", "act", "dve", "pool", "sp")

    def __init__(self, esems, dsems):
        self.streams = {e: [] for e in self.ENGS}
        self.cnt = {e: 0 for e in self.ENGS}
        self.sat = {e: {} for e in self.ENGS}
        self.esem = esems
        self.dsl = {e: [[s, 0] for s in ss] for e, ss in dsems.items()}
        self.dqi = {e: 0 for e in dsems}
        self.semid = {}
        self.all_events = {}

    def _key(self, sem):
        return id(sem)

    def _prune(self, eng, evs):
        best = {}
        for ev in evs:
            if ev is None:
                continue
            sem, val = ev
            k = self._key(sem)
            if eng == "pe" and sem is self.esem["pe"]:
                continue
            if k not in best or best[k][1] < val:
                best[k] = (sem, val)
        waits = []
        for k, (sem, val) in best.items():
            if self.sat[eng].get(k, 0) < val:
                self.sat[eng][k] = val
                waits.append((sem, val))
        return waits

    def _deps(self, reads, writes):
        evs = []
        for b in reads:
            if b.w is not None:
                evs.append(b.w)
        for b in writes:
            if b.w is not None:
                evs.append(b.w)
            evs.extend(b.r)
        return evs

    def _commit(self, ev, reads, writes):
        for b in reads:
            b.r.append(ev)
        for b in writes:
            b.w = ev
            b.r = []
        self.all_events[self._key(ev[0])] = ev

    def op(self, eng, fn, reads=(), writes=(), extra=()):
        evs = self._deps(reads, writes) + list(extra)
        waits = self._prune(eng, evs)
        self.cnt[eng] += 1
        ev = (self.esem[eng], self.cnt[eng])
        self.streams[eng].append((fn, waits, ev, 1))
        self._commit(ev, reads, writes)
        return ev

    def dma(self, eng, fn, reads=(), writes=(), extra=()):
        slots = self.dsl[eng]
        slot = slots[self.dqi[eng] % len(slots)]
        self.dqi[eng] += 1
        evs = self._deps(reads, writes) + list(extra)
        if slot[1] > 0:
            evs.append((slot[0], slot[1]))
        waits = self._prune(eng, evs)
        slot[1] += 16
        ev = (slot[0], slot[1])
        self.streams[eng].append((fn, waits, ev, 16))
        self._commit(ev, reads, writes)
        return ev

    def last_ev(self, eng):
        return (self.esem[eng], self.cnt[eng])

    def barrier(self):
        evs = list(self.all_events.values())
        for e in self.ENGS:
            waits = self._prune(e, evs)
            if waits:
                self.streams[e].append((None, waits, None, 0))

    def emit(self, nc):
        engmap = {}
        with nc.Block() as block:
            def run(eng_name):
                def body(e):
                    for fn, waits, ev, inc in self.streams[eng_name]:
                        for sem, val in waits:
                            e.wait_ge(sem, val)
                        if fn is None:
                            continue
                        ins = fn(e)
                        ins.then_inc(ev[0], inc)
                return body
            block.tensor(run("pe"))
            block.scalar(run("act"))
            block.vector(run("dve"))
            block.gpsimd(run("pool"))
            block.sync(run("sp"))


class TB:
    def __init__(self, t, nb=1):
        self.t = t
        self.b = [Buf() for _ in range(nb)]


class Ring:
    def __init__(self, items):
        self.items = items
        self.i = 0

    def next(self):
        it = self.items[self.i % len(self.items)]
        self.i += 1
        return it


def build(S, C, DEPTH, ffn_tb=1024, mix_tb=512, stop=99, dbg=False, skipffn=False):
    T = S + C
    NBT = T // 128
    NBL = S // 128
    NBC = C // 128
    NT = S // 512
    GS = min(8, NBL)
    NG = NBL // GS
    NV = DEPTH * 3 + 1

    nc = bass.Bass("TRN2", target_bir_lowering=False)

    def din(name, shape, dt=F32):
        return nc.dram_tensor(name, list(shape), dt, kind="ExternalInput").ap()

    x_l = din("x_l", [128, KC, S])
    ctx_l = din("ctx_l", [128, KC, C])
    cvec = din("cvec", [128, KC, 2])
    w_ada_l = din("w_ada_l", [DEPTH, 18, 128, KC, 512])
    b_ada_l = din("b_ada_l", [128, DEPTH, 72])
    gvec = din("gvec", [128, NV, KC])
    wgu_l = din("wgu_l", [DEPTH, 2, FC, 128, 2, KC, 128])
    wd_l = din("wd_l", [DEPTH, 2, KC, 128, FC, 128])
    win_l = din("win_l", [DEPTH, 40, 128, KC, 128])
    wp_l = din("wp_l", [DEPTH, KC, 128, 8, 128])
    wout_l = din("wout_l", [DEPTH, KC, 128, KC, 128])
    lrug_l = din("lrug_l", [DEPTH, 128, 2, 4, 2, 128])
    lconv_l = din("lconv_l", [128, DEPTH, 4, 5])
    lgate_l = din("lgate_l", [128, DEPTH, 2, 4, 3])
    sconv_l = din("sconv_l", [128, DEPTH, 2, 3])
    ccbd_l = din("ccbd_l", [128, 2, 128])
    cctx_l = din("cctx_l", [128, 2, NBC, C])
    NTH = NT // 2
    dft_l = din("dft_l", [NTH, NG, 128, 2, GS, 512], BF16)
    yT = nc.dram_tensor("yT", [128, KC, S], F32, kind="ExternalOutput").ap()
    hbuf = nc.dram_tensor("hbuf_i", [128, KC, T], F32).ap()
    yfs = nc.dram_tensor("yfs_i", [128, 2, T], BF16).ap()
    gls = nc.dram_tensor("gls_i", [128, 4, T], BF16).ap()
    ycs = nc.dram_tensor("ycs_i", [128, 2, T], BF16).ap()
    zscr = nc.dram_tensor("zscr_i", [T // 128, 128, 512], BF16).ap()
    if dbg:
        hdump = nc.dram_tensor("hbuf", [128, KC, T], F32, kind="ExternalOutput").ap()
        yfdump = nc.dram_tensor("yfs", [128, 2, T], BF16, kind="ExternalOutput").ap()
        moddbg = nc.dram_tensor("moddbg", [128, DEPTH, 72, 2], F32, kind="ExternalOutput").ap()
        dbgGL = nc.dram_tensor("dbgGL", [128, 4, T], BF16, kind="ExternalOutput").ap()
        dbgYC = nc.dram_tensor("dbgYC", [128, 2, T], BF16, kind="ExternalOutput").ap()
        dbgLX = nc.dram_tensor("dbgLX", [128, 4, T], BF16, kind="ExternalOutput").ap()
        dbgU = nc.dram_tensor("dbgU", [128, KC, 512], BF16, kind="ExternalOutput").ap()
        dbgR = nc.dram_tensor("dbgR", [128, 512], F32, kind="ExternalOutput").ap()
        dbgSQ = nc.dram_tensor("dbgSQ", [128, KC, 512], BF16, kind="ExternalOutput").ap()
        dbgH = nc.dram_tensor("dbgH", [128, KC, 512], F32, kind="ExternalOutput").ap()
        dbgH0 = nc.dram_tensor("dbgH0", [128, KC, 512], F32, kind="ExternalOutput").ap()
        hdump0 = nc.dram_tensor("hdump0", [128, KC, T], F32, kind="ExternalOutput").ap()

    _uid = [0]

    def sbt(name, shape, dt):
        _uid[0] += 1
        return nc.sbuf_tensor(f"{name}_{_uid[0]}", shape, dt)

    es = ExitStack()
    with es:
        def sb(name, shape, dt):
            return es.enter_context(sbt(name, list(shape), dt))

        esems = {e: es.enter_context(nc.semaphore("es_" + e)) for e in Prog.ENGS}
        dsems = {
            "pool": [es.enter_context(nc.semaphore(f"dq_pool{i}")) for i in range(8)],
            "sp": [es.enter_context(nc.semaphore(f"dq_sp{i}")) for i in range(8)],
        }
        P = Prog(esems, dsems)

        psb = [TB(es.enter_context(nc.psum_tensor(f"ps{i}", [128, 512], F32))) for i in range(8)]
        psring = Ring(psb[:7])

        ones_bf = TB(sb("ones_bf", [128, 128], BF16))
        s2 = TB(sb("s2", [128, KC, 2], F32))
        s2b = TB(sb("s2b", [128, KC, 2], BF16))
        modT = TB(sb("modT", [128, DEPTH, 72, 2], F32))
        Aall = TB(sb("Aall", [128, NV, KC, 2], F32))
        gv = TB(sb("gv", [128, NV, KC], F32))
        bada = TB(sb("bada", [128, DEPTH, 72], F32))
        Gall = TB(sb("Gall", [128, DEPTH * 3, KC, 2], F32))
        lconv = TB(sb("lconv", [128, DEPTH, 4, 5], F32))
        lgate = TB(sb("lgate", [128, DEPTH, 2, 4, 3], F32))
        lc = TB(sb("lc", [128, DEPTH, 2, 4, 2], F32))
        lgh = TB(sb("lgh", [128, DEPTH, 2, 4, 2], F32))
        sconv = TB(sb("sconv", [128, DEPTH, 2, 3], F32))
        ccbd = TB(sb("ccbd", [128, 2, 128], BF16))
        cctx = TB(sb("cctx", [128, 2, NBC, C], BF16))
        lrug = TB(sb("lrug", [128, 2, 4, 2, 128], BF16))
        epsb = TB(sb("epsb", [128, 1], F32))
        oneb = TB(sb("oneb", [128, 1], F32))
        qtrb = TB(sb("qtrb", [128, 1], F32))
        isqb = TB(sb("isqb", [128, 2], BF16))

        hblk = [Buf() for _ in range(T // 128)]
        glsb = [Buf() for _ in range(4)]
        ycsb = Buf()
        yfblk = [Buf() for _ in range(T // 128)]

        def blocks(bl, t0, n):
            return bl[t0 // 128:(t0 + n) // 128]

        def make_tiles(tb, with_ctx=True):
            res = []
            if with_ctx:
                t = S
                while t < T:
                    n = min(tb, T - t)
                    res.append((t, n, True))
                    t += n
            t = 0
            while t < S:
                n = min(tb, S - t)
                res.append((t, n, False))
                t += n
            return res

        def subs_of(n):
            return [(o, min(512, n - o)) for o in range(0, n, 512)]

        P.op("dve", lambda e: e.memset(ones_bf.t[:], 1.0), writes=ones_bf.b)
        P.op("dve", lambda e: e.memset(epsb.t[:], float(D * RMS_EPS)), writes=epsb.b)
        P.op("dve", lambda e: e.memset(oneb.t[:], 1.0), writes=oneb.b)
        P.op("dve", lambda e: e.memset(qtrb.t[:], 0.25), writes=qtrb.b)
        P.op("dve", lambda e: e.memset(isqb.t[:], 1.0 / math.sqrt(S)), writes=isqb.b)
        cv = TB(sb("cv", [128, KC, 2], F32))
        P.dma("sp", lambda e: e.dma_start(out=cv.t[:], in_=cvec), writes=cv.b)
        P.dma("sp", lambda e: e.dma_start(out=gv.t[:], in_=gvec), writes=gv.b)
        P.dma("sp", lambda e: e.dma_start(out=bada.t[:], in_=b_ada_l), writes=bada.b)
        P.dma("sp", lambda e: e.dma_start(out=lconv.t[:], in_=lconv_l), writes=lconv.b)
        P.dma("sp", lambda e: e.dma_start(out=lgate.t[:], in_=lgate_l), writes=lgate.b)
        P.dma("sp", lambda e: e.dma_start(out=sconv.t[:], in_=sconv_l), writes=sconv.b)
        P.dma("pool", lambda e: e.dma_start(out=ccbd.t[:], in_=ccbd_l), writes=ccbd.b)
        P.dma("pool", lambda e: e.dma_start(out=cctx.t[:], in_=cctx_l), writes=cctx.b)
        P.op("act", lambda e: e.activation(out=s2.t[:], in_=cv.t[:], func=AF.Silu),
             reads=cv.b, writes=s2.b)
        P.op("act", lambda e: e.activation(out=s2b.t[:], in_=cv.t[:], func=AF.Silu),
             reads=cv.b, writes=s2b.b)
        lt = TB(sb("lt", [128, DEPTH, 2, 4], F32))
        P.op("act", lambda e: e.activation(out=lt.t[:], in_=lgate.t[:, :, :, :, 2], func=AF.Exp, scale=-1.0),
             reads=lgate.b, writes=lt.b)
        P.op("act", lambda e: e.activation(out=lt.t[:], in_=lt.t[:], func=AF.Ln, bias=1.0),
             reads=lt.b, writes=lt.b)
        P.op("dve", lambda e: e.tensor_scalar(lc.t[:, :, :, :, 0], lt.t[:], -4.0, None, op0=ALU.mult),
             reads=lt.b, writes=lc.b)
        P.op("dve", lambda e: e.tensor_scalar(lc.t[:, :, :, :, 1], lt.t[:], -8.0, None, op0=ALU.mult),
             reads=lt.b + lc.b, writes=lc.b)
        P.op("dve", lambda e: e.tensor_scalar(lgh.t[:], lgate.t[:, :, :, :, 0:2], 0.5, None, op0=ALU.mult),
             reads=lgate.b, writes=lgh.b)

        parb = [Buf() for _ in range(DEPTH + 1)]
        pm_bank = psb[7]
        pmv_all = pm_bank.t[:, 0:144].rearrange("p (j t) -> p j t", t=2)
        NAP = 36

        def adaln_steps(l, wa_ring):
            for piece in range(NAP):
                w32, wa = wa_ring.next()
                P.dma("sp", lambda e, w32=w32, piece=piece: e.dma_start(
                    out=w32.t[:], in_=w_ada_l[l, piece // 2][:, :, (piece % 2) * 256:(piece % 2 + 1) * 256]),
                    writes=w32.b)
                P.op("act", lambda e, w32=w32, wa=wa: e.activation(out=wa.t[:], in_=w32.t[:], func=AF.Copy),
                     reads=w32.b, writes=wa.b)

                def fn(e, wa=wa, piece=piece):
                    ins = None
                    for j in range(2):
                        for kc in range(KC):
                            ins = e.matmul(pmv_all[:, piece * 2 + j, :], wa.t[:, kc, j * 128:(j + 1) * 128],
                                           s2b.t[:, kc, :], start=(kc == 0), stop=(kc == KC - 1))
                    return ins
                P.op("pe", fn, reads=wa.b + s2b.b, writes=pm_bank.b if piece == 0 else ())
                pm_bank.b[0].w = (P.esem["pe"], P.cnt["pe"])
                yield
            for t in range(2):
                P.op("dve", lambda e, t=t: e.tensor_tensor(
                    out=modT.t[:, l, :, t], in0=pmv_all[:, :, t], in1=bada.t[:, l, :], op=ALU.add),
                    reads=pm_bank.b + bada.b, writes=[parb[l]])
            for w in range(3):
                for t in range(2):
                    P.op("dve", lambda e, w=w, t=t: e.scalar_tensor_tensor(
                        out=Aall.t[:, l * 3 + w, :, t], in0=modT.t[:, l, (3 * w + 1) * 8:(3 * w + 2) * 8, t],
                        scalar=1.0, in1=gv.t[:, l * 3 + w, :], op0=ALU.add, op1=ALU.mult),
                        reads=gv.b + [parb[l]], writes=[parb[l]])
                    P.op("dve", lambda e, w=w, t=t: e.tensor_scalar(
                        Gall.t[:, l * 3 + w, :, t], modT.t[:, l, (3 * w + 2) * 8:(3 * w + 3) * 8, t],
                        (1.0 if w == 1 else 0.5), None, op0=ALU.mult),
                        reads=[parb[l]], writes=[parb[l]])
            P.op("dve", lambda e: e.tensor_scalar(Aall.t[:, l * 3:l * 3 + 3], Aall.t[:, l * 3:l * 3 + 3], 32.0, None,
                                                  op0=ALU.mult), reads=[parb[l]], writes=[parb[l]])
            yield

        with ExitStack() as es2:
            wa_ring0 = Ring([(TB(es2.enter_context(sbt(f"wada32_{i}", [128, KC, 256], F32))),
                              TB(es2.enter_context(sbt(f"wada{i}", [128, KC, 256], BF16)))) for i in range(2)])
            for _ in adaln_steps(0, wa_ring0):
                pass
            P.barrier()
        P.op("dve", lambda e: e.tensor_scalar(Aall.t[:, NV - 1, :, 0], gv.t[:, NV - 1, :], 32.0, None, op0=ALU.mult),
             reads=gv.b, writes=[parb[DEPTH]])

        if dbg:
            P.dma("sp", lambda e: e.dma_start(out=moddbg, in_=modT.t[:]), reads=[parb[0]], writes=[Buf()])

        def A_ap(l, w, kc, t):
            return Aall.t[:, l * 3 + w, kc, t:t + 1]

        def B_ap(l, w, kc, t):
            return modT.t[:, l, (3 * w) * 8 + kc, t:t + 1]

        def G_ap(l, w, kc, t):
            return Gall.t[:, l * 3 + w, kc, t:t + 1]

        def norm_mod(hin, n, xn, sqb, rstd, tmps, a_fn, b_fn, pb, stage="all"):
            for si, (o, m) in enumerate(subs_of(n)):
                sq = sqb[si] if isinstance(sqb, list) else sqb
                rs = rstd[si] if isinstance(rstd, list) else rstd
                if stage in ("all", "sq"):
                    P.op("act", lambda e, o=o, m=m, sq=sq: e.activation(
                        out=sq.t[:, :, 0:m], in_=hin.t[:, :, o:o + m], func=AF.Square),
                        reads=[hin.b[si]], writes=sq.b)
                if stage in ("all", "stat"):
                    pn = psring.next()

                    def fn(e, pn=pn, m=m, sq=sq):
                        ins = None
                        for kc in range(KC):
                            ins = e.matmul(pn.t[:, 0:m], ones_bf.t[:], sq.t[:, kc, 0:m],
                                           start=(kc == 0), stop=(kc == KC - 1))
                        return ins
                    P.op("pe", fn, reads=sq.b + ones_bf.b, writes=pn.b)
                    P.op("act", lambda e, pn=pn, m=m, rs=rs: e.activation(
                        out=rs.t[:, 0:m], in_=pn.t[:, 0:m], func=AF.Sqrt, bias=epsb.t[:, 0:1]),
                        reads=pn.b + epsb.b, writes=rs.b)
                    P.op("dve", lambda e, m=m, rs=rs: e.reciprocal(out=rs.t[:, 0:m], in_=rs.t[:, 0:m]),
                         reads=rs.b, writes=rs.b)
                if stage in ("all", "apply"):
                    for kc in range(KC):
                        tm = tmps.next()
                        P.op("dve", lambda e, tm=tm, kc=kc, o=o, m=m, rs=rs: e.tensor_tensor(
                            out=tm.t[:, 0:m], in0=hin.t[:, kc, o:o + m], in1=rs.t[:, 0:m], op=ALU.mult),
                            reads=[hin.b[si]] + rs.b, writes=tm.b)
                        bb = b_fn(kc)
                        aa = a_fn(kc)
                        P.op("act", lambda e, tm=tm, kc=kc, o=o, m=m, bb=bb, aa=aa, si=si: e.activation(
                            out=xn.t[:, kc, o:o + m], in_=tm.t[:, 0:m], func=AF.Identity,
                            bias=bb, scale=aa),
                            reads=tm.b + pb, writes=[xn.b[si]])

        def load_tile(hin, t0, n, from_input):
            if from_input:
                src = ctx_l[:, :, t0 - S:t0 - S + n] if t0 >= S else x_l[:, :, t0:t0 + n]
                rd = []
            else:
                src = hbuf[:, :, t0:t0 + n]
                rd = blocks(hblk, t0, n)
            P.dma("sp", lambda e: e.dma_start(out=hin.t[:, :, 0:n], in_=src), reads=rd, writes=hin.b)

        def store_tile(hin, t0, n):
            P.dma("sp", lambda e: e.dma_start(out=hbuf[:, :, t0:t0 + n], in_=hin.t[:, :, 0:n]),
                  reads=hin.b, writes=blocks(hblk, t0, n))

        def ffn_phase(l, j, from_input, with_ctx, final_norm=False):
            w = 0 if j == 0 else 2
            tiles = make_tiles(ffn_tb, with_ctx)
            NS = ffn_tb // 512
            with ExitStack() as e2:
                def sb2(name, shape, dt):
                    return e2.enter_context(sbt(name, list(shape), dt))
                hins = Ring([TB(sb2(f"f_hin{i}", [128, KC, ffn_tb], F32), NS) for i in range(2)])
                xn = TB(sb2("f_xn", [128, KC, ffn_tb], BF16), NS)
                At = [[None] * NS for _ in range(FC)]
                Atile = sb2("f_A", [128, FC, ffn_tb], BF16)
                Ab = [[Buf() for _ in range(NS)] for _ in range(FC)]
                sqb = [TB(sb2(f"f_sqb{i}", [128, KC, 512], BF16)) for i in range(NS)]
                rstd = [TB(sb2(f"f_rstd{i}", [128, 512], F32)) for i in range(NS)]
                tmps = Ring([TB(sb2(f"f_tmp{i}", [128, 512], F32)) for i in range(2)])
                sgs = Ring([TB(sb2(f"f_sg{i}", [128, 512], F32)) for i in range(2)])
                wgus = Ring([TB(sb2(f"f_wgu{i}", [128, 2, KC, 128], BF16)) for i in range(4)])
                wds = Ring([TB(sb2(f"f_wd{i}", [128, FC, 128], BF16)) for i in range(3)])
                if final_norm:
                    outs = Ring([TB(sb2(f"f_out{i}", [128, 512], F32)) for i in range(2)])

                def do_norm(hin, n, tc, stage="all"):
                    norm_mod(hin, n, xn, sqb, rstd, tmps,
                             lambda kc: A_ap(l, w, kc, tc), lambda kc: B_ap(l, w, kc, tc), [parb[l]], stage=stage)

                cur = hins.next()
                load_tile(cur, tiles[0][0], tiles[0][1], from_input)
                do_norm(cur, tiles[0][1], 1 if tiles[0][2] else 0)
                for ti, (t0, n, isc) in enumerate(tiles):
                    tc = 1 if isc else 0
                    subs = subs_of(n)
                    nxt = None
                    if ti + 1 < len(tiles):
                        nxt = hins.next()
                        load_tile(nxt, tiles[ti + 1][0], tiles[ti + 1][1], from_input)
                    for fc in range(FC):
                        if nxt is not None and fc == 6:
                            do_norm(nxt, tiles[ti + 1][1], 1 if tiles[ti + 1][2] else 0, "sq")
                        if nxt is not None and fc == 12:
                            do_norm(nxt, tiles[ti + 1][1], 1 if tiles[ti + 1][2] else 0, "stat")
                        wgu = wgus.next()
                        P.dma("pool", lambda e, wgu=wgu, fc=fc: e.dma_start(out=wgu.t[:], in_=wgu_l[l, j, fc]),
                              writes=wgu.b)
                        for si, (o, m) in enumerate(subs):
                            pg = psring.next()
                            pu = psring.next()

                            def fn(e, wgu=wgu, pg=pg, pu=pu, o=o, m=m):
                                ins = None
                                for gu, pp in ((0, pg), (1, pu)):
                                    for kc in range(KC):
                                        ins = e.matmul(pp.t[:, 0:m], wgu.t[:, gu, kc, :], xn.t[:, kc, o:o + m],
                                                       start=(kc == 0), stop=(kc == KC - 1))
                                return ins
                            P.op("pe", fn, reads=wgu.b + [xn.b[si]], writes=pg.b + pu.b)
                            sg = sgs.next()
                            P.op("act", lambda e, sg=sg, pg=pg, m=m: e.activation(
                                out=sg.t[:, 0:m], in_=pg.t[:, 0:m], func=AF.Silu),
                                reads=pg.b, writes=sg.b)
                            P.op("dve", lambda e, sg=sg, pu=pu, fc=fc, o=o, m=m: e.tensor_tensor(
                                out=Atile[:, fc, o:o + m], in0=sg.t[:, 0:m], in1=pu.t[:, 0:m], op=ALU.mult),
                                reads=sg.b + pu.b, writes=[Ab[fc][si]])
                    if nxt is not None:
                        do_norm(nxt, tiles[ti + 1][1], 1 if tiles[ti + 1][2] else 0, "apply")
                    for dc in range(KC):
                        wd = wds.next()
                        P.dma("pool", lambda e, wd=wd, dc=dc: e.dma_start(out=wd.t[:], in_=wd_l[l, j, dc]),
                              writes=wd.b)
                        for si, (o, m) in enumerate(subs):
                            po = psring.next()

                            def fn(e, wd=wd, po=po, o=o, m=m):
                                ins = None
                                for fc in range(FC):
                                    ins = e.matmul(po.t[:, 0:m], wd.t[:, fc, :], Atile[:, fc, o:o + m],
                                                   start=(fc == 0), stop=(fc == FC - 1))
                                return ins
                            P.op("pe", fn, reads=wd.b + [Ab[fc][si] for fc in range(FC)], writes=po.b)
                            P.op("dve", lambda e, po=po, dc=dc, o=o, m=m, cur=cur, tc=tc: e.scalar_tensor_tensor(
                                out=cur.t[:, dc, o:o + m], in0=po.t[:, 0:m], scalar=G_ap(l, w, dc, tc),
                                in1=cur.t[:, dc, o:o + m], op0=ALU.mult, op1=ALU.add),
                                reads=po.b + [parb[l]] + [cur.b[si]], writes=[cur.b[si]])
                    if not final_norm:
                        store_tile(cur, t0, n)
                    else:
                        for si, (o, m) in enumerate(subs):
                            sq, rs = sqb[si], rstd[si]
                            P.op("act", lambda e, o=o, m=m, cur=cur, sq=sq: e.activation(
                                out=sq.t[:, :, 0:m], in_=cur.t[:, :, o:o + m], func=AF.Square),
                                reads=[cur.b[si]], writes=sq.b)
                            pn = psring.next()

                            def fn(e, pn=pn, m=m, sq=sq):
                                ins = None
                                for kc in range(KC):
                                    ins = e.matmul(pn.t[:, 0:m], ones_bf.t[:], sq.t[:, kc, 0:m],
                                                   start=(kc == 0), stop=(kc == KC - 1))
                                return ins
                            P.op("pe", fn, reads=sq.b + ones_bf.b, writes=pn.b)
                            P.op("act", lambda e, pn=pn, m=m, rs=rs: e.activation(
                                out=rs.t[:, 0:m], in_=pn.t[:, 0:m], func=AF.Sqrt, bias=epsb.t[:, 0:1]),
                                reads=pn.b + epsb.b, writes=rs.b)
                            P.op("dve", lambda e, m=m, rs=rs: e.reciprocal(out=rs.t[:, 0:m], in_=rs.t[:, 0:m]),
                                 reads=rs.b, writes=rs.b)
                            for kc in range(KC):
                                ot = outs.next()
                                P.op("dve", lambda e, ot=ot, kc=kc, o=o, m=m, cur=cur, rs=rs: e.scalar_tensor_tensor(
                                    out=ot.t[:, 0:m], in0=cur.t[:, kc, o:o + m], scalar=Aall.t[:, NV - 1, kc, 0:1],
                                    in1=rs.t[:, 0:m], op0=ALU.mult, op1=ALU.mult),
                                    reads=[cur.b[si]] + rs.b + [parb[DEPTH]], writes=ot.b)
                                P.dma("sp", lambda e, ot=ot, kc=kc, o=o, m=m, t0=t0: e.dma_start(
                                    out=yT[:, kc, t0 + o:t0 + o + m], in_=ot.t[:, 0:m]),
                                    reads=ot.b, writes=[Buf()])
                    cur = nxt
                P.barrier()

        def mixer(l, from_input_unused, last):
            tb = mix_tb
            tiles1 = make_tiles(tb, True)
            tiles2 = make_tiles(tb, not last)
            with ExitStack() as em:
                def sbm(name, shape, dt):
                    return em.enter_context(sbt(name, list(shape), dt))
                GL = sbm("m_GL", [128, 4, T], BF16)
                el = ExitStack()
                LX = el.enter_context(sbt("m_LX", [128, 4, T], BF16))
                GLb = [[Buf() for _ in range(NBT)] for _ in range(4)]
                LXb = [[Buf() for _ in range(NBT)] for _ in range(4)]
                zsb = [Buf() for _ in range(NBT)]
                P.dma("pool", lambda e: e.dma_start(out=lrug.t[:], in_=lrug_l[l]), writes=lrug.b)

                def dump_mix():
                    if dbg:
                        P.dma("sp", lambda e: e.dma_start(out=dbgGL, in_=GL[:]), writes=[Buf()])
                        P.dma("sp", lambda e: e.dma_start(out=dbgLX, in_=LX[:]), writes=[Buf()])
                        P.dma("sp", lambda e: e.dma_start(out=dbgYC, in_=ycs), reads=[ycsb], writes=[Buf()])
                        P.barrier()

                if True:
                    with ExitStack() as e1:
                        def sb1(name, shape, dt):
                            return e1.enter_context(sbt(name, list(shape), dt))
                        hins1 = Ring([TB(sb1(f"p1_hin{i}", [128, KC, tb], F32)) for i in range(2)])
                        us1 = Ring([TB(sb1(f"p1_u{i}", [128, KC, tb], BF16)) for i in range(2)])
                        sqb = TB(sb1("p1_sqb", [128, KC, 512], BF16))
                        rstd = TB(sb1("p1_rstd", [128, 512], F32))
                        tmps = Ring([TB(sb1(f"p1_tmp{i}", [128, 512], F32)) for i in range(2)])
                        xfb = TB(sb1("p1_xfb", [128, 2, 512], BF16), 2)
                        sBt = Ring([TB(sb1(f"p1_sB{i}", [128, 512], F32)) for i in range(2)])
                        sCt = Ring([TB(sb1(f"p1_sC{i}", [128, 512], F32)) for i in range(2)])
                        qt = TB(sb1("p1_q", [128, 512], F32))
                        yct = TB(sb1("p1_yc", [128, 512], F32))
                        yco = Ring([TB(sb1(f"p1_yco{i}", [128, 512], BF16)) for i in range(2)])
                        zst = Ring([TB(sb1(f"p1_zst{i}", [128, 512], BF16)) for i in range(3)])
                        wres = [TB(sb1(f"p1_w{i}", [128, KC, 128], BF16)) for i in range(16)]
                        for cch in range(16):
                            P.dma("pool", lambda e, cch=cch: e.dma_start(out=wres[cch].t[:], in_=win_l[l, cch]),
                                  writes=wres[cch].b)

                        def p1_norm(hin, u, n, isc):
                            tc = 1 if isc else 0
                            norm_mod(hin, n, u, sqb, rstd, tmps,
                                     lambda kc: A_ap(l, 1, kc, tc), lambda kc: B_ap(l, 1, kc, tc), [parb[l]])

                        def p1_proj(u, t0, n, isc, cchs, sBs, sCs):
                            b0 = t0 // 128
                            nb = n // 128
                            for cch in cchs:
                                wn = wres[cch]
                                pp = psring.next()

                                def fn(e, wn=wn, pp=pp):
                                    ins = None
                                    for kc in range(KC):
                                        ins = e.matmul(pp.t[:, 0:n], wn.t[:, kc, :], u.t[:, kc, 0:n],
                                                       start=(kc == 0), stop=(kc == KC - 1))
                                    return ins
                                P.op("pe", fn, reads=wn.b + u.b, writes=pp.b)
                                if cch < 2:
                                    P.op("act", lambda e, pp=pp, cch=cch: e.activation(
                                        out=xfb.t[:, cch, 0:n], in_=pp.t[:, 0:n], func=AF.Copy),
                                        reads=pp.b, writes=[xfb.b[cch]])
                                elif cch < 6:
                                    cc = cch - 2
                                    P.op("dve", lambda e, pp=pp, cc=cc: e.tensor_copy(
                                        out=LX[:, cc, t0:t0 + n], in_=pp.t[:, 0:n]),
                                        reads=pp.b, writes=LXb[cc][b0:b0 + nb])
                                elif cch < 10:
                                    cc = cch - 6
                                    P.op("act", lambda e, pp=pp, cc=cc: e.activation(
                                        out=GL[:, cc, t0:t0 + n], in_=pp.t[:, 0:n], func=AF.Gelu),
                                        reads=pp.b, writes=GLb[cc][b0:b0 + nb])
                                elif cch < 12:
                                    cc = cch - 10
                                    st = sBt.next()
                                    sBs[cc] = st
                                    P.op("act", lambda e, pp=pp, st=st: e.activation(
                                        out=st.t[:, 0:n], in_=pp.t[:, 0:n], func=AF.Copy),
                                        reads=pp.b, writes=st.b)
                                elif cch < 14:
                                    cc = cch - 12
                                    st = sCt.next()
                                    sCs[cc] = st
                                    P.op("act", lambda e, pp=pp, st=st: e.activation(
                                        out=st.t[:, 0:n], in_=pp.t[:, 0:n], func=AF.Copy),
                                        reads=pp.b, writes=st.b)
                                else:
                                    cc = cch - 14
                                    rw = n if isc else GRID_W
                                    sC = sCs[cc]
                                    sB = sBs[cc]
                                    P.op("dve", lambda e, pp=pp, sC=sC: e.tensor_tensor(
                                        out=qt.t[:, 0:n], in0=sC.t[:, 0:n], in1=pp.t[:, 0:n], op=ALU.mult),
                                        reads=pp.b + sC.b, writes=qt.b)
                                    P.op("act", lambda e, cc=cc: e.activation(
                                        out=yct.t[:, 0:n], in_=qt.t[:, 0:n], func=AF.Copy,
                                        scale=sconv.t[:, l, cc, 1:2]),
                                        reads=qt.b + sconv.b, writes=yct.b)
                                    q3 = qt.t[:, 0:n].rearrange("p (r w) -> p r w", w=rw)
                                    y3 = yct.t[:, 0:n].rearrange("p (r w) -> p r w", w=rw)
                                    P.op("dve", lambda e, cc=cc, q3=q3, y3=y3, rw=rw: e.scalar_tensor_tensor(
                                        out=y3[:, :, 1:rw], in0=q3[:, :, 0:rw - 1], scalar=sconv.t[:, l, cc, 0:1],
                                        in1=y3[:, :, 1:rw], op0=ALU.mult, op1=ALU.add),
                                        reads=qt.b + sconv.b + yct.b, writes=yct.b)
                                    P.op("dve", lambda e, cc=cc, q3=q3, y3=y3, rw=rw: e.scalar_tensor_tensor(
                                        out=y3[:, :, 0:rw - 1], in0=q3[:, :, 1:rw], scalar=sconv.t[:, l, cc, 2:3],
                                        in1=y3[:, :, 0:rw - 1], op0=ALU.mult, op1=ALU.add),
                                        reads=qt.b + sconv.b + yct.b, writes=yct.b)
                                    yo = yco.next()
                                    P.op("dve", lambda e, yo=yo, sB=sB: e.tensor_tensor(
                                        out=yo.t[:, 0:n], in0=yct.t[:, 0:n], in1=sB.t[:, 0:n], op=ALU.mult),
                                        reads=yct.b + sB.b, writes=yo.b)
                                    P.dma("sp", lambda e, yo=yo, cc=cc: e.dma_start(
                                        out=ycs[:, cc, t0:t0 + n], in_=yo.t[:, 0:n]), reads=yo.b, writes=[ycsb])

                        def p1_zdft(t0, n):
                            b0 = t0 // 128
                            for bi in range(n // 128):
                                pz = psring.next()

                                def fn(e, pz=pz, bi=bi):
                                    ins = None
                                    for cs in range(2):
                                        for chc in range(2):
                                            ins = e.matmul(pz.t[:, cs * 256 + chc * 128:cs * 256 + (chc + 1) * 128],
                                                           xfb.t[:, chc, bi * 128:(bi + 1) * 128], ccbd.t[:, cs, :],
                                                           start=True, stop=True)
                                    return ins
                                P.op("pe", fn, reads=xfb.b + ccbd.b, writes=pz.b)
                                zt = zst.next()
                                P.op("act", lambda e, pz=pz, zt=zt: e.activation(
                                    out=zt.t[:], in_=pz.t[:], func=AF.Copy), reads=pz.b, writes=zt.b)
                                P.dma("sp", lambda e, zt=zt, bi=bi: e.dma_start(out=zscr[b0 + bi], in_=zt.t[:]),
                                      reads=zt.b, writes=[zsb[b0 + bi]])

                        cur_h, cur_u = hins1.next(), us1.next()
                        load_tile(cur_h, tiles1[0][0], tiles1[0][1], False)
                        p1_norm(cur_h, cur_u, tiles1[0][1], tiles1[0][2])
                        for ti, (t0, n, isc) in enumerate(tiles1):
                            nxt_h = nxt_u = None
                            if ti + 1 < len(tiles1):
                                nxt_h, nxt_u = hins1.next(), us1.next()
                                load_tile(nxt_h, tiles1[ti + 1][0], tiles1[ti + 1][1], False)
                            sBs, sCs = [None, None], [None, None]
                            p1_proj(cur_u, t0, n, isc, range(0, 8), sBs, sCs)
                            if nxt_h is not None:
                                p1_norm(nxt_h, nxt_u, tiles1[ti + 1][1], tiles1[ti + 1][2])
                            p1_proj(cur_u, t0, n, isc, range(8, 16), sBs, sCs)
                            p1_zdft(t0, n)
                            cur_h, cur_u = nxt_h, nxt_u
                        P.barrier()

                    if stop <= 1.2:
                        dump_mix()
                        return
                    with ExitStack() as e1:
                        Zt = e1.enter_context(sbt("d_Z", [128, NBT, 512], BF16))
                        Zb = [Buf() for _ in range(NBT)]
                        zh = NBT // 2
                        for (za, zb_) in ((0, zh), (zh, NBT)):
                            P.dma("sp", lambda e, za=za, zb_=zb_: e.dma_start(
                                out=Zt[:, za:zb_, :], in_=zscr[za:zb_].rearrange("b p c -> p b c")),
                                reads=zsb[za:zb_], writes=Zb[za:zb_])
                        dfr = Ring([TB(e1.enter_context(sbt(f"d_cs{i}", [128, 2, GS, 512], BF16)))
                                    for i in range(2)])
                        yfo = Ring([TB(e1.enter_context(sbt(f"d_yf{i}", [128, 512], BF16)))
                                    for i in range(6)])
                        bsr = Ring([TB(e1.enter_context(sbt(f"d_bs{i}", [128, 512], F32))) for i in range(2)])
                        isq = 1.0 / math.sqrt(S)
                        p0 = psring.next()

                        def fn0(e, p0=p0):
                            ins = None
                            for chc in range(2):
                                for tin in range(NBL):
                                    ins = e.matmul(p0.t[:, chc:chc + 1], Zt[:, tin, chc * 128:(chc + 1) * 128],
                                                   isqb.t[:, 0:1], start=(tin == 0), stop=(tin == NBL - 1))
                            return ins
                        P.op("pe", fn0, reads=Zb[0:NBL] + isqb.b, writes=p0.b)
                        y0 = yfo.next()
                        P.op("act", lambda e, y0=y0, p0=p0: e.activation(out=y0.t[:, 0:2], in_=p0.t[:, 0:2], func=AF.Copy),
                             reads=p0.b, writes=y0.b)
                        for chc in range(2):
                            P.dma("sp", lambda e, y0=y0, chc=chc: e.dma_start(out=yfs[:, chc, 0:1], in_=y0.t[:, chc:chc + 1],
                                                                                 allow_slow_non_contiguous=True),
                                  reads=y0.b, writes=blocks(yfblk, 0, 128))
                        for tk in range(NTH):
                            pa = [psring.next(), psring.next()]
                            pb = [psring.next(), psring.next()]
                            for g in range(NG):
                                dd = dfr.next()
                                P.dma("sp", lambda e, dd=dd, tk=tk, g=g: e.dma_start(
                                    out=dd.t[:], in_=dft_l[tk, g]), writes=dd.b)
                                for chc in range(2):
                                    def fn(e, dd=dd, g=g, chc=chc, pac=pa[chc], pbc=pb[chc]):
                                        ins = None
                                        for cs, pp in ((0, pac), (1, pbc)):
                                            for blk in range(GS):
                                                tin = g * GS + blk
                                                ins = e.matmul(pp.t[:, :],
                                                               Zt[:, tin, cs * 256 + chc * 128:cs * 256 + (chc + 1) * 128],
                                                               dd.t[:, cs, blk, :],
                                                               start=(g == 0 and blk == 0),
                                                               stop=(g == NG - 1 and blk == GS - 1))
                                        return ins
                                    P.op("pe", fn, reads=dd.b + Zb[g * GS:(g + 1) * GS],
                                         writes=(pa[chc].b + pb[chc].b) if g == 0 else ())
                                    pa[chc].b[0].w = P.last_ev("pe")
                                    pb[chc].b[0].w = P.last_ev("pe")
                            for chc in range(2):
                                bs = bsr.next()
                                P.op("act", lambda e, bs=bs, pbc=pb[chc]: e.activation(
                                    out=bs.t[:], in_=pbc.t[:], func=AF.Copy), reads=pb[chc].b, writes=bs.b)
                                yd, ym = yfo.next(), yfo.next()
                                P.op("dve", lambda e, yd=yd, bs=bs, pac=pa[chc]: e.tensor_tensor(
                                    out=yd.t[:], in0=pac.t[:], in1=bs.t[:], op=ALU.add),
                                    reads=pa[chc].b + bs.b, writes=yd.b)
                                P.op("dve", lambda e, ym=ym, bs=bs, pac=pa[chc]: e.tensor_tensor(
                                    out=ym.t[:, ::-1], in0=pac.t[:], in1=bs.t[:], op=ALU.subtract),
                                    reads=pa[chc].b + bs.b, writes=ym.b)
                                d0 = 1 + 512 * tk
                                P.dma("sp", lambda e, yd=yd, chc=chc, d0=d0: e.dma_start(
                                    out=yfs[:, chc, d0:d0 + 512], in_=yd.t[:]),
                                    reads=yd.b, writes=yfblk[d0 // 128:(d0 + 511) // 128 + 1])
                                m0 = S - 512 * (tk + 1)
                                P.dma("sp", lambda e, ym=ym, chc=chc, m0=m0: e.dma_start(
                                    out=yfs[:, chc, m0:m0 + 512], in_=ym.t[:]),
                                    reads=ym.b, writes=blocks(yfblk, m0, 512))
                        if not last:
                            for chc in range(2):
                                pyc = psring.next()

                                def fn(e, chc=chc, pyc=pyc):
                                    ins = None
                                    for blk in range(NBC):
                                        for cs in range(2):
                                            ins = e.matmul(pyc.t[:, 0:C],
                                                           Zt[:, NBL + blk, cs * 256 + chc * 128:cs * 256 + (chc + 1) * 128],
                                                           cctx.t[:, cs, blk, :],
                                                           start=(blk == 0 and cs == 0),
                                                           stop=(blk == NBC - 1 and cs == 1))
                                    return ins
                                P.op("pe", fn, reads=cctx.b + Zb[NBL:NBT], writes=pyc.b)
                                yo = yfo.next()
                                P.op("act", lambda e, yo=yo, pyc=pyc: e.activation(
                                    out=yo.t[:, 0:C], in_=pyc.t[:, 0:C], func=AF.Copy), reads=pyc.b, writes=yo.b)
                                P.dma("sp", lambda e, yo=yo, chc=chc: e.dma_start(
                                    out=yfs[:, chc, S:T], in_=yo.t[:, 0:C]),
                                    reads=yo.b, writes=blocks(yfblk, S, C))
                        P.barrier()

                if stop <= 1.4:
                    dump_mix()
                    return
                with ExitStack() as e1:
                    def sb1(name, shape, dt):
                        return e1.enter_context(sbt(name, list(shape), dt))
                    xc = TB(sb1("s_xc", [128, T], F32))
                    xcb = TB(sb1("s_xcb", [128, T], BF16))
                    Hf = TB(sb1("s_Hf", [128, T], F32), NBT)
                    def rng(nm, k=2):
                        return Ring([TB(sb1(f"s_{nm}{i}", [128, 512], F32)) for i in range(k)])
                    SG = 5
                    ag = None
                    if l + 1 < DEPTH:
                        ag = adaln_steps(l + 1, Ring([(TB(sb1(f"s_wada32_{i}", [128, KC, 256], F32)),
                                                      TB(sb1(f"s_wada{i}", [128, KC, 256], BF16))) for i in range(2)]))
                    r_r, r_i, r_a, r_e, r_b, r_h = (rng("r", 2), rng("i", SG + 1), rng("a", SG + 1),
                                                    rng("e", SG + 1), rng("b", 3), rng("h", 3))
                    segs = [(S, C), (0, S)]
                    for cc in range(4):
                        if cc > 0:
                            P.dma("sp", lambda e, c0=cc - 1: e.dma_start(out=gls[:, c0, :], in_=GL[:, c0, :]),
                                  reads=list(GLb[cc - 1]), writes=[glsb[cc - 1]])
                        lxall = [b for b in LXb[cc]]
                        P.op("act", lambda e, cc=cc: e.activation(
                            out=xc.t[:], in_=LX[:, cc, :], func=AF.Identity,
                            bias=lconv.t[:, l, cc, 4:5], scale=lconv.t[:, l, cc, 2:3]),
                            reads=lxall + lconv.b, writes=xc.b)
                        for (s0, sn) in segs:
                            for (tap, sh) in ((0, 2), (1, 1)):
                                P.op("dve", lambda e, cc=cc, s0=s0, sn=sn, tap=tap, sh=sh: e.scalar_tensor_tensor(
                                    out=xc.t[:, s0 + sh:s0 + sn], in0=LX[:, cc, s0:s0 + sn - sh],
                                    scalar=lconv.t[:, l, cc, tap:tap + 1], in1=xc.t[:, s0 + sh:s0 + sn],
                                    op0=ALU.mult, op1=ALU.add),
                                    reads=lxall + lconv.b + xc.b, writes=xc.b)
                            P.op("dve", lambda e, cc=cc, s0=s0, sn=sn: e.scalar_tensor_tensor(
                                out=xc.t[:, s0:s0 + sn - 1], in0=LX[:, cc, s0 + 1:s0 + sn],
                                scalar=lconv.t[:, l, cc, 3:4], in1=xc.t[:, s0:s0 + sn - 1],
                                op0=ALU.mult, op1=ALU.add),
                                reads=lxall + lconv.b + xc.b, writes=xc.b)
                        P.op("act", lambda e: e.activation(out=xcb.t[:], in_=xc.t[:], func=AF.Copy),
                             reads=xc.b, writes=xcb.b)
                        for dr in range(2):
                            prev_h = None
                            plist = []
                            for (s0, sn) in segs:
                                pcs = [(s0 + o, m) for (o, m) in subs_of(sn)]
                                if dr == 1:
                                    pcs = pcs[::-1]
                                plist += [(p0, m, s0) for (p0, m) in pcs]
                            for gi in range(0, len(plist), SG):
                                grp = plist[gi:gi + SG]
                                st = []
                                if ag is not None:
                                    next(ag, None)
                                    next(ag, None)
                                    next(ag, None)
                                for (p0, m, s0) in grp:
                                    pr = psring.next()
                                    pi = psring.next()

                                    def fn(e, pr=pr, pi=pi, p0=p0, m=m, cc=cc, dr=dr):
                                        e.matmul(pr.t[:, 0:m], lrug.t[:, dr, cc, 0, :], xcb.t[:, p0:p0 + m],
                                                 start=True, stop=True)
                                        return e.matmul(pi.t[:, 0:m], lrug.t[:, dr, cc, 1, :], xcb.t[:, p0:p0 + m],
                                                        start=True, stop=True)
                                    P.op("pe", fn, reads=lrug.b + xcb.b, writes=pr.b + pi.b)
                                    tr, ti_, ta, te = r_r.next(), r_i.next(), r_a.next(), r_e.next()
                                    P.op("act", lambda e, pr=pr, tr=tr, m=m, cc=cc, dr=dr: e.activation(
                                        out=tr.t[:, 0:m], in_=pr.t[:, 0:m], func=AF.Tanh, scale=0.5,
                                        bias=lgh.t[:, l, dr, cc, 0:1]), reads=pr.b + lgh.b, writes=tr.b)
                                    P.op("act", lambda e, pi=pi, ti_=ti_, m=m, cc=cc, dr=dr: e.activation(
                                        out=ti_.t[:, 0:m], in_=pi.t[:, 0:m], func=AF.Tanh, scale=0.5,
                                        bias=lgh.t[:, l, dr, cc, 1:2]), reads=pi.b + lgh.b, writes=ti_.b)
                                    P.op("act", lambda e, tr=tr, ta=ta, m=m, cc=cc, dr=dr: e.activation(
                                        out=ta.t[:, 0:m], in_=tr.t[:, 0:m], func=AF.Exp,
                                        scale=lc.t[:, l, dr, cc, 0:1], bias=lc.t[:, l, dr, cc, 0:1]),
                                        reads=tr.b + lc.b, writes=ta.b)
                                    P.op("dve", lambda e, ta=ta, m=m: e.tensor_scalar(
                                        ta.t[:, 0:m], ta.t[:, 0:m], 1.0, None, op0=ALU.min),
                                        reads=ta.b, writes=ta.b)
                                    P.op("dve", lambda e, ta=ta, te=te, m=m: e.tensor_tensor(
                                        out=te.t[:, 0:m], in0=ta.t[:, 0:m], in1=ta.t[:, 0:m], op=ALU.mult),
                                        reads=ta.b, writes=te.b)
                                    P.op("dve", lambda e, ti_=ti_, p0=p0, m=m: e.scalar_tensor_tensor(
                                        out=ti_.t[:, 0:m], in0=ti_.t[:, 0:m], scalar=1.0, in1=xc.t[:, p0:p0 + m],
                                        op0=ALU.add, op1=ALU.mult),
                                        reads=ti_.b + xc.b, writes=ti_.b)
                                    st.append((p0, m, s0, ti_, ta, te))
                                for (p0, m, s0, ti_, ta, te) in st:
                                    tb_ = r_b.next()
                                    P.op("act", lambda e, te=te, m=m: e.activation(
                                        out=te.t[:, 0:m], in_=te.t[:, 0:m], func=AF.Sqrt, scale=-0.25,
                                        bias=qtrb.t[:, 0:1]), reads=te.b + qtrb.b, writes=te.b)
                                    P.op("dve", lambda e, te=te, ti_=ti_, tb_=tb_, m=m: e.tensor_tensor(
                                        out=tb_.t[:, 0:m], in0=te.t[:, 0:m], in1=ti_.t[:, 0:m], op=ALU.mult),
                                        reads=te.b + ti_.b, writes=tb_.b)
                                    first_in_seq = (s0 == S and ((dr == 0 and p0 == S) or
                                                                 (dr == 1 and p0 + m == T)))
                                    blks = list(range(p0 // 128, (p0 + m) // 128))
                                    if dr == 0:
                                        init = 0.0 if first_in_seq else Hf.t[:, p0 - 1:p0] if p0 != 0 else Hf.t[:, T - 1:T]
                                        rd = ta.b + tb_.b
                                        if not first_in_seq:
                                            rd = rd + [Hf.b[(p0 - 1) // 128 if p0 != 0 else NBT - 1]]
                                        P.op("dve", lambda e, ta=ta, tb_=tb_, p0=p0, m=m, init=init: e.tensor_tensor_scan(
                                            out=Hf.t[:, p0:p0 + m], data0=ta.t[:, 0:m], data1=tb_.t[:, 0:m],
                                            initial=init, op0=ALU.mult, op1=ALU.add),
                                            reads=rd, writes=[Hf.b[k] for k in blks])
                                    else:
                                        th = r_h.next()
                                        if first_in_seq:
                                            init = 0.0
                                            rd = ta.b + tb_.b
                                        else:
                                            init = prev_h.t[:, 0:1]
                                            rd = ta.b + tb_.b + prev_h.b
                                        P.op("dve", lambda e, ta=ta, tb_=tb_, th=th, m=m, init=init: e.tensor_tensor_scan(
                                            out=th.t[:, 0:m][:, ::-1], data0=ta.t[:, 0:m][:, ::-1],
                                            data1=tb_.t[:, 0:m][:, ::-1],
                                            initial=init, op0=ALU.mult, op1=ALU.add),
                                            reads=rd, writes=th.b)
                                        prev_h = th
                                        P.op("dve", lambda e, th=th, tb_=tb_, p0=p0, m=m: e.tensor_tensor(
                                            out=tb_.t[:, 0:m], in0=th.t[:, 0:m], in1=Hf.t[:, p0:p0 + m], op=ALU.add),
                                            reads=th.b + [Hf.b[k] for k in blks] + tb_.b, writes=tb_.b)
                                        P.op("dve", lambda e, tb_=tb_, p0=p0, m=m, cc=cc: e.tensor_tensor(
                                            out=GL[:, cc, p0:p0 + m], in0=tb_.t[:, 0:m], in1=GL[:, cc, p0:p0 + m],
                                            op=ALU.mult),
                                            reads=tb_.b + [GLb[cc][k] for k in blks],
                                            writes=[GLb[cc][k] for k in blks])
                    P.dma("sp", lambda e: e.dma_start(out=gls[:, 3, :], in_=GL[:, 3, :]),
                          reads=list(GLb[3]), writes=[glsb[3]])
                    if ag is not None:
                        for _ in ag:
                            pass
                    P.barrier()
                    dump_mix()

                el.close()
                if stop <= 1.6:
                    return
                em.close()
                tb2 = 1024
                NS2 = tb2 // 512
                tiles2 = make_tiles(tb2, not last)
                with ExitStack() as e1:
                    def sb1(name, shape, dt):
                        return e1.enter_context(sbt(name, list(shape), dt))
                    hins2 = Ring([TB(sb1(f"p2_hin{i}", [128, KC, tb2], F32), NS2) for i in range(2)])
                    u2 = TB(sb1("p2_u", [128, KC, tb2], BF16), NS2)
                    sqb2 = [TB(sb1(f"p2_sqb{i}", [128, KC, 512], BF16)) for i in range(NS2)]
                    rstd2 = [TB(sb1(f"p2_rstd{i}", [128, 512], F32)) for i in range(NS2)]
                    tmps2 = Ring([TB(sb1(f"p2_tmp{i}", [128, 512], F32)) for i in range(2)])
                    ytl = Ring([TB(sb1(f"p2_y{i}", [128, 8, tb2], BF16)) for i in range(2)])
                    sgm = Ring([TB(sb1(f"p2_sg{i}", [128, 512], F32)) for i in range(3)])
                    mts = Ring([TB(sb1(f"p2_mt{i}", [128, 512], F32)) for i in range(3)])
                    maccs = Ring([TB(sb1(f"p2_macc{i}", [128, 512], F32)) for i in range(4)])
                    Mbt = sb1("p2_Mb", [128, KC, tb2], BF16)
                    Mbb = [[Buf() for _ in range(NS2)] for _ in range(KC)]
                    wgs = Ring([TB(sb1(f"p2_wg{i}", [128, KC, 128], BF16)) for i in range(4)])
                    wps = Ring([TB(sb1(f"p2_wp{i}", [128, 8, 128], BF16)) for i in range(2)])
                    wos = Ring([TB(sb1(f"p2_wo{i}", [128, KC, 128], BF16)) for i in range(2)])

                    def p2_load(hin, yt, t0, n):
                        load_tile(hin, t0, n, False)
                        P.dma("sp", lambda e: e.dma_start(out=yt.t[:, 0:2, 0:n], in_=yfs[:, :, t0:t0 + n]),
                              reads=blocks(yfblk, t0, n), writes=yt.b)
                        P.dma("sp", lambda e: e.dma_start(out=yt.t[:, 2:6, 0:n], in_=gls[:, :, t0:t0 + n]),
                              reads=glsb + yt.b, writes=yt.b)
                        P.dma("sp", lambda e: e.dma_start(out=yt.t[:, 6:8, 0:n], in_=ycs[:, :, t0:t0 + n]),
                              reads=[ycsb] + yt.b, writes=yt.b)

                    def p2_norm(hin, n, isc, stage="all"):
                        tc = 1 if isc else 0
                        norm_mod(hin, n, u2, sqb2, rstd2, tmps2,
                                 lambda kc: A_ap(l, 1, kc, tc), lambda kc: B_ap(l, 1, kc, tc), [parb[l]], stage=stage)

                    def p2_gates(yt, t0, n, hook=None):
                        subs = subs_of(n)
                        for dc in range(KC):
                            if hook is not None:
                                hook(dc)
                            wp = wps.next()
                            P.dma("pool", lambda e, wp=wp, dc=dc: e.dma_start(out=wp.t[:], in_=wp_l[l, dc]),
                                  writes=wp.b)
                            accs = [maccs.next() for _ in subs]
                            for br in range(3):
                                wg = wgs.next()
                                P.dma("pool", lambda e, wg=wg, br=br, dc=dc: e.dma_start(
                                    out=wg.t[:], in_=win_l[l, 16 + br * 8 + dc]), writes=wg.b)
                                k0, nk = ((0, 2), (2, 4), (6, 2))[br]
                                for si, (o, m) in enumerate(subs):
                                    pg = psring.next()
                                    pv = psring.next()

                                    def fn(e, wg=wg, wp=wp, pg=pg, pv=pv, o=o, m=m, k0=k0, nk=nk):
                                        ins = None
                                        for kc in range(KC):
                                            ins = e.matmul(pg.t[:, 0:m], wg.t[:, kc, :], u2.t[:, kc, o:o + m],
                                                           start=(kc == 0), stop=(kc == KC - 1))
                                        for i in range(nk):
                                            ins = e.matmul(pv.t[:, 0:m], wp.t[:, k0 + i, :], yt.t[:, k0 + i, o:o + m],
                                                           start=(i == 0), stop=(i == nk - 1))
                                        return ins
                                    P.op("pe", fn, reads=wg.b + wp.b + [u2.b[si]] + yt.b, writes=pg.b + pv.b)
                                    sg = sgm.next()
                                    P.op("act", lambda e, sg=sg, pg=pg, m=m: e.activation(
                                        out=sg.t[:, 0:m], in_=pg.t[:, 0:m], func=AF.Sigmoid),
                                        reads=pg.b, writes=sg.b)
                                    acc = accs[si]
                                    if br == 0:
                                        P.op("dve", lambda e, sg=sg, pv=pv, acc=acc, m=m: e.tensor_tensor(
                                            out=acc.t[:, 0:m], in0=sg.t[:, 0:m], in1=pv.t[:, 0:m], op=ALU.mult),
                                            reads=sg.b + pv.b, writes=acc.b)
                                    else:
                                        mt = mts.next()
                                        P.op("dve", lambda e, sg=sg, pv=pv, mt=mt, m=m: e.tensor_tensor(
                                            out=mt.t[:, 0:m], in0=sg.t[:, 0:m], in1=pv.t[:, 0:m], op=ALU.mult),
                                            reads=sg.b + pv.b, writes=mt.b)
                                        if br == 1:
                                            P.op("dve", lambda e, mt=mt, acc=acc, m=m: e.tensor_tensor(
                                                out=acc.t[:, 0:m], in0=acc.t[:, 0:m], in1=mt.t[:, 0:m], op=ALU.add),
                                                reads=mt.b + acc.b, writes=acc.b)
                                        else:
                                            P.op("dve", lambda e, mt=mt, acc=acc, m=m, o=o, dc=dc: e.tensor_tensor(
                                                out=Mbt[:, dc, o:o + m], in0=acc.t[:, 0:m], in1=mt.t[:, 0:m],
                                                op=ALU.add),
                                                reads=mt.b + acc.b, writes=[Mbb[dc][si]])

                    def p2_out(hin, t0, n, isc):
                        tc = 1 if isc else 0
                        subs = subs_of(n)
                        for dc in range(KC):
                            wo = wos.next()
                            P.dma("pool", lambda e, wo=wo, dc=dc: e.dma_start(out=wo.t[:], in_=wout_l[l, dc]),
                                  writes=wo.b)
                            for si, (o, m) in enumerate(subs):
                                po = psring.next()

                                def fn(e, wo=wo, po=po, o=o, m=m):
                                    ins = None
                                    for kc in range(KC):
                                        ins = e.matmul(po.t[:, 0:m], wo.t[:, kc, :], Mbt[:, kc, o:o + m],
                                                       start=(kc == 0), stop=(kc == KC - 1))
                                    return ins
                                P.op("pe", fn, reads=wo.b + [Mbb[kc][si] for kc in range(KC)], writes=po.b)
                                P.op("dve", lambda e, po=po, dc=dc, o=o, m=m, si=si: e.scalar_tensor_tensor(
                                    out=hin.t[:, dc, o:o + m], in0=po.t[:, 0:m], scalar=G_ap(l, 1, dc, tc),
                                    in1=hin.t[:, dc, o:o + m], op0=ALU.mult, op1=ALU.add),
                                    reads=po.b + [parb[l]] + [hin.b[si]], writes=[hin.b[si]])
                        store_tile(hin, t0, n)

                    cur_h, cur_y = hins2.next(), ytl.next()
                    p2_load(cur_h, cur_y, tiles2[0][0], tiles2[0][1])
                    p2_norm(cur_h, tiles2[0][1], tiles2[0][2])
                    for ti, (t0, n, isc) in enumerate(tiles2):
                        nxt_h = nxt_y = None
                        if ti + 1 < len(tiles2):
                            nxt_h, nxt_y = hins2.next(), ytl.next()
                            p2_load(nxt_h, nxt_y, tiles2[ti + 1][0], tiles2[ti + 1][1])
                        hook = None
                        if nxt_h is not None:
                            def hook(dc, nxt_h=nxt_h, nn=tiles2[ti + 1][1], ni=tiles2[ti + 1][2]):
                                if dc == 2:
                                    p2_norm(nxt_h, nn, ni, "sq")
                                if dc == 4:
                                    p2_norm(nxt_h, nn, ni, "stat")
                        p2_gates(cur_y, t0, n, hook)
                        if nxt_h is not None:
                            p2_norm(nxt_h, tiles2[ti + 1][1], tiles2[ti + 1][2], "apply")
                        p2_out(cur_h, t0, n, isc)
                        cur_h, cur_y = nxt_h, nxt_y
                    P.barrier()

        P.barrier()
        for l in range(DEPTH):
            last = (l == DEPTH - 1)
            if not skipffn:
                ffn_phase(l, 0, from_input=(l == 0), with_ctx=True)
            if stop <= 1:
                break
            if dbg and l == 0:
                P.dma("sp", lambda e: e.dma_start(out=hdump0, in_=hbuf), writes=[Buf()])
                P.barrier()
            mixer(l, False, last)
            if stop <= 2:
                break
            ffn_phase(l, 1, from_input=False, with_ctx=not last, final_norm=last)
        P.barrier()
        if dbg:
            P.dma("sp", lambda e: e.dma_start(out=hdump, in_=hbuf), writes=[Buf()])
            P.dma("sp", lambda e: e.dma_start(out=yfdump, in_=yfs), writes=[Buf()])
            P.barrier()
        P.emit(nc)
    return nc


def _fm(a):
    a = np.asarray(a)
    lead = a.shape[:-1]
    nk = a.shape[-1] // 128
    a = a.reshape(lead + (nk, 128))
    perm = (len(lead) + 1, len(lead)) + tuple(range(len(lead)))
    return np.ascontiguousarray(a.transpose(perm))


def prep_shared(inp, S, C, DEPTH):
    f = np.float32
    sh = {}
    w_ada = np.asarray(inp["w_ada"], f)
    sh["w_ada_l"] = np.ascontiguousarray(w_ada.reshape(DEPTH, KC, 128, 18, 512).transpose(0, 3, 2, 1, 4))
    sh["b_ada_l"] = np.ascontiguousarray(np.asarray(inp["b_ada"], f).reshape(DEPTH, 72, 128).transpose(2, 0, 1))
    g = np.concatenate([np.stack([inp["g_ffn1"][l], inp["g_mix"][l], inp["g_ffn2"][l]]) for l in range(DEPTH)]
                       + [np.asarray(inp["g_final"])[None]], 0).astype(f)
    sh["gvec"] = np.ascontiguousarray(g.reshape(-1, KC, 128).transpose(2, 0, 1))
    wg = np.asarray(inp["ffn_w_gate"], f).reshape(DEPTH, 2, KC, 128, FC, 128)
    wu = np.asarray(inp["ffn_w_up"], f).reshape(DEPTH, 2, KC, 128, FC, 128)
    gu = np.stack([wg, wu], 2)
    sh["wgu_l"] = np.ascontiguousarray(gu.transpose(0, 1, 5, 4, 2, 3, 6))
    del gu, wg, wu
    wd = np.asarray(inp["ffn_w_down"], f).reshape(DEPTH, 2, FC, 128, KC, 128)
    sh["wd_l"] = np.ascontiguousarray(wd.transpose(0, 1, 4, 3, 2, 5))
    win = np.asarray(inp["w_in"], f).reshape(DEPTH, KC, 128, 40, 128)
    sh["win_l"] = np.ascontiguousarray(win.transpose(0, 3, 2, 1, 4))
    wp = np.concatenate([inp["wp_fourier"], inp["wp_lru"], inp["wp_conv"]], 1).astype(f)
    sh["wp_l"] = np.ascontiguousarray(wp.reshape(DEPTH, 8, 128, KC, 128).transpose(0, 3, 2, 1, 4))
    wo = np.asarray(inp["w_out"], f).reshape(DEPTH, KC, 128, KC, 128)
    sh["wout_l"] = np.ascontiguousarray(wo.transpose(0, 3, 2, 1, 4))
    bd = np.zeros((DEPTH, 128, 2, 4, 2, 128), f)
    for ax, key in enumerate(("lru_wa", "lru_wx")):
        w = np.asarray(inp[key], f)
        for cc in range(4):
            for hl in range(2):
                bd[:, hl * 64:(hl + 1) * 64, :, cc, ax, hl * 64:(hl + 1) * 64] = \
                    w[:, :, cc * 2 + hl].transpose(0, 2, 1, 3)
    sh["lrug_l"] = bd
    lc = np.concatenate([np.asarray(inp["lru_conv_w"], f), np.asarray(inp["lru_conv_b"], f)[:, None]], 1)
    sh["lconv_l"] = np.ascontiguousarray(lc.reshape(DEPTH, 5, 4, 128).transpose(3, 0, 2, 1))
    lg = np.stack([np.asarray(inp["lru_ba"], f).reshape(DEPTH, 2, 512),
                   np.asarray(inp["lru_bx"], f).reshape(DEPTH, 2, 512),
                   np.asarray(inp["lru_lambda"], f)], -1)
    sh["lgate_l"] = np.ascontiguousarray(lg.reshape(DEPTH, 2, 4, 128, 3).transpose(3, 0, 1, 2, 4))
    sc = np.asarray(inp["sc_conv_w"], f)
    sh["sconv_l"] = np.ascontiguousarray(sc.reshape(DEPTH, 3, 2, 128).transpose(3, 0, 2, 1))
    i64 = np.arange(64)
    ang = 2 * np.pi * np.outer(i64, i64) / 64
    cc_, sc_ = np.cos(ang) / 8.0, -np.sin(ang) / 8.0
    ccbd = np.zeros((128, 2, 128), f)
    for h in range(2):
        ccbd[h * 64:(h + 1) * 64, 0, h * 64:(h + 1) * 64] = cc_
        ccbd[h * 64:(h + 1) * 64, 1, h * 64:(h + 1) * 64] = sc_
    sh["ccbd_l"] = ccbd
    tc_ = np.arange(C)
    angc = 2 * np.pi * (np.outer(tc_, tc_) % C) / C
    cs = np.stack([np.cos(angc), np.sin(angc)], 0) / math.sqrt(C)
    sh["cctx_l"] = np.ascontiguousarray(cs.reshape(2, C // 128, 128, C).transpose(2, 0, 1, 3)).astype(f)
    ts_ = np.arange(S, dtype=np.int64)
    NT = S // 512
    NTH = NT // 2
    NBL = S // 128
    GS = min(8, NBL)
    NG = NBL // GS
    dft = np.empty((NTH, NG, 128, 2, GS, 512), ml_dtypes.bfloat16)
    scl = 1.0 / math.sqrt(S)
    for tk in range(NTH):
        prod = np.outer(ts_, ts_[1 + tk * 512:1 + (tk + 1) * 512]) % S
        a = 2 * np.pi * prod / S
        for cs_i, m in enumerate((np.cos(a) * scl, np.sin(a) * scl)):
            m = m.reshape(NG, GS, 128, 512).transpose(0, 2, 1, 3)
            dft[tk, :, :, cs_i] = m.astype(ml_dtypes.bfloat16)
    sh["dft_l"] = dft
    return sh


def prep_core(inp, b):
    f = np.float32
    d = {}
    d["x_l"] = _fm(np.asarray(inp["x"][b], f))
    d["ctx_l"] = _fm(np.asarray(inp["ctx"][b], f))
    cv = np.stack([np.asarray(inp["c"][b], f), np.asarray(inp["c_ctx"], f)], 0)
    d["cvec"] = _fm(cv)
    return d


_NC_CACHE = {}


def run(inp, S, C, DEPTH, n_cores):
    key = (S, C, DEPTH)
    if key not in _NC_CACHE:
        _NC_CACHE[key] = build(S, C, DEPTH)
    nc = _NC_CACHE[key]
    sh = prep_shared(inp, S, C, DEPTH)
    in_maps = []
    for b in range(n_cores):
        m = dict(sh)
        m.update(prep_core(inp, b))
        in_maps.append(m)
    res = run_bass_kernel_spmd(nc, in_maps, core_ids=list(range(n_cores)))
    out = np.empty((n_cores, S, D), np.float32)
    for b in range(n_cores):
        yT = np.asarray(res.results[b]["yT"])
        out[b] = yT.transpose(2, 1, 0).reshape(S, D)
    return out


def kernel(**inputs):
    return run(inputs, 4096, 256, 4, N_CORES)
```

```python
import math
from contextlib import ExitStack

import numpy as np
import ml_dtypes

import concourse.bass as bass
import concourse.mybir as mybir
from concourse.bass_utils import run_bass_kernel_spmd

F32 = mybir.dt.float32
BF16 = mybir.dt.bfloat16
AF = mybir.ActivationFunctionType
ALU = mybir.AluOpType

D = 1024
KC = 8
DFF = 2816
FC = 22
NMOD = 9
GRID_W = 64
RMS_EPS = 1e-6
IN_COLS = 5120
N_CORES = 8


class Buf:
    __slots__ = ("w", "r")

    def __init__(self):
        self.w = None
        self.r = []


class Prog:
    ENGS = ("pe", "act", "dve", "pool", "sp")

    def __init__(self, esems, dsems):
        self.streams = {e: [] for e in self.ENGS}
        self.cnt = {e: 0 for e in self.ENGS}
        self.sat = {e: {} for e in self.ENGS}
        self.esem = esems
        self.dsl = {e: [[s, 0] for s in ss] for e, ss in dsems.items()}
        self.dqi = {e: 0 for e in dsems}
        self.semid = {}
        self.all_events = {}

    def _key(self, sem):
        return id(sem)

    def _prune(self, eng, evs):
        best = {}
        for ev in evs:
            if ev is None:
                continue
            sem, val = ev
            k = self._key(sem)
            if eng == "pe" and sem is self.esem["pe"]:
                continue
            if k not in best or best[k][1] < val:
                best[k] = (sem, val)
        waits = []
        for k, (sem, val) in best.items():
            if self.sat[eng].get(k, 0) < val:
                self.sat[eng][k] = val
                waits.append((sem, val))
        return waits

    def _deps(self, reads, writes):
        evs = []
        for b in reads:
            if b.w is not None:
                evs.append(b.w)
        for b in writes:
            if b.w is not None:
                evs.append(b.w)
            evs.extend(b.r)
        return evs

    def _commit(self, ev, reads, writes):
        for b in reads:
            b.r.append(ev)
        for b in writes:
            b.w = ev
            b.r = []
        self.all_events[self._key(ev[0])] = ev

    def op(self, eng, fn, reads=(), writes=(), extra=()):
        evs = self._deps(reads, writes) + list(extra)
        waits = self._prune(eng, evs)
        self.cnt[eng] += 1
        ev = (self.esem[eng], self.cnt[eng])
        self.streams[eng].append((fn, waits, ev, 1))
        self._commit(ev, reads, writes)
        return ev

    def dma(self, eng, fn, reads=(), writes=(), extra=()):
        slots = self.dsl[eng]
        slot = slots[self.dqi[eng] % len(slots)]
        self.dqi[eng] += 1
        evs = self._deps(reads, writes) + list(extra)
        if slot[1] > 0:
            evs.append((slot[0], slot[1]))
        waits = self._prune(eng, evs)
        slot[1] += 16
        ev = (slot[0], slot[1])
        self.streams[eng].append((fn, waits, ev, 16))
        self._commit(ev, reads, writes)
        return ev

    def last_ev(self, eng):
        return (self.esem[eng], self.cnt[eng])

    def barrier(self):
        evs = list(self.all_events.values())
        for e in self.ENGS:
            waits = self._prune(e, evs)
            if waits:
                self.streams[e].append((None, waits, None, 0))

    def emit(self, nc):
        engmap = {}
        with nc.Block() as block:
            def run(eng_name):
                def body(e):
                    for fn, waits, ev, inc in self.streams[eng_name]:
                        for sem, val in waits:
                            e.wait_ge(sem, val)
                        if fn is None:
                            continue
                        ins = fn(e)
                        ins.then_inc(ev[0], inc)
                return body
            block.tensor(run("pe"))
            block.scalar(run("act"))
            block.vector(run("dve"))
            block.gpsimd(run("pool"))
            block.sync(run("sp"))


class TB:
    def __init__(self, t, nb=1):
        self.t = t
        self.b = [Buf() for _ in range(nb)]


class Ring:
    def __init__(self, items):
        self.items = items
        self.i = 0

    def next(self):
        it = self.items[self.i % len(self.items)]
        self.i += 1
        return it


def build(S, C, DEPTH, ffn_tb=1024, mix_tb=512, stop=99, dbg=False, skipffn=False):
    T = S + C
    NBT = T // 128
    NBL = S // 128
    NBC = C // 128
    NT = S // 512
    GS = min(8, NBL)
    NG = NBL // GS
    NV = DEPTH * 3 + 1

    nc = bass.Bass("TRN2", target_bir_lowering=False)

    def din(name, shape, dt=F32):
        return nc.dram_tensor(name, list(shape), dt, kind="ExternalInput").ap()

    x_l = din("x_l", [128, KC, S])
    ctx_l = din("ctx_l", [128, KC, C])
    cvec = din("cvec", [128, KC, 2])
    w_ada_l = din("w_ada_l", [DEPTH, 18, 128, KC, 512])
    b_ada_l = din("b_ada_l", [128, DEPTH, 72])
    gvec = din("gvec", [128, NV, KC])
    wgu_l = din("wgu_l", [DEPTH, 2, FC, 128, 2, KC, 128])
    wd_l = din("wd_l", [DEPTH, 2, KC, 128, FC, 128])
    win_l = din("win_l", [DEPTH, 40, 128, KC, 128])
    wp_l = din("wp_l", [DEPTH, KC, 128, 8, 128])
    wout_l = din("wout_l", [DEPTH, KC, 128, KC, 128])
    lrug_l = din("lrug_l", [DEPTH, 128, 2, 4, 2, 128])
    lconv_l = din("lconv_l", [128, DEPTH, 4, 5])
    lgate_l = din("lgate_l", [128, DEPTH, 2, 4, 3])
    sconv_l = din("sconv_l", [128, DEPTH, 2, 3])
    ccbd_l = din("ccbd_l", [128, 2, 128])
    cctx_l = din("cctx_l", [128, 2, NBC, C])
    NTH = NT // 2
    dft_l = din("dft_l", [NTH, NG, 128, 2, GS, 512], BF16)
    yT = nc.dram_tensor("yT", [128, KC, S], F32, kind="ExternalOutput").ap()
    hbuf = nc.dram_tensor("hbuf_i", [128, KC, T], F32).ap()
    yfs = nc.dram_tensor("yfs_i", [128, 2, T], BF16).ap()
    gls = nc.dram_tensor("gls_i", [128, 4, T], BF16).ap()
    ycs = nc.dram_tensor("ycs_i", [128, 2, T], BF16).ap()
    zscr = nc.dram_tensor("zscr_i", [T // 128, 128, 512], BF16).ap()
    if dbg:
        hdump = nc.dram_tensor("hbuf", [128, KC, T], F32, kind="ExternalOutput").ap()
        yfdump = nc.dram_tensor("yfs", [128, 2, T], BF16, kind="ExternalOutput").ap()
        moddbg = nc.dram_tensor("moddbg", [128, DEPTH, 72, 2], F32, kind="ExternalOutput").ap()
        dbgGL = nc.dram_tensor("dbgGL", [128, 4, T], BF16, kind="ExternalOutput").ap()
        dbgYC = nc.dram_tensor("dbgYC", [128, 2, T], BF16, kind="ExternalOutput").ap()
        dbgLX = nc.dram_tensor("dbgLX", [128, 4, T], BF16, kind="ExternalOutput").ap()
        dbgU = nc.dram_tensor("dbgU", [128, KC, 512], BF16, kind="ExternalOutput").ap()
        dbgR = nc.dram_tensor("dbgR", [128, 512], F32, kind="ExternalOutput").ap()
        dbgSQ = nc.dram_tensor("dbgSQ", [128, KC, 512], BF16, kind="ExternalOutput").ap()
        dbgH = nc.dram_tensor("dbgH", [128, KC, 512], F32, kind="ExternalOutput").ap()
        dbgH0 = nc.dram_tensor("dbgH0", [128, KC, 512], F32, kind="ExternalOutput").ap()
        hdump0 = nc.dram_tensor("hdump0", [128, KC, T], F32, kind="ExternalOutput").ap()

    _uid = [0]

    def sbt(name, shape, dt):
        _uid[0] += 1
        return nc.sbuf_tensor(f"{name}_{_uid[0]}", shape, dt)

    es = ExitStack()
    with es:
        def sb(name, shape, dt):
            return es.enter_context(sbt(name, list(shape), dt))

        esems = {e: es.enter_context(nc.semaphore("es_" + e)) for e in Prog.ENGS}
        dsems = {
            "pool": [es.enter_context(nc.semaphore(f"dq_pool{i}")) for i in range(8)],
            "sp": [es.enter_context(nc.semaphore(f"dq_sp{i}")) for i in range(8)],
        }
        P = Prog(esems, dsems)

        psb = [TB(es.enter_context(nc.psum_tensor(f"ps{i}", [128, 512], F32))) for i in range(8)]
        psring = Ring(psb[:7])

        ones_bf = TB(sb("ones_bf", [128, 128], BF16))
        s2 = TB(sb("s2", [128, KC, 2], F32))
        s2b = TB(sb("s2b", [128, KC, 2], BF16))
        modT = TB(sb("modT", [128, DEPTH, 72, 2], F32))
        Aall = TB(sb("Aall", [128, NV, KC, 2], F32))
        gv = TB(sb("gv", [128, NV, KC], F32))
        bada = TB(sb("bada", [128, DEPTH, 72], F32))
        Gall = TB(sb("Gall", [128, DEPTH * 3, KC, 2], F32))
        lconv = TB(sb("lconv", [128, DEPTH, 4, 5], F32))
        lgate = TB(sb("lgate", [128, DEPTH, 2, 4, 3], F32))
        lc = TB(sb("lc", [128, DEPTH, 2, 4, 2], F32))
        lgh = TB(sb("lgh", [128, DEPTH, 2, 4, 2], F32))
        sconv = TB(sb("sconv", [128, DEPTH, 2, 3], F32))
        ccbd = TB(sb("ccbd", [128, 2, 128], BF16))
        cctx = TB(sb("cctx", [128, 2, NBC, C], BF16))
        lrug = TB(sb("lrug", [128, 2, 4, 2, 128], BF16))
        epsb = TB(sb("epsb", [128, 1], F32))
        oneb = TB(sb("oneb", [128, 1], F32))
        qtrb = TB(sb("qtrb", [128, 1], F32))
        isqb = TB(sb("isqb", [128, 2], BF16))

        hblk = [Buf() for _ in range(T // 128)]
        glsb = [Buf() for _ in range(4)]
        ycsb = Buf()
        yfblk = [Buf() for _ in range(T // 128)]

        def blocks(bl, t0, n):
            return bl[t0 // 128:(t0 + n) // 128]

        def make_tiles(tb, with_ctx=True):
            res = []
            if with_ctx:
                t = S
                while t < T:
                    n = min(tb, T - t)
                    res.append((t, n, True))
                    t += n
            t = 0
            while t < S:
                n = min(tb, S - t)
                res.append((t, n, False))
                t += n
            return res

        def subs_of(n):
            return [(o, min(512, n - o)) for o in range(0, n, 512)]

        P.op("dve", lambda e: e.memset(ones_bf.t[:], 1.0), writes=ones_bf.b)
        P.op("dve", lambda e: e.memset(epsb.t[:], float(D * RMS_EPS)), writes=epsb.b)
        P.op("dve", lambda e: e.memset(oneb.t[:], 1.0), writes=oneb.b)
        P.op("dve", lambda e: e.memset(qtrb.t[:], 0.25), writes=qtrb.b)
        P.op("dve", lambda e: e.memset(isqb.t[:], 1.0 / math.sqrt(S)), writes=isqb.b)
        cv = TB(sb("cv", [128, KC, 2], F32))
        P.dma("sp", lambda e: e.dma_start(out=cv.t[:], in_=cvec), writes=cv.b)
        P.dma("sp", lambda e: e.dma_start(out=gv.t[:], in_=gvec), writes=gv.b)
        P.dma("sp", lambda e: e.dma_start(out=bada.t[:], in_=b_ada_l), writes=bada.b)
        P.dma("sp", lambda e: e.dma_start(out=lconv.t[:], in_=lconv_l), writes=lconv.b)
        P.dma("sp", lambda e: e.dma_start(out=lgate.t[:], in_=lgate_l), writes=lgate.b)
        P.dma("sp", lambda e: e.dma_start(out=sconv.t[:], in_=sconv_l), writes=sconv.b)
        P.dma("pool", lambda e: e.dma_start(out=ccbd.t[:], in_=ccbd_l), writes=ccbd.b)
        P.dma("pool", lambda e: e.dma_start(out=cctx.t[:], in_=cctx_l), writes=cctx.b)
        P.op("act", lambda e: e.activation(out=s2.t[:], in_=cv.t[:], func=AF.Silu),
             reads=cv.b, writes=s2.b)
        P.op("act", lambda e: e.activation(out=s2b.t[:], in_=cv.t[:], func=AF.Silu),
             reads=cv.b, writes=s2b.b)
        lt = TB(sb("lt", [128, DEPTH, 2, 4], F32))
        P.op("act", lambda e: e.activation(out=lt.t[:], in_=lgate.t[:, :, :, :, 2], func=AF.Exp, scale=-1.0),
             reads=lgate.b, writes=lt.b)
        P.op("act", lambda e: e.activation(out=lt.t[:], in_=lt.t[:], func=AF.Ln, bias=1.0),
             reads=lt.b, writes=lt.b)
        P.op("dve", lambda e: e.tensor_scalar(lc.t[:, :, :, :, 0], lt.t[:], -4.0, None, op0=ALU.mult),
             reads=lt.b, writes=lc.b)
        P.op("dve", lambda e: e.tensor_scalar(lc.t[:, :, :, :, 1], lt.t[:], -8.0, None, op0=ALU.mult),
             reads=lt.b + lc.b, writes=lc.b)
        P.op("dve", lambda e: e.tensor_scalar(lgh.t[:], lgate.t[:, :, :, :, 0:2], 0.5, None, op0=ALU.mult),
             reads=lgate.b, writes=lgh.b)

        parb = [Buf() for _ in range(DEPTH + 1)]
        pm_bank = psb[7]
        pmv_all = pm_bank.t[:, 0:144].rearrange("p (j t) -> p j t", t=2)
        NAP = 36

        def adaln_steps(l, wa_ring):
            for piece in range(NAP):
                w32, wa = wa_ring.next()
                P.dma("sp", lambda e, w32=w32, piece=piece: e.dma_start(
                    out=w32.t[:], in_=w_ada_l[l, piece // 2][:, :, (piece % 2) * 256:(piece % 2 + 1) * 256]),
                    writes=w32.b)
                P.op("act", lambda e, w32=w32, wa=wa: e.activation(out=wa.t[:], in_=w32.t[:], func=AF.Copy),
                     reads=w32.b, writes=wa.b)

                def fn(e, wa=wa, piece=piece):
                    ins = None
                    for j in range(2):
                        for kc in range(KC):
                            ins = e.matmul(pmv_all[:, piece * 2 + j, :], wa.t[:, kc, j * 128:(j + 1) * 128],
                                           s2b.t[:, kc, :], start=(kc == 0), stop=(kc == KC - 1))
                    return ins
                P.op("pe", fn, reads=wa.b + s2b.b, writes=pm_bank.b if piece == 0 else ())
                pm_bank.b[0].w = (P.esem["pe"], P.cnt["pe"])
                yield
            for t in range(2):
                P.op("dve", lambda e, t=t: e.tensor_tensor(
                    out=modT.t[:, l, :, t], in0=pmv_all[:, :, t], in1=bada.t[:, l, :], op=ALU.add),
                    reads=pm_bank.b + bada.b, writes=[parb[l]])
            for w in range(3):
                for t in range(2):
                    P.op("dve", lambda e, w=w, t=t: e.scalar_tensor_tensor(
                        out=Aall.t[:, l * 3 + w, :, t], in0=modT.t[:, l, (3 * w + 1) * 8:(3 * w + 2) * 8, t],
                        scalar=1.0, in1=gv.t[:, l * 3 + w, :], op0=ALU.add, op1=ALU.mult),
                        reads=gv.b + [parb[l]], writes=[parb[l]])
                    P.op("dve", lambda e, w=w, t=t: e.tensor_scalar(
                        Gall.t[:, l * 3 + w, :, t], modT.t[:, l, (3 * w + 2) * 8:(3 * w + 3) * 8, t],
                        (1.0 if w == 1 else 0.5), None, op0=ALU.mult),
                        reads=[parb[l]], writes=[parb[l]])
            P.op("dve", lambda e: e.tensor_scalar(Aall.t[:, l * 3:l * 3 + 3], Aall.t[:, l * 3:l * 3 + 3], 32.0, None,
                                                  op0=ALU.mult), reads=[parb[l]], writes=[parb[l]])
            yield

        with ExitStack() as es2:
            wa_ring0 = Ring([(TB(es2.enter_context(sbt(f"wada32_{i}", [128, KC, 256], F32))),
                              TB(es2.enter_context(sbt(f"wada{i}", [128, KC, 256], BF16)))) for i in range(2)])
            for _ in adaln_steps(0, wa_ring0):
                pass
            P.barrier()
        P.op("dve", lambda e: e.tensor_scalar(Aall.t[:, NV - 1, :, 0], gv.t[:, NV - 1, :], 32.0, None, op0=ALU.mult),
             reads=gv.b, writes=[parb[DEPTH]])

        if dbg:
            P.dma("sp", lambda e: e.dma_start(out=moddbg, in_=modT.t[:]), reads=[parb[0]], writes=[Buf()])

        def A_ap(l, w, kc, t):
            return Aall.t[:, l * 3 + w, kc, t:t + 1]

        def B_ap(l, w, kc, t):
            return modT.t[:, l, (3 * w) * 8 + kc, t:t + 1]

        def G_ap(l, w, kc, t):
            return Gall.t[:, l * 3 + w, kc, t:t + 1]

        def norm_mod(hin, n, xn, sqb, rstd, tmps, a_fn, b_fn, pb, stage="all"):
            for si, (o, m) in enumerate(subs_of(n)):
                sq = sqb[si] if isinstance(sqb, list) else sqb
                rs = rstd[si] if isinstance(rstd, list) else rstd
                if stage in ("all", "sq"):
                    P.op("act", lambda e, o=o, m=m, sq=sq: e.activation(
                        out=sq.t[:, :, 0:m], in_=hin.t[:, :, o:o + m], func=AF.Square),
                        reads=[hin.b[si]], writes=sq.b)
                if stage in ("all", "stat"):
                    pn = psring.next()

                    def fn(e, pn=pn, m=m, sq=sq):
                        ins = None
                        for kc in range(KC):
                            ins = e.matmul(pn.t[:, 0:m], ones_bf.t[:], sq.t[:, kc, 0:m],
                                           start=(kc == 0), stop=(kc == KC - 1))
                        return ins
                    P.op("pe", fn, reads=sq.b + ones_bf.b, writes=pn.b)
                    P.op("act", lambda e, pn=pn, m=m, rs=rs: e.activation(
                        out=rs.t[:, 0:m], in_=pn.t[:, 0:m], func=AF.Sqrt, bias=epsb.t[:, 0:1]),
                        reads=pn.b + epsb.b, writes=rs.b)
                    P.op("dve", lambda e, m=m, rs=rs: e.reciprocal(out=rs.t[:, 0:m], in_=rs.t[:, 0:m]),
                         reads=rs.b, writes=rs.b)
                if stage in ("all", "apply"):
                    for kc in range(KC):
                        tm = tmps.next()
                        P.op("dve", lambda e, tm=tm, kc=kc, o=o, m=m, rs=rs: e.tensor_tensor(
                            out=tm.t[:, 0:m], in0=hin.t[:, kc, o:o + m], in1=rs.t[:, 0:m], op=ALU.mult),
                            reads=[hin.b[si]] + rs.b, writes=tm.b)
                        bb = b_fn(kc)
                        aa = a_fn(kc)
                        P.op("act", lambda e, tm=tm, kc=kc, o=o, m=m, bb=bb, aa=aa, si=si: e.activation(
                            out=xn.t[:, kc, o:o + m], in_=tm.t[:, 0:m], func=AF.Identity,
                            bias=bb, scale=aa),
                            reads=tm.b + pb, writes=[xn.b[si]])

        def load_tile(hin, t0, n, from_input):
            if from_input:
                src = ctx_l[:, :, t0 - S:t0 - S + n] if t0 >= S else x_l[:, :, t0:t0 + n]
                rd = []
            else:
                src = hbuf[:, :, t0:t0 + n]
                rd = blocks(hblk, t0, n)
            P.dma("sp", lambda e: e.dma_start(out=hin.t[:, :, 0:n], in_=src), reads=rd, writes=hin.b)

        def store_tile(hin, t0, n):
            P.dma("sp", lambda e: e.dma_start(out=hbuf[:, :, t0:t0 + n], in_=hin.t[:, :, 0:n]),
                  reads=hin.b, writes=blocks(hblk, t0, n))

        def ffn_phase(l, j, from_input, with_ctx, final_norm=False):
            w = 0 if j == 0 else 2
            tiles = make_tiles(ffn_tb, with_ctx)
            NS = ffn_tb // 512
            with ExitStack() as e2:
                def sb2(name, shape, dt):
                    return e2.enter_context(sbt(name, list(shape), dt))
                hins = Ring([TB(sb2(f"f_hin{i}", [128, KC, ffn_tb], F32), NS) for i in range(2)])
                xn = TB(sb2("f_xn", [128, KC, ffn_tb], BF16), NS)
                At = [[None] * NS for _ in range(FC)]
                Atile = sb2("f_A", [128, FC, ffn_tb], BF16)
                Ab = [[Buf() for _ in range(NS)] for _ in range(FC)]
                sqb = [TB(sb2(f"f_sqb{i}", [128, KC, 512], BF16)) for i in range(NS)]
                rstd = [TB(sb2(f"f_rstd{i}", [128, 512], F32)) for i in range(NS)]
                tmps = Ring([TB(sb2(f"f_tmp{i}", [128, 512], F32)) for i in range(2)])
                sgs = Ring([TB(sb2(f"f_sg{i}", [128, 512], F32)) for i in range(2)])
                wgus = Ring([TB(sb2(f"f_wgu{i}", [128, 2, KC, 128], BF16)) for i in range(4)])
                wds = Ring([TB(sb2(f"f_wd{i}", [128, FC, 128], BF16)) for i in range(3)])
                if final_norm:
                    outs = Ring([TB(sb2(f"f_out{i}", [128, 512], F32)) for i in range(2)])

                def do_norm(hin, n, tc, stage="all"):
                    norm_mod(hin, n, xn, sqb, rstd, tmps,
                             lambda kc: A_ap(l, w, kc, tc), lambda kc: B_ap(l, w, kc, tc), [parb[l]], stage=stage)

                cur = hins.next()
                load_tile(cur, tiles[0][0], tiles[0][1], from_input)
                do_norm(cur, tiles[0][1], 1 if tiles[0][2] else 0)
                for ti, (t0, n, isc) in enumerate(tiles):
                    tc = 1 if isc else 0
                    subs = subs_of(n)
                    nxt = None
                    if ti + 1 < len(tiles):
                        nxt = hins.next()
                        load_tile(nxt, tiles[ti + 1][0], tiles[ti + 1][1], from_input)
                    for fc in range(FC):
                        if nxt is not None and fc == 6:
                            do_norm(nxt, tiles[ti + 1][1], 1 if tiles[ti + 1][2] else 0, "sq")
                        if nxt is not None and fc == 12:
                            do_norm(nxt, tiles[ti + 1][1], 1 if tiles[ti + 1][2] else 0, "stat")
                        wgu = wgus.next()
                        P.dma("pool", lambda e, wgu=wgu, fc=fc: e.dma_start(out=wgu.t[:], in_=wgu_l[l, j, fc]),
                              writes=wgu.b)
                        for si, (o, m) in enumerate(subs):
                            pg = psring.next()
                            pu = psring.next()

                            def fn(e, wgu=wgu, pg=pg, pu=pu, o=o, m=m):
                                ins = None
                                for gu, pp in ((0, pg), (1, pu)):
                                    for kc in range(KC):
                                        ins = e.matmul(pp.t[:, 0:m], wgu.t[:, gu, kc, :], xn.t[:, kc, o:o + m],
                                                       start=(kc == 0), stop=(kc == KC - 1))
                                return ins
                            P.op("pe", fn, reads=wgu.b + [xn.b[si]], writes=pg.b + pu.b)
                            sg = sgs.next()
                            P.op("act", lambda e, sg=sg, pg=pg, m=m: e.activation(
                                out=sg.t[:, 0:m], in_=pg.t[:, 0:m], func=AF.Silu),
                                reads=pg.b, writes=sg.b)
                            P.op("dve", lambda e, sg=sg, pu=pu, fc=fc, o=o, m=m: e.tensor_tensor(
                                out=Atile[:, fc, o:o + m], in0=sg.t[:, 0:m], in1=pu.t[:, 0:m], op=ALU.mult),
                                reads=sg.b + pu.b, writes=[Ab[fc][si]])
                    if nxt is not None:
                        do_norm(nxt, tiles[ti + 1][1], 1 if tiles[ti + 1][2] else 0, "apply")
                    for dc in range(KC):
                        wd = wds.next()
                        P.dma("pool", lambda e, wd=wd, dc=dc: e.dma_start(out=wd.t[:], in_=wd_l[l, j, dc]),
                              writes=wd.b)
                        for si, (o, m) in enumerate(subs):
                            po = psring.next()

                            def fn(e, wd=wd, po=po, o=o, m=m):
                                ins = None
                                for fc in range(FC):
                                    ins = e.matmul(po.t[:, 0:m], wd.t[:, fc, :], Atile[:, fc, o:o + m],
                                                   start=(fc == 0), stop=(fc == FC - 1))
                                return ins
                            P.op("pe", fn, reads=wd.b + [Ab[fc][si] for fc in range(FC)], writes=po.b)
                            P.op("dve", lambda e, po=po, dc=dc, o=o, m=m, cur=cur, tc=tc: e.scalar_tensor_tensor(
                                out=cur.t[:, dc, o:o + m], in0=po.t[:, 0:m], scalar=G_ap(l, w, dc, tc),
                                in1=cur.t[:, dc, o:o + m], op0=ALU.mult, op1=ALU.add),
                                reads=po.b + [parb[l]] + [cur.b[si]], writes=[cur.b[si]])
                    if not final_norm:
                        store_tile(cur, t0, n)
                    else:
                        for si, (o, m) in enumerate(subs):
                            sq, rs = sqb[si], rstd[si]
                            P.op("act", lambda e, o=o, m=m, cur=cur, sq=sq: e.activation(
                                out=sq.t[:, :, 0:m], in_=cur.t[:, :, o:o + m], func=AF.Square),
                                reads=[cur.b[si]], writes=sq.b)
                            pn = psring.next()

                            def fn(e, pn=pn, m=m, sq=sq):
                                ins = None
                                for kc in range(KC):
                                    ins = e.matmul(pn.t[:, 0:m], ones_bf.t[:], sq.t[:, kc, 0:m],
                                                   start=(kc == 0), stop=(kc == KC - 1))
                                return ins
                            P.op("pe", fn, reads=sq.b + ones_bf.b, writes=pn.b)
                            P.op("act", lambda e, pn=pn, m=m, rs=rs: e.activation(
                                out=rs.t[:, 0:m], in_=pn.t[:, 0:m], func=AF.Sqrt, bias=epsb.t[:, 0:1]),
                                reads=pn.b + epsb.b, writes=rs.b)
                            P.op("dve", lambda e, m=m, rs=rs: e.reciprocal(out=rs.t[:, 0:m], in_=rs.t[:, 0:m]),
                                 reads=rs.b, writes=rs.b)
                            for kc in range(KC):
                                ot = outs.next()
                                P.op("dve", lambda e, ot=ot, kc=kc, o=o, m=m, cur=cur, rs=rs: e.scalar_tensor_tensor(
                                    out=ot.t[:, 0:m], in0=cur.t[:, kc, o:o + m], scalar=Aall.t[:, NV - 1, kc, 0:1],
                                    in1=rs.t[:, 0:m], op0=ALU.mult, op1=ALU.mult),
                                    reads=[cur.b[si]] + rs.b + [parb[DEPTH]], writes=ot.b)
                                P.dma("sp", lambda e, ot=ot, kc=kc, o=o, m=m, t0=t0: e.dma_start(
                                    out=yT[:, kc, t0 + o:t0 + o + m], in_=ot.t[:, 0:m]),
                                    reads=ot.b, writes=[Buf()])
                    cur = nxt
                P.barrier()

        def mixer(l, from_input_unused, last):
            tb = mix_tb
            tiles1 = make_tiles(tb, True)
            tiles2 = make_tiles(tb, not last)
            with ExitStack() as em:
                def sbm(name, shape, dt):
                    return em.enter_context(sbt(name, list(shape), dt))
                GL = sbm("m_GL", [128, 4, T], BF16)
                el = ExitStack()
                LX = el.enter_context(sbt("m_LX", [128, 4, T], BF16))
                GLb = [[Buf() for _ in range(NBT)] for _ in range(4)]
                LXb = [[Buf() for _ in range(NBT)] for _ in range(4)]
                zsb = [Buf() for _ in range(NBT)]
                P.dma("pool", lambda e: e.dma_start(out=lrug.t[:], in_=lrug_l[l]), writes=lrug.b)

                def dump_mix():
                    if dbg:
                        P.dma("sp", lambda e: e.dma_start(out=dbgGL, in_=GL[:]), writes=[Buf()])
                        P.dma("sp", lambda e: e.dma_start(out=dbgLX, in_=LX[:]), writes=[Buf()])
                        P.dma("sp", lambda e: e.dma_start(out=dbgYC, in_=ycs), reads=[ycsb], writes=[Buf()])
                        P.barrier()

                if True:
                    with ExitStack() as e1:
                        def sb1(name, shape, dt):
                            return e1.enter_context(sbt(name, list(shape), dt))
                        hins1 = Ring([TB(sb1(f"p1_hin{i}", [128, KC, tb], F32)) for i in range(2)])
                        us1 = Ring([TB(sb1(f"p1_u{i}", [128, KC, tb], BF16)) for i in range(2)])
                        sqb = TB(sb1("p1_sqb", [128, KC, 512], BF16))
                        rstd = TB(sb1("p1_rstd", [128, 512], F32))
                        tmps = Ring([TB(sb1(f"p1_tmp{i}", [128, 512], F32)) for i in range(2)])
                        xfb = TB(sb1("p1_xfb", [128, 2, 512], BF16), 2)
                        sBt = Ring([TB(sb1(f"p1_sB{i}", [128, 512], F32)) for i in range(2)])
                        sCt = Ring([TB(sb1(f"p1_sC{i}", [128, 512], F32)) for i in range(2)])
                        qt = TB(sb1("p1_q", [128, 512], F32))
                        yct = TB(sb1("p1_yc", [128, 512], F32))
                        yco = Ring([TB(sb1(f"p1_yco{i}", [128, 512], BF16)) for i in range(2)])
                        zst = Ring([TB(sb1(f"p1_zst{i}", [128, 512], BF16)) for i in range(3)])
                        wres = [TB(sb1(f"p1_w{i}", [128, KC, 128], BF16)) for i in range(16)]
                        for cch in range(16):
                            P.dma("pool", lambda e, cch=cch: e.dma_start(out=wres[cch].t[:], in_=win_l[l, cch]),
                                  writes=wres[cch].b)

                        def p1_norm(hin, u, n, isc):
                            tc = 1 if isc else 0
                            norm_mod(hin, n, u, sqb, rstd, tmps,
                                     lambda kc: A_ap(l, 1, kc, tc), lambda kc: B_ap(l, 1, kc, tc), [parb[l]])

                        def p1_proj(u, t0, n, isc, cchs, sBs, sCs):
                            b0 = t0 // 128
                            nb = n // 128
                            for cch in cchs:
                                wn = wres[cch]
                                pp = psring.next()

                                def fn(e, wn=wn, pp=pp):
                                    ins = None
                                    for kc in range(KC):
                                        ins = e.matmul(pp.t[:, 0:n], wn.t[:, kc, :], u.t[:, kc, 0:n],
                                                       start=(kc == 0), stop=(kc == KC - 1))
                                    return ins
                                P.op("pe", fn, reads=wn.b + u.b, writes=pp.b)
                                if cch < 2:
                                    P.op("act", lambda e, pp=pp, cch=cch: e.activation(
                                        out=xfb.t[:, cch, 0:n], in_=pp.t[:, 0:n], func=AF.Copy),
                                        reads=pp.b, writes=[xfb.b[cch]])
                                elif cch < 6:
                                    cc = cch - 2
                                    P.op("dve", lambda e, pp=pp, cc=cc: e.tensor_copy(
                                        out=LX[:, cc, t0:t0 + n], in_=pp.t[:, 0:n]),
                                        reads=pp.b, writes=LXb[cc][b0:b0 + nb])
                                elif cch < 10:
                                    cc = cch - 6
                                    P.op("act", lambda e, pp=pp, cc=cc: e.activation(
                                        out=GL[:, cc, t0:t0 + n], in_=pp.t[:, 0:n], func=AF.Gelu),
                                        reads=pp.b, writes=GLb[cc][b0:b0 + nb])
                                elif cch < 12:
                                    cc = cch - 10
                                    st = sBt.next()
                                    sBs[cc] = st
                                    P.op("act", lambda e, pp=pp, st=st: e.activation(
                                        out=st.t[:, 0:n], in_=pp.t[:, 0:n], func=AF.Copy),
                                        reads=pp.b, writes=st.b)
                                elif cch < 14:
                                    cc = cch - 12
                                    st = sCt.next()
                                    sCs[cc] = st
                                    P.op("act", lambda e, pp=pp, st=st: e.activation(
                                        out=st.t[:, 0:n], in_=pp.t[:, 0:n], func=AF.Copy),
                                        reads=pp.b, writes=st.b)
                                else:
                                    cc = cch - 14
                                    rw = n if isc else GRID_W
                                    sC = sCs[cc]
                                    sB = sBs[cc]
                                    P.op("dve", lambda e, pp=pp, sC=sC: e.tensor_tensor(
                                        out=qt.t[:, 0:n], in0=sC.t[:, 0:n], in1=pp.t[:, 0:n], op=ALU.mult),
                                        reads=pp.b + sC.b, writes=qt.b)
                                    P.op("act", lambda e, cc=cc: e.activation(
                                        out=yct.t[:, 0:n], in_=qt.t[:, 0:n], func=AF.Copy,
                                        scale=sconv.t[:, l, cc, 1:2]),
                                        reads=qt.b + sconv.b, writes=yct.b)
                                    q3 = qt.t[:, 0:n].rearrange("p (r w) -> p r w", w=rw)
                                    y3 = yct.t[:, 0:n].rearrange("p (r w) -> p r w", w=rw)
                                    P.op("dve", lambda e, cc=cc, q3=q3, y3=y3, rw=rw: e.scalar_tensor_tensor(
                                        out=y3[:, :, 1:rw], in0=q3[:, :, 0:rw - 1], scalar=sconv.t[:, l, cc, 0:1],
                                        in1=y3[:, :, 1:rw], op0=ALU.mult, op1=ALU.add),
                                        reads=qt.b + sconv.b + yct.b, writes=yct.b)
                                    P.op("dve", lambda e, cc=cc, q3=q3, y3=y3, rw=rw: e.scalar_tensor_tensor(
                                        out=y3[:, :, 0:rw - 1], in0=q3[:, :, 1:rw], scalar=sconv.t[:, l, cc, 2:3],
                                        in1=y3[:, :, 0:rw - 1], op0=ALU.mult, op1=ALU.add),
                                        reads=qt.b + sconv.b + yct.b, writes=yct.b)
                                    yo = yco.next()
                                    P.op("dve", lambda e, yo=yo, sB=sB: e.tensor_tensor(
                                        out=yo.t[:, 0:n], in0=yct.t[:, 0:n], in1=sB.t[:, 0:n], op=ALU.mult),
                                        reads=yct.b + sB.b, writes=yo.b)
                                    P.dma("sp", lambda e, yo=yo, cc=cc: e.dma_start(
                                        out=ycs[:, cc, t0:t0 + n], in_=yo.t[:, 0:n]), reads=yo.b, writes=[ycsb])

                        def p1_zdft(t0, n):
                            b0 = t0 // 128
                            for bi in range(n // 128):
                                pz = psring.next()

                                def fn(e, pz=pz, bi=bi):
                                    ins = None
                                    for cs in range(2):
                                        for chc in range(2):
                                            ins = e.matmul(pz.t[:, cs * 256 + chc * 128:cs * 256 + (chc + 1) * 128],
                                                           xfb.t[:, chc, bi * 128:(bi + 1) * 128], ccbd.t[:, cs, :],
                                                           start=True, stop=True)
                                    return ins
                                P.op("pe", fn, reads=xfb.b + ccbd.b, writes=pz.b)
                                zt = zst.next()
                                P.op("act", lambda e, pz=pz, zt=zt: e.activation(
                                    out=zt.t[:], in_=pz.t[:], func=AF.Copy), reads=pz.b, writes=zt.b)
                                P.dma("sp", lambda e, zt=zt, bi=bi: e.dma_start(out=zscr[b0 + bi], in_=zt.t[:]),
                                      reads=zt.b, writes=[zsb[b0 + bi]])

                        cur_h, cur_u = hins1.next(), us1.next()
                        load_tile(cur_h, tiles1[0][0], tiles1[0][1], False)
                        p1_norm(cur_h, cur_u, tiles1[0][1], tiles1[0][2])
                        for ti, (t0, n, isc) in enumerate(tiles1):
                            nxt_h = nxt_u = None
                            if ti + 1 < len(tiles1):
                                nxt_h, nxt_u = hins1.next(), us1.next()
                                load_tile(nxt_h, tiles1[ti + 1][0], tiles1[ti + 1][1], False)
                            sBs, sCs = [None, None], [None, None]
                            p1_proj(cur_u, t0, n, isc, range(0, 8), sBs, sCs)
                            if nxt_h is not None:
                                p1_norm(nxt_h, nxt_u, tiles1[ti + 1][1], tiles1[ti + 1][2])
                            p1_proj(cur_u, t0, n, isc, range(8, 16), sBs, sCs)
                            p1_zdft(t0, n)
                            cur_h, cur_u = nxt_h, nxt_u
                        P.barrier()

                    if stop <= 1.2:
                        dump_mix()
                        return
                    with ExitStack() as e1:
                        Zt = e1.enter_context(sbt("d_Z", [128, NBT, 512], BF16))
                        Zb = [Buf() for _ in range(NBT)]
                        zh = NBT // 2
                        for (za, zb_) in ((0, zh), (zh, NBT)):
                            P.dma("sp", lambda e, za=za, zb_=zb_: e.dma_start(
                                out=Zt[:, za:zb_, :], in_=zscr[za:zb_].rearrange("b p c -> p b c")),
                                reads=zsb[za:zb_], writes=Zb[za:zb_])
                        dfr = Ring([TB(e1.enter_context(sbt(f"d_cs{i}", [128, 2, GS, 512], BF16)))
                                    for i in range(2)])
                        yfo = Ring([TB(e1.enter_context(sbt(f"d_yf{i}", [128, 512], BF16)))
                                    for i in range(6)])
                        bsr = Ring([TB(e1.enter_context(sbt(f"d_bs{i}", [128, 512], F32))) for i in range(2)])
                        isq = 1.0 / math.sqrt(S)
                        p0 = psring.next()

                        def fn0(e, p0=p0):
                            ins = None
                            for chc in range(2):
                                for tin in range(NBL):
                                    ins = e.matmul(p0.t[:, chc:chc + 1], Zt[:, tin, chc * 128:(chc + 1) * 128],
                                                   isqb.t[:, 0:1], start=(tin == 0), stop=(tin == NBL - 1))
                            return ins
                        P.op("pe", fn0, reads=Zb[0:NBL] + isqb.b, writes=p0.b)
                        y0 = yfo.next()
                        P.op("act", lambda e, y0=y0, p0=p0: e.activation(out=y0.t[:, 0:2], in_=p0.t[:, 0:2], func=AF.Copy),
                             reads=p0.b, writes=y0.b)
                        for chc in range(2):
                            P.dma("sp", lambda e, y0=y0, chc=chc: e.dma_start(out=yfs[:, chc, 0:1], in_=y0.t[:, chc:chc + 1],
                                                                                 allow_slow_non_contiguous=True),
                                  reads=y0.b, writes=blocks(yfblk, 0, 128))
                        for tk in range(NTH):
                            pa = [psring.next(), psring.next()]
                            pb = [psring.next(), psring.next()]
                            for g in range(NG):
                                dd = dfr.next()
                                P.dma("sp", lambda e, dd=dd, tk=tk, g=g: e.dma_start(
                                    out=dd.t[:], in_=dft_l[tk, g]), writes=dd.b)
                                for chc in range(2):
                                    def fn(e, dd=dd, g=g, chc=chc, pac=pa[chc], pbc=pb[chc]):
                                        ins = None
                                        for cs, pp in ((0, pac), (1, pbc)):
                                            for blk in range(GS):
                                                tin = g * GS + blk
                                                ins = e.matmul(pp.t[:, :],
                                                               Zt[:, tin, cs * 256 + chc * 128:cs * 256 + (chc + 1) * 128],
                                                               dd.t[:, cs, blk, :],
                                                               start=(g == 0 and blk == 0),
                                                               stop=(g == NG - 1 and blk == GS - 1))
                                        return ins
                                    P.op("pe", fn, reads=dd.b + Zb[g * GS:(g + 1) * GS],
                                         writes=(pa[chc].b + pb[chc].b) if g == 0 else ())
                                    pa[chc].b[0].w = P.last_ev("pe")
                                    pb[chc].b[0].w = P.last_ev("pe")
                            for chc in range(2):
                                bs = bsr.next()
                                P.op("act", lambda e, bs=bs, pbc=pb[chc]: e.activation(
                                    out=bs.t[:], in_=pbc.t[:], func=AF.Copy), reads=pb[chc].b, writes=bs.b)
                                yd, ym = yfo.next(), yfo.next()
                                P.op("dve", lambda e, yd=yd, bs=bs, pac=pa[chc]: e.tensor_tensor(
                                    out=yd.t[:], in0=pac.t[:], in1=bs.t[:], op=ALU.add),
                                    reads=pa[chc].b + bs.b, writes=yd.b)
                                P.op("dve", lambda e, ym=ym, bs=bs, pac=pa[chc]: e.tensor_tensor(
                                    out=ym.t[:, ::-1], in0=pac.t[:], in1=bs.t[:], op=ALU.subtract),
                                    reads=pa[chc].b + bs.b, writes=ym.b)
                                d0 = 1 + 512 * tk
                                P.dma("sp", lambda e, yd=yd, chc=chc, d0=d0: e.dma_start(
                                    out=yfs[:, chc, d0:d0 + 512], in_=yd.t[:]),
                                    reads=yd.b, writes=yfblk[d0 // 128:(d0 + 511) // 128 + 1])
                                m0 = S - 512 * (tk + 1)
                                P.dma("sp", lambda e, ym=ym, chc=chc, m0=m0: e.dma_start(
                                    out=yfs[:, chc, m0:m0 + 512], in_=ym.t[:]),
                                    reads=ym.b, writes=blocks(yfblk, m0, 512))
                        if not last:
                            for chc in range(2):
                                pyc = psring.next()

                                def fn(e, chc=chc, pyc=pyc):
                                    ins = None
                                    for blk in range(NBC):
                                        for cs in range(2):
                                            ins = e.matmul(pyc.t[:, 0:C],
                                                           Zt[:, NBL + blk, cs * 256 + chc * 128:cs * 256 + (chc + 1) * 128],
                                                           cctx.t[:, cs, blk, :],
                                                           start=(blk == 0 and cs == 0),
                                                           stop=(blk == NBC - 1 and cs == 1))
                                    return ins
                                P.op("pe", fn, reads=cctx.b + Zb[NBL:NBT], writes=pyc.b)
                                yo = yfo.next()
                                P.op("act", lambda e, yo=yo, pyc=pyc: e.activation(
                                    out=yo.t[:, 0:C], in_=pyc.t[:, 0:C], func=AF.Copy), reads=pyc.b, writes=yo.b)
                                P.dma("sp", lambda e, yo=yo, chc=chc: e.dma_start(
                                    out=yfs[:, chc, S:T], in_=yo.t[:, 0:C]),
                                    reads=yo.b, writes=blocks(yfblk, S, C))
                        P.barrier()

                if stop <= 1.4:
                    dump_mix()
                    return
                with ExitStack() as e1:
                    def sb1(name, shape, dt):
                        return e1.enter_context(sbt(name, list(shape), dt))
                    xc = TB(sb1("s_xc", [128, T], F32))
                    xcb = TB(sb1("s_xcb", [128, T], BF16))
                    Hf = TB(sb1("s_Hf", [128, T], F32), NBT)
                    def rng(nm, k=2):
                        return Ring([TB(sb1(f"s_{nm}{i}", [128, 512], F32)) for i in range(k)])
                    SG = 2
                    ag = None
                    if l + 1 < DEPTH:
                        ag = adaln_steps(l + 1, Ring([(TB(sb1(f"s_wada32_{i}", [128, KC, 256], F32)),
                                                      TB(sb1(f"s_wada{i}", [128, KC, 256], BF16))) for i in range(2)]))
                    r_r, r_i, r_a, r_e, r_b, r_h = (rng("r", 2), rng("i", SG + 1), rng("a", SG + 1),
                                                    rng("e", SG + 1), rng("b", 3), rng("h", 3))
                    segs = [(S, C), (0, S)]
                    for cc in range(4):
                        if cc > 0:
                            P.dma("sp", lambda e, c0=cc - 1: e.dma_start(out=gls[:, c0, :], in_=GL[:, c0, :]),
                                  reads=list(GLb[cc - 1]), writes=[glsb[cc - 1]])
                        lxall = [b for b in LXb[cc]]
                        P.op("act", lambda e, cc=cc: e.activation(
                            out=xc.t[:], in_=LX[:, cc, :], func=AF.Identity,
                            bias=lconv.t[:, l, cc, 4:5], scale=lconv.t[:, l, cc, 2:3]),
                            reads=lxall + lconv.b, writes=xc.b)
                        for (s0, sn) in segs:
                            for (tap, sh) in ((0, 2), (1, 1)):
                                P.op("dve", lambda e, cc=cc, s0=s0, sn=sn, tap=tap, sh=sh: e.scalar_tensor_tensor(
                                    out=xc.t[:, s0 + sh:s0 + sn], in0=LX[:, cc, s0:s0 + sn - sh],
                                    scalar=lconv.t[:, l, cc, tap:tap + 1], in1=xc.t[:, s0 + sh:s0 + sn],
                                    op0=ALU.mult, op1=ALU.add),
                                    reads=lxall + lconv.b + xc.b, writes=xc.b)
                            P.op("dve", lambda e, cc=cc, s0=s0, sn=sn: e.scalar_tensor_tensor(
                                out=xc.t[:, s0:s0 + sn - 1], in0=LX[:, cc, s0 + 1:s0 + sn],
                                scalar=lconv.t[:, l, cc, 3:4], in1=xc.t[:, s0:s0 + sn - 1],
                                op0=ALU.mult, op1=ALU.add),
                                reads=lxall + lconv.b + xc.b, writes=xc.b)
                        P.op("act", lambda e: e.activation(out=xcb.t[:], in_=xc.t[:], func=AF.Copy),
                             reads=xc.b, writes=xcb.b)
                        for dr in range(2):
                            prev_h = None
                            plist = []
                            for (s0, sn) in segs:
                                pcs = [(s0 + o, m) for (o, m) in subs_of(sn)]
                                if dr == 1:
                                    pcs = pcs[::-1]
                                plist += [(p0, m, s0) for (p0, m) in pcs]
                            for gi in range(0, len(plist), SG):
                                grp = plist[gi:gi + SG]
                                st = []
                                if ag is not None:
                                    next(ag, None)
                                    next(ag, None)
                                for (p0, m, s0) in grp:
                                    pr = psring.next()
                                    pi = psring.next()

                                    def fn(e, pr=pr, pi=pi, p0=p0, m=m, cc=cc, dr=dr):
                                        e.matmul(pr.t[:, 0:m], lrug.t[:, dr, cc, 0, :], xcb.t[:, p0:p0 + m],
                                                 start=True, stop=True)
                                        return e.matmul(pi.t[:, 0:m], lrug.t[:, dr, cc, 1, :], xcb.t[:, p0:p0 + m],
                                                        start=True, stop=True)
                                    P.op("pe", fn, reads=lrug.b + xcb.b, writes=pr.b + pi.b)
                                    tr, ti_, ta, te = r_r.next(), r_i.next(), r_a.next(), r_e.next()
                                    P.op("act", lambda e, pr=pr, tr=tr, m=m, cc=cc, dr=dr: e.activation(
                                        out=tr.t[:, 0:m], in_=pr.t[:, 0:m], func=AF.Tanh, scale=0.5,
                                        bias=lgh.t[:, l, dr, cc, 0:1]), reads=pr.b + lgh.b, writes=tr.b)
                                    P.op("act", lambda e, pi=pi, ti_=ti_, m=m, cc=cc, dr=dr: e.activation(
                                        out=ti_.t[:, 0:m], in_=pi.t[:, 0:m], func=AF.Tanh, scale=0.5,
                                        bias=lgh.t[:, l, dr, cc, 1:2]), reads=pi.b + lgh.b, writes=ti_.b)
                                    P.op("act", lambda e, tr=tr, ta=ta, m=m, cc=cc, dr=dr: e.activation(
                                        out=ta.t[:, 0:m], in_=tr.t[:, 0:m], func=AF.Exp,
                                        scale=lc.t[:, l, dr, cc, 0:1], bias=lc.t[:, l, dr, cc, 0:1]),
                                        reads=tr.b + lc.b, writes=ta.b)
                                    P.op("dve", lambda e, ta=ta, m=m: e.tensor_scalar(
                                        ta.t[:, 0:m], ta.t[:, 0:m], 1.0, None, op0=ALU.min),
                                        reads=ta.b, writes=ta.b)
                                    P.op("dve", lambda e, ta=ta, te=te, m=m: e.tensor_tensor(
                                        out=te.t[:, 0:m], in0=ta.t[:, 0:m], in1=ta.t[:, 0:m], op=ALU.mult),
                                        reads=ta.b, writes=te.b)
                                    P.op("dve", lambda e, ti_=ti_, p0=p0, m=m: e.scalar_tensor_tensor(
                                        out=ti_.t[:, 0:m], in0=ti_.t[:, 0:m], scalar=1.0, in1=xc.t[:, p0:p0 + m],
                                        op0=ALU.add, op1=ALU.mult),
                                        reads=ti_.b + xc.b, writes=ti_.b)
                                    st.append((p0, m, s0, ti_, ta, te))
                                for (p0, m, s0, ti_, ta, te) in st:
                                    tb_ = r_b.next()
                                    P.op("act", lambda e, te=te, m=m: e.activation(
                                        out=te.t[:, 0:m], in_=te.t[:, 0:m], func=AF.Sqrt, scale=-0.25,
                                        bias=qtrb.t[:, 0:1]), reads=te.b + qtrb.b, writes=te.b)
                                    P.op("dve", lambda e, te=te, ti_=ti_, tb_=tb_, m=m: e.tensor_tensor(
                                        out=tb_.t[:, 0:m], in0=te.t[:, 0:m], in1=ti_.t[:, 0:m], op=ALU.mult),
                                        reads=te.b + ti_.b, writes=tb_.b)
                                    first_in_seq = (s0 == S and ((dr == 0 and p0 == S) or
                                                                 (dr == 1 and p0 + m == T)))
                                    blks = list(range(p0 // 128, (p0 + m) // 128))
                                    if dr == 0:
                                        init = 0.0 if first_in_seq else Hf.t[:, p0 - 1:p0] if p0 != 0 else Hf.t[:, T - 1:T]
                                        rd = ta.b + tb_.b
                                        if not first_in_seq:
                                            rd = rd + [Hf.b[(p0 - 1) // 128 if p0 != 0 else NBT - 1]]
                                        P.op("dve", lambda e, ta=ta, tb_=tb_, p0=p0, m=m, init=init: e.tensor_tensor_scan(
                                            out=Hf.t[:, p0:p0 + m], data0=ta.t[:, 0:m], data1=tb_.t[:, 0:m],
                                            initial=init, op0=ALU.mult, op1=ALU.add),
                                            reads=rd, writes=[Hf.b[k] for k in blks])
                                    else:
                                        th = r_h.next()
                                        if first_in_seq:
                                            init = 0.0
                                            rd = ta.b + tb_.b
                                        else:
                                            init = prev_h.t[:, 0:1]
                                            rd = ta.b + tb_.b + prev_h.b
                                        P.op("dve", lambda e, ta=ta, tb_=tb_, th=th, m=m, init=init: e.tensor_tensor_scan(
                                            out=th.t[:, 0:m][:, ::-1], data0=ta.t[:, 0:m][:, ::-1],
                                            data1=tb_.t[:, 0:m][:, ::-1],
                                            initial=init, op0=ALU.mult, op1=ALU.add),
                                            reads=rd, writes=th.b)
                                        prev_h = th
                                        P.op("dve", lambda e, th=th, tb_=tb_, p0=p0, m=m: e.tensor_tensor(
                                            out=tb_.t[:, 0:m], in0=th.t[:, 0:m], in1=Hf.t[:, p0:p0 + m], op=ALU.add),
                                            reads=th.b + [Hf.b[k] for k in blks] + tb_.b, writes=tb_.b)
                                        P.op("dve", lambda e, tb_=tb_, p0=p0, m=m, cc=cc: e.tensor_tensor(
                                            out=GL[:, cc, p0:p0 + m], in0=tb_.t[:, 0:m], in1=GL[:, cc, p0:p0 + m],
                                            op=ALU.mult),
                                            reads=tb_.b + [GLb[cc][k] for k in blks],
                                            writes=[GLb[cc][k] for k in blks])
                    P.dma("sp", lambda e: e.dma_start(out=gls[:, 3, :], in_=GL[:, 3, :]),
                          reads=list(GLb[3]), writes=[glsb[3]])
                    if ag is not None:
                        for _ in ag:
                            pass
                    P.barrier()
                    dump_mix()

                el.close()
                if stop <= 1.6:
                    return
                em.close()
                tb2 = 1024
                NS2 = tb2 // 512
                tiles2 = make_tiles(tb2, not last)
                with ExitStack() as e1:
                    def sb1(name, shape, dt):
                        return e1.enter_context(sbt(name, list(shape), dt))
                    hins2 = Ring([TB(sb1(f"p2_hin{i}", [128, KC, tb2], F32), NS2) for i in range(2)])
                    u2 = TB(sb1("p2_u", [128, KC, tb2], BF16), NS2)
                    sqb2 = [TB(sb1(f"p2_sqb{i}", [128, KC, 512], BF16)) for i in range(NS2)]
                    rstd2 = [TB(sb1(f"p2_rstd{i}", [128, 512], F32)) for i in range(NS2)]
                    tmps2 = Ring([TB(sb1(f"p2_tmp{i}", [128, 512], F32)) for i in range(2)])
                    ytl = Ring([TB(sb1(f"p2_y{i}", [128, 8, tb2], BF16)) for i in range(2)])
                    sgm = Ring([TB(sb1(f"p2_sg{i}", [128, 512], F32)) for i in range(3)])
                    mts = Ring([TB(sb1(f"p2_mt{i}", [128, 512], F32)) for i in range(3)])
                    maccs = Ring([TB(sb1(f"p2_macc{i}", [128, 512], F32)) for i in range(4)])
                    Mbt = sb1("p2_Mb", [128, KC, tb2], BF16)
                    Mbb = [[Buf() for _ in range(NS2)] for _ in range(KC)]
                    wgs = Ring([TB(sb1(f"p2_wg{i}", [128, KC, 128], BF16)) for i in range(4)])
                    wps = Ring([TB(sb1(f"p2_wp{i}", [128, 8, 128], BF16)) for i in range(2)])
                    wos = Ring([TB(sb1(f"p2_wo{i}", [128, KC, 128], BF16)) for i in range(2)])

                    def p2_load(hin, yt, t0, n):
                        load_tile(hin, t0, n, False)
                        P.dma("sp", lambda e: e.dma_start(out=yt.t[:, 0:2, 0:n], in_=yfs[:, :, t0:t0 + n]),
                              reads=blocks(yfblk, t0, n), writes=yt.b)
                        P.dma("sp", lambda e: e.dma_start(out=yt.t[:, 2:6, 0:n], in_=gls[:, :, t0:t0 + n]),
                              reads=glsb + yt.b, writes=yt.b)
                        P.dma("sp", lambda e: e.dma_start(out=yt.t[:, 6:8, 0:n], in_=ycs[:, :, t0:t0 + n]),
                              reads=[ycsb] + yt.b, writes=yt.b)

                    def p2_norm(hin, n, isc, stage="all"):
                        tc = 1 if isc else 0
                        norm_mod(hin, n, u2, sqb2, rstd2, tmps2,
                                 lambda kc: A_ap(l, 1, kc, tc), lambda kc: B_ap(l, 1, kc, tc), [parb[l]], stage=stage)

                    def p2_gates(yt, t0, n, hook=None):
                        subs = subs_of(n)
                        for dc in range(KC):
                            if hook is not None:
                                hook(dc)
                            wp = wps.next()
                            P.dma("pool", lambda e, wp=wp, dc=dc: e.dma_start(out=wp.t[:], in_=wp_l[l, dc]),
                                  writes=wp.b)
                            accs = [maccs.next() for _ in subs]
                            for br in range(3):
                                wg = wgs.next()
                                P.dma("pool", lambda e, wg=wg, br=br, dc=dc: e.dma_start(
                                    out=wg.t[:], in_=win_l[l, 16 + br * 8 + dc]), writes=wg.b)
                                k0, nk = ((0, 2), (2, 4), (6, 2))[br]
                                for si, (o, m) in enumerate(subs):
                                    pg = psring.next()
                                    pv = psring.next()

                                    def fn(e, wg=wg, wp=wp, pg=pg, pv=pv, o=o, m=m, k0=k0, nk=nk):
                                        ins = None
                                        for kc in range(KC):
                                            ins = e.matmul(pg.t[:, 0:m], wg.t[:, kc, :], u2.t[:, kc, o:o + m],
                                                           start=(kc == 0), stop=(kc == KC - 1))
                                        for i in range(nk):
                                            ins = e.matmul(pv.t[:, 0:m], wp.t[:, k0 + i, :], yt.t[:, k0 + i, o:o + m],
                                                           start=(i == 0), stop=(i == nk - 1))
                                        return ins
                                    P.op("pe", fn, reads=wg.b + wp.b + [u2.b[si]] + yt.b, writes=pg.b + pv.b)
                                    sg = sgm.next()
                                    P.op("act", lambda e, sg=sg, pg=pg, m=m: e.activation(
                                        out=sg.t[:, 0:m], in_=pg.t[:, 0:m], func=AF.Sigmoid),
                                        reads=pg.b, writes=sg.b)
                                    acc = accs[si]
                                    if br == 0:
                                        P.op("dve", lambda e, sg=sg, pv=pv, acc=acc, m=m: e.tensor_tensor(
                                            out=acc.t[:, 0:m], in0=sg.t[:, 0:m], in1=pv.t[:, 0:m], op=ALU.mult),
                                            reads=sg.b + pv.b, writes=acc.b)
                                    else:
                                        mt = mts.next()
                                        P.op("dve", lambda e, sg=sg, pv=pv, mt=mt, m=m: e.tensor_tensor(
                                            out=mt.t[:, 0:m], in0=sg.t[:, 0:m], in1=pv.t[:, 0:m], op=ALU.mult),
                                            reads=sg.b + pv.b, writes=mt.b)
                                        if br == 1:
                                            P.op("dve", lambda e, mt=mt, acc=acc, m=m: e.tensor_tensor(
                                                out=acc.t[:, 0:m], in0=acc.t[:, 0:m], in1=mt.t[:, 0:m], op=ALU.add),
                                                reads=mt.b + acc.b, writes=acc.b)
                                        else:
                                            P.op("dve", lambda e, mt=mt, acc=acc, m=m, o=o, dc=dc: e.tensor_tensor(
                                                out=Mbt[:, dc, o:o + m], in0=acc.t[:, 0:m], in1=mt.t[:, 0:m],
                                                op=ALU.add),
                                                reads=mt.b + acc.b, writes=[Mbb[dc][si]])

                    def p2_out(hin, t0, n, isc):
                        tc = 1 if isc else 0
                        subs = subs_of(n)
                        for dc in range(KC):
                            wo = wos.next()
                            P.dma("pool", lambda e, wo=wo, dc=dc: e.dma_start(out=wo.t[:], in_=wout_l[l, dc]),
                                  writes=wo.b)
                            for si, (o, m) in enumerate(subs):
                                po = psring.next()

                                def fn(e, wo=wo, po=po, o=o, m=m):
                                    ins = None
                                    for kc in range(KC):
                                        ins = e.matmul(po.t[:, 0:m], wo.t[:, kc, :], Mbt[:, kc, o:o + m],
                                                       start=(kc == 0), stop=(kc == KC - 1))
                                    return ins
                                P.op("pe", fn, reads=wo.b + [Mbb[kc][si] for kc in range(KC)], writes=po.b)
                                P.op("dve", lambda e, po=po, dc=dc, o=o, m=m, si=si: e.scalar_tensor_tensor(
                                    out=hin.t[:, dc, o:o + m], in0=po.t[:, 0:m], scalar=G_ap(l, 1, dc, tc),
                                    in1=hin.t[:, dc, o:o + m], op0=ALU.mult, op1=ALU.add),
                                    reads=po.b + [parb[l]] + [hin.b[si]], writes=[hin.b[si]])
                        store_tile(hin, t0, n)

                    cur_h, cur_y = hins2.next(), ytl.next()
                    p2_load(cur_h, cur_y, tiles2[0][0], tiles2[0][1])
                    p2_norm(cur_h, tiles2[0][1], tiles2[0][2])
                    for ti, (t0, n, isc) in enumerate(tiles2):
                        nxt_h = nxt_y = None
                        if ti + 1 < len(tiles2):
                            nxt_h, nxt_y = hins2.next(), ytl.next()
                            p2_load(nxt_h, nxt_y, tiles2[ti + 1][0], tiles2[ti + 1][1])
                        hook = None
                        if nxt_h is not None:
                            def hook(dc, nxt_h=nxt_h, nn=tiles2[ti + 1][1], ni=tiles2[ti + 1][2]):
                                if dc == 2:
                                    p2_norm(nxt_h, nn, ni, "sq")
                                if dc == 4:
                                    p2_norm(nxt_h, nn, ni, "stat")
                        p2_gates(cur_y, t0, n, hook)
                        if nxt_h is not None:
                            p2_norm(nxt_h, tiles2[ti + 1][1], tiles2[ti + 1][2], "apply")
                        p2_out(cur_h, t0, n, isc)
                        cur_h, cur_y = nxt_h, nxt_y
                    P.barrier()

        P.barrier()
        for l in range(DEPTH):
            last = (l == DEPTH - 1)
            if not skipffn:
                ffn_phase(l, 0, from_input=(l == 0), with_ctx=True)
            if stop <= 1:
                break
            if dbg and l == 0:
                P.dma("sp", lambda e: e.dma_start(out=hdump0, in_=hbuf), writes=[Buf()])
                P.barrier()
            mixer(l, False, last)
            if stop <= 2:
                break
            ffn_phase(l, 1, from_input=False, with_ctx=not last, final_norm=last)
        P.barrier()
        if dbg:
            P.dma("sp", lambda e: e.dma_start(out=hdump, in_=hbuf), writes=[Buf()])
            P.dma("sp", lambda e: e.dma_start(out=yfdump, in_=yfs), writes=[Buf()])
            P.barrier()
        P.emit(nc)
    return nc


def _fm(a):
    a = np.asarray(a)
    lead = a.shape[:-1]
    nk = a.shape[-1] // 128
    a = a.reshape(lead + (nk, 128))
    perm = (len(lead) + 1, len(lead)) + tuple(range(len(lead)))
    return np.ascontiguousarray(a.transpose(perm))


def prep_shared(inp, S, C, DEPTH):
    f = np.float32
    sh = {}
    w_ada = np.asarray(inp["w_ada"], f)
    sh["w_ada_l"] = np.ascontiguousarray(w_ada.reshape(DEPTH, KC, 128, 18, 512).transpose(0, 3, 2, 1, 4))
    sh["b_ada_l"] = np.ascontiguousarray(np.asarray(inp["b_ada"], f).reshape(DEPTH, 72, 128).transpose(2, 0, 1))
    g = np.concatenate([np.stack([inp["g_ffn1"][l], inp["g_mix"][l], inp["g_ffn2"][l]]) for l in range(DEPTH)]
                       + [np.asarray(inp["g_final"])[None]], 0).astype(f)
    sh["gvec"] = np.ascontiguousarray(g.reshape(-1, KC, 128).transpose(2, 0, 1))
    wg = np.asarray(inp["ffn_w_gate"], f).reshape(DEPTH, 2, KC, 128, FC, 128)
    wu = np.asarray(inp["ffn_w_up"], f).reshape(DEPTH, 2, KC, 128, FC, 128)
    gu = np.stack([wg, wu], 2)
    sh["wgu_l"] = np.ascontiguousarray(gu.transpose(0, 1, 5, 4, 2, 3, 6))
    del gu, wg, wu
    wd = np.asarray(inp["ffn_w_down"], f).reshape(DEPTH, 2, FC, 128, KC, 128)
    sh["wd_l"] = np.ascontiguousarray(wd.transpose(0, 1, 4, 3, 2, 5))
    win = np.asarray(inp["w_in"], f).reshape(DEPTH, KC, 128, 40, 128)
    sh["win_l"] = np.ascontiguousarray(win.transpose(0, 3, 2, 1, 4))
    wp = np.concatenate([inp["wp_fourier"], inp["wp_lru"], inp["wp_conv"]], 1).astype(f)
    sh["wp_l"] = np.ascontiguousarray(wp.reshape(DEPTH, 8, 128, KC, 128).transpose(0, 3, 2, 1, 4))
    wo = np.asarray(inp["w_out"], f).reshape(DEPTH, KC, 128, KC, 128)
    sh["wout_l"] = np.ascontiguousarray(wo.transpose(0, 3, 2, 1, 4))
    bd = np.zeros((DEPTH, 128, 2, 4, 2, 128), f)
    for ax, key in enumerate(("lru_wa", "lru_wx")):
        w = np.asarray(inp[key], f)
        for cc in range(4):
            for hl in range(2):
                bd[:, hl * 64:(hl + 1) * 64, :, cc, ax, hl * 64:(hl + 1) * 64] = \
                    w[:, :, cc * 2 + hl].transpose(0, 2, 1, 3)
    sh["lrug_l"] = bd
    lc = np.concatenate([np.asarray(inp["lru_conv_w"], f), np.asarray(inp["lru_conv_b"], f)[:, None]], 1)
    sh["lconv_l"] = np.ascontiguousarray(lc.reshape(DEPTH, 5, 4, 128).transpose(3, 0, 2, 1))
    lg = np.stack([np.asarray(inp["lru_ba"], f).reshape(DEPTH, 2, 512),
                   np.asarray(inp["lru_bx"], f).reshape(DEPTH, 2, 512),
                   np.asarray(inp["lru_lambda"], f)], -1)
    sh["lgate_l"] = np.ascontiguousarray(lg.reshape(DEPTH, 2, 4, 128, 3).transpose(3, 0, 1, 2, 4))
    sc = np.asarray(inp["sc_conv_w"], f)
    sh["sconv_l"] = np.ascontiguousarray(sc.reshape(DEPTH, 3, 2, 128).transpose(3, 0, 2, 1))
    i64 = np.arange(64)
    ang = 2 * np.pi * np.outer(i64, i64) / 64
    cc_, sc_ = np.cos(ang) / 8.0, -np.sin(ang) / 8.0
    ccbd = np.zeros((128, 2, 128), f)
    for h in range(2):
        ccbd[h * 64:(h + 1) * 64, 0, h * 64:(h + 1) * 64] = cc_
        ccbd[h * 64:(h + 1) * 64, 1, h * 64:(h + 1) * 64] = sc_
    sh["ccbd_l"] = ccbd
    tc_ = np.arange(C)
    angc = 2 * np.pi * (np.outer(tc_, tc_) % C) / C
    cs = np.stack([np.cos(angc), np.sin(angc)], 0) / math.sqrt(C)
    sh["cctx_l"] = np.ascontiguousarray(cs.reshape(2, C // 128, 128, C).transpose(2, 0, 1, 3)).astype(f)
    ts_ = np.arange(S, dtype=np.int64)
    NT = S // 512
    NTH = NT // 2
    NBL = S // 128
    GS = min(8, NBL)
    NG = NBL // GS
    dft = np.empty((NTH, NG, 128, 2, GS, 512), ml_dtypes.bfloat16)
    scl = 1.0 / math.sqrt(S)
    for tk in range(NTH):
        prod = np.outer(ts_, ts_[1 + tk * 512:1 + (tk + 1) * 512]) % S
        a = 2 * np.pi * prod / S
        for cs_i, m in enumerate((np.cos(a) * scl, np.sin(a) * scl)):
            m = m.reshape(NG, GS, 128, 512).transpose(0, 2, 1, 3)
            dft[tk, :, :, cs_i] = m.astype(ml_dtypes.bfloat16)
    sh["dft_l"] = dft
    return sh


def prep_core(inp, b):
    f = np.float32
    d = {}
    d["x_l"] = _fm(np.asarray(inp["x"][b], f))
    d["ctx_l"] = _fm(np.asarray(inp["ctx"][b], f))
    cv = np.stack([np.asarray(inp["c"][b], f), np.asarray(inp["c_ctx"], f)], 0)
    d["cvec"] = _fm(cv)
    return d


_NC_CACHE = {}


def run(inp, S, C, DEPTH, n_cores):
    key = (S, C, DEPTH)
    if key not in _NC_CACHE:
        _NC_CACHE[key] = build(S, C, DEPTH)
    nc = _NC_CACHE[key]
    sh = prep_shared(inp, S, C, DEPTH)
    in_maps = []
    for b in range(n_cores):
        m = dict(sh)
        m.update(prep_core(inp, b))
        in_maps.append(m)
    res = run_bass_kernel_spmd(nc, in_maps, core_ids=list(range(n_cores)))
    out = np.empty((n_cores, S, D), np.float32)
    for b in range(n_cores):
        yT = np.asarray(res.results[b]["yT"])
        out[b] = yT.transpose(2, 1, 0).reshape(S, D)
    return out


def kernel(**inputs):
    return run(inputs, 4096, 256, 4, N_CORES)
```
